# Optimizing a Trainium2 kernel written in Bass

```python
import jax, jax.numpy as jnp
from jax import lax
import numpy as np

D_MODEL = 1024
BATCH = 8
SEQ = 2048
DEPTH = 1

CHUNK = 64
D_MIX = D_MODEL
D_POOL = D_MIX // 2
POOL_WINDOWS = (2, 4, 8, 16)
N_POOL_GROUPS = len(POOL_WINDOWS)
POOL_GROUP = D_POOL // N_POOL_GROUPS
D_RWKV = D_MIX - D_POOL
RWKV_HEAD = 64
N_RWKV_HEADS = D_RWKV // RWKV_HEAD
DECAY_LORA = 64
AAA_LORA = 64
D_SHIFT = 3 * D_RWKV + DECAY_LORA + AAA_LORA
D_IN = 2 * D_POOL + D_SHIFT + D_RWKV
NORM_EPS = 1e-6
GN_EPS = 64e-5

kernel_name = "hybrid_pool_rwkv7_block"


def rmsnorm(x, gain):
    x32 = x.astype(jnp.float32)
    inv = lax.rsqrt(jnp.mean(x32 * x32, axis=-1, keepdims=True) + NORM_EPS)
    return (x32 * inv * gain.astype(jnp.float32)).astype(x.dtype)


def multi_scale_pool(u, pool_w, pool_scale):
    B, S, _ = u.shape
    u32 = u.astype(jnp.float32).reshape(B, S, N_POOL_GROUPS, POOL_GROUP)
    cs = jnp.cumsum(u32, axis=1)
    cs0 = jnp.concatenate([jnp.zeros_like(cs[:, :1]), cs], axis=1)
    pos = jnp.arange(S)
    outs = []
    for g, w in enumerate(POOL_WINDOWS):
        hi = cs[:, :, g]
        lo = jnp.concatenate([jnp.zeros((B, w - 1, POOL_GROUP), jnp.float32),
                              cs0[:, :S + 1 - w, g]], axis=1)
        cnt = jnp.minimum(pos + 1, w).astype(jnp.float32)[None, :, None]
        outs.append((hi - lo) / cnt - u32[:, :, g])
    pooled = jnp.stack(outs, axis=2)
    mixed = jnp.einsum('bsgc,gcd->bsgd', pooled, pool_w.astype(jnp.float32))
    return (mixed.reshape(B, S, D_POOL) * pool_scale.astype(jnp.float32)).astype(u.dtype)


def rwkv7_time_mix(feat, w0, w_up, a0, a_up, k_k, k_a, r_k, gn_gain, gn_bias):
    B, S, _ = feat.shape
    H, N = N_RWKV_HEADS, RWKV_HEAD
    f = feat.astype(jnp.float32)
    r = f[..., :D_RWKV]
    k = f[..., D_RWKV:2 * D_RWKV]
    v = f[..., 2 * D_RWKV:3 * D_RWKV]
    w_lr = f[..., 3 * D_RWKV:3 * D_RWKV + DECAY_LORA]
    a_lr = f[..., 3 * D_RWKV + DECAY_LORA:]
    ww = w0.astype(jnp.float32) + jnp.tanh(w_lr) @ w_up.astype(jnp.float32)
    decay = jnp.exp(-jnp.exp(-jax.nn.softplus(-ww) - 0.5))
    a = jax.nn.sigmoid(a0.astype(jnp.float32) + a_lr @ a_up.astype(jnp.float32))
    heads = lambda t: t.reshape(B, S, H, N)
    kk = heads(k * k_k.astype(jnp.float32))
    kk = kk / jnp.maximum(jnp.sqrt(jnp.sum(kk * kk, axis=-1, keepdims=True)), 1e-12)
    k = k * (1.0 + (a - 1.0) * k_a.astype(jnp.float32))
    r, k, v, decay, a = heads(r), heads(k), heads(v), heads(decay), heads(a)

    def step(state, inp):
        r_t, k_t, v_t, w_t, kk_t, a_t = inp
        sa = jnp.einsum('bhvk,bhk->bhv', state, -kk_t)
        state = (state * w_t[:, :, None, :]
                 + sa[..., None] * (kk_t * a_t)[:, :, None, :]
                 + v_t[..., None] * k_t[:, :, None, :])
        y_t = jnp.einsum('bhvk,bhk->bhv', state, r_t)
        return state, y_t

    tm = lambda t: jnp.moveaxis(t, 1, 0)
    state0 = jnp.zeros((B, H, N, N), jnp.float32)
    _, y = lax.scan(step, state0, (tm(r), tm(k), tm(v), tm(decay), tm(kk), tm(a)))
    y = jnp.moveaxis(y, 0, 1)
    mu = jnp.mean(y, axis=-1, keepdims=True)
    var = jnp.mean(jnp.square(y - mu), axis=-1, keepdims=True)
    y = (y - mu) * lax.rsqrt(var + GN_EPS)
    y = y * gn_gain.astype(jnp.float32).reshape(H, N) + gn_bias.astype(jnp.float32).reshape(H, N)
    bonus = jnp.sum(r * k * r_k.astype(jnp.float32), axis=-1, keepdims=True) * v
    return (y + bonus).reshape(B, S, D_RWKV)


def setup_inputs(seed: int = 0) -> dict:
    key = jax.random.key(seed)
    ks = jax.random.split(key, 20)
    f32 = jnp.float32
    nrm = lambda k, shape, s: jax.random.normal(k, shape, f32) * s
    x = jax.random.normal(ks[0], (BATCH, SEQ, D_MODEL), f32)
    norm_gain = 1.0 + nrm(ks[1], (DEPTH, D_MODEL), 0.05)
    w_in = nrm(ks[2], (DEPTH, D_MODEL, D_IN), D_MODEL ** -0.5)
    pool_w = nrm(ks[3], (DEPTH, N_POOL_GROUPS, POOL_GROUP, POOL_GROUP), POOL_GROUP ** -0.5)
    pool_scale = 1.0 + nrm(ks[4], (DEPTH, D_POOL), 0.05)
    shift_mu = jax.random.uniform(ks[5], (DEPTH, D_SHIFT), f32, 0.0, 1.0)
    w0 = jax.random.uniform(ks[6], (DEPTH, D_RWKV), f32, -4.0, 2.0)
    w_up = nrm(ks[7], (DEPTH, DECAY_LORA, D_RWKV), 0.1)
    a0 = nrm(ks[8], (DEPTH, D_RWKV), 0.1)
    a_up = nrm(ks[9], (DEPTH, AAA_LORA, D_RWKV), 0.5 * AAA_LORA ** -0.5)
    k_k = 0.85 + nrm(ks[10], (DEPTH, D_RWKV), 0.05)
    k_a = 1.0 + nrm(ks[11], (DEPTH, D_RWKV), 0.05)
    r_k = nrm(ks[12], (DEPTH, N_RWKV_HEADS, RWKV_HEAD), 0.1)
    gn_gain = 1.0 + nrm(ks[13], (DEPTH, D_RWKV), 0.05)
    gn_bias = nrm(ks[14], (DEPTH, D_RWKV), 0.02)
    w_out = nrm(ks[15], (DEPTH, D_MIX, D_MODEL), D_MIX ** -0.5)
    final_gain = 1.0 + nrm(ks[16], (D_MODEL,), 0.05)
    return {"x": x, "norm_gain": norm_gain, "w_in": w_in, "pool_w": pool_w,
            "pool_scale": pool_scale, "shift_mu": shift_mu, "w0": w0, "w_up": w_up,
            "a0": a0, "a_up": a_up, "k_k": k_k, "k_a": k_a, "r_k": r_k,
            "gn_gain": gn_gain, "gn_bias": gn_bias, "w_out": w_out, "final_gain": final_gain}


def reference(x, norm_gain, w_in, pool_w, pool_scale, shift_mu, w0, w_up, a0, a_up,
              k_k, k_a, r_k, gn_gain, gn_bias, w_out, final_gain):
    for l in range(DEPTH):
        h = rmsnorm(x, norm_gain[l])
        z = h @ w_in[l]
        u_a = z[..., :D_POOL]
        g_a = z[..., D_POOL:2 * D_POOL]
        sh = z[..., 2 * D_POOL:2 * D_POOL + D_SHIFT]
        g_b = z[..., 2 * D_POOL + D_SHIFT:]
        sh_prev = jnp.pad(sh, ((0, 0), (1, 0), (0, 0)))[:, :-1]
        sh = sh + shift_mu[l] * (sh_prev - sh)
        y_a = multi_scale_pool(u_a, pool_w[l], pool_scale[l])
        y_b = rwkv7_time_mix(sh, w0[l], w_up[l], a0[l], a_up[l], k_k[l], k_a[l],
                             r_k[l], gn_gain[l], gn_bias[l]).astype(x.dtype)
        y = jnp.concatenate([y_a * jax.nn.silu(g_a), y_b * jax.nn.silu(g_b)], axis=-1)
        x = x + y @ w_out[l]
    return rmsnorm(x, final_gain)
```

```python
import contextlib
import numpy as np
import concourse.bass as bass
import concourse.mybir as mybir
from concourse.bass_utils import run_bass_kernel_spmd

F32 = mybir.dt.float32
BF16 = mybir.dt.bfloat16
AF = mybir.ActivationFunctionType
ALU = mybir.AluOpType
AX = mybir.AxisListType

S = 2048
D = 1024
DIN = 3200
TB = 256
NB = S // TB
NCH = TB // 64
NTT = TB // 128
C0 = float(np.exp(-0.5))
NORM_EPS = 1e-6
GN_EPS = 64e-5
WST = 800

CO_ID = 0
CO_BONES = 128
CO_MMT = 256
CO_ML = CO_MMT + 128
CO_IREP = CO_ML + 128
CO_RST = CO_IREP + 64
CO_ICNT = CO_RST + TB
CST_W = CO_ICNT + 60


def _make_consts():
    c = np.zeros((128, CST_W), np.float32)
    p = np.arange(128)
    c[:, CO_ID:CO_ID + 128] = np.eye(128, dtype=np.float32)
    c[:, CO_BONES:CO_BONES + 128] = (p[:, None] // 64 == p[None, :] // 64).astype(np.float32)
    s = (p % 64)[:, None]
    t = (p % 64)[None, :]
    colr = (p[None, :] < 64)
    m = np.where(colr, (s <= t), (s < t)).astype(np.float32)
    c[:, CO_MMT:CO_MMT + 128] = m
    tt = (p % 64)[:, None]
    ss = np.arange(64)[None, :]
    c[:, CO_ML:CO_ML + 64] = (ss < tt).astype(np.float32)
    c[:, CO_ML + 64:CO_ML + 128] = (tt < ss).astype(np.float32)
    c[:, CO_IREP:CO_IREP + 64] = (ss == tt).astype(np.float32)
    rst = np.ones(TB, np.float32)
    rst[::64] = 0.0
    c[:, CO_RST:CO_RST + TB] = rst[None, :]
    for g, w in enumerate((2, 4, 8, 16)):
        for tq in range(15):
            c[:, CO_ICNT + g * 15 + tq] = 1.0 / min(tq + 1, w)
    return c


_INFO = {}
SCHED_TRIALS = 40
SCHED_PICK = 0
SCHED_NOISE = 0.01


class _Op:
    __slots__ = ("eng", "fn", "deps", "dma", "semval", "signal", "group", "dur", "lat", "idx", "prio", "succ",
                 "npred", "ready_t", "fin", "tag", "st", "pos", "wdeps", "mode")


class _Rec:
    def __getattr__(self, name):
        def f(*a, **k):
            return (name, a, k)
        return f


def _free_size(ap):
    n = 1
    for d in list(ap.shape)[1:]:
        n *= int(d)
    return n


def _estimate(eng, fn, dma):
    name, a, k = fn(_Rec())
    out = k.get("out", a[0] if a else None)
    if dma is not None:
        nbytes = _free_size(out) * int(out.shape[0]) * 4
        return 0.06, 2.0 + nbytes / 150e3, None
    if eng == "pe":
        def r32(v):
            return 32 if v <= 32 else (64 if v <= 64 else 128)
        if name == "transpose":
            n = int(k["in_"].shape[0])
            mode = ("T", r32(int(k["in_"].shape[0])), r32(_free_size(k["in_"])))
        else:
            n = _free_size(k["rhs"])
            mode = ("M", r32(int(k["lhsT"].shape[0])), r32(_free_size(k["lhsT"])))
        d = (0.025 + 0.0006 * n) if n >= 256 else (0.03 + 0.0002 * n)
        return d, d + 0.12, mode
    n = _free_size(out)
    if eng == "act":
        d = 0.22 + 0.00075 * n
    elif eng == "dve":
        d = 0.12 + 0.00105 * n
    else:
        d = 0.2 + 0.0021 * n
    return d, d + 0.08, None


class Plan:
    ENGS = ("sync", "act", "dve", "pool", "pe")

    def __init__(self):
        self.ops = {e: [] for e in self.ENGS}
        self.lastw = {}
        self.readers = {}
        self.dma_eng = {}
        self.nops = 0

    def op(self, eng, fn, reads=(), writes=(), dma=None, group=None):
        only = getattr(self, "only", None)
        if only is not None and getattr(self, "section", None) not in only:
            return None
        if getattr(self, "in_region", False):
            if self.region_count >= self.region_budget:
                return None
            self.region_count += 1
        o = _Op()
        o.eng = eng
        o.fn = fn
        o.dma = dma
        o.deps = {}
        o.signal = False
        o.semval = None
        o.group = group
        o.dur, o.lat, o.mode = _estimate(eng, fn, dma)
        if dma is not None:
            assert self.dma_eng.setdefault(dma, eng) == eng
        for b in reads:
            w = self.lastw.get(b)
            if w is not None:
                o.deps[w] = True
        for b in writes:
            w = self.lastw.get(b)
            if w is not None and w not in o.deps:
                o.deps[w] = False
            for r in self.readers.get(b, ()):
                if r not in o.deps:
                    o.deps[r] = False
        for b in writes:
            self.lastw[b] = o
            self.readers[b] = []
        for b in reads:
            if b not in writes:
                self.readers.setdefault(b, []).append(o)
        o.idx = self.nops
        o.tag = getattr(self, "tag", None)
        self.nops += 1
        self.ops[eng].append(o)
        return o

    @staticmethod
    def _needs_wait(o, d, raw):
        if d.dma is not None or o.dma is not None:
            return True
        if d.eng != o.eng:
            return True
        if o.eng == "pe":
            return False
        return True

    def schedule(self):
        allops = []
        for e in self.ENGS:
            allops.extend(self.ops[e])
        for o in allops:
            o.succ = []
            o.npred = len(o.deps)
        for o in allops:
            for d in o.deps:
                d.succ.append(o)
        allops.sort(key=lambda o: o.idx)
        for o in reversed(allops):
            m = 0.0
            for sc in o.succ:
                if sc.prio > m:
                    m = sc.prio
            o.prio = o.lat + m
        rnd = getattr(self, "prio_rng", None)
        if rnd is not None:
            for o in allops:
                o.prio *= 1.0 + self.prio_noise * (rnd.random() - 0.5)
        free = {e: 0.0 for e in self.ENGS}
        ready = {e: [] for e in self.ENGS}
        for o in allops:
            o.ready_t = 0.0
            if o.npred == 0:
                ready[o.eng].append(o)
        order = {e: [] for e in self.ENGS}
        remaining = len(allops)
        SYNC = 0.25
        MODE_SW = 0.18
        pe_mode = None
        while remaining:
            best = None
            for e in self.ENGS:
                lst = ready[e]
                if not lst:
                    continue
                fe = free[e]
                cand = None
                cs = None
                for o in lst:
                    st = o.ready_t if o.ready_t > fe else fe
                    if e == "pe" and o.mode != pe_mode:
                        st += MODE_SW
                    key = (st, -o.prio, o.idx)
                    if cs is None or key < cs:
                        cs = key
                        cand = o
                if best is None or cs < best[0]:
                    best = (cs, cand)
            cs, o = best
            st = cs[0]
            if o.eng == "pe":
                pe_mode = o.mode
            ready[o.eng].remove(o)
            free[o.eng] = st + o.dur
            o.st = st
            o.fin = st + o.lat
            order[o.eng].append(o)
            remaining -= 1
            for sc in o.succ:
                if sc.eng == o.eng == "pe":
                    t = st + o.dur
                else:
                    t = o.fin + (SYNC if sc.eng != o.eng else 0.05)
                if t > sc.ready_t:
                    sc.ready_t = t
                sc.npred -= 1
                if sc.npred == 0:
                    ready[sc.eng].append(sc)
        self.ops = order
        self.est_makespan = max(free.values())

    def finalize(self):
        for e in self.ENGS:
            for i, o in enumerate(self.ops[e]):
                o.pos = i
        for e in self.ENGS:
            for o in self.ops[e]:
                last = {}
                o.wdeps = []
                for d, raw in o.deps.items():
                    if not self._needs_wait(o, d, raw):
                        continue
                    if d.dma is not None:
                        o.wdeps.append(d)
                        continue
                    cur = last.get(d.eng)
                    if cur is None or d.pos > cur.pos:
                        last[d.eng] = d
                for d in last.values():
                    d.signal = True
                    o.wdeps.append(d)
        self.dma_tot = {}
        groups = {}
        for e in self.ENGS:
            c = 0
            for o in self.ops[e]:
                if o.dma is not None:
                    v = self.dma_tot.get(o.dma, 0) + 16
                    self.dma_tot[o.dma] = v
                    o.semval = v
                    if o.group is not None:
                        groups.setdefault((o.dma, o.group), []).append(o)
                elif o.signal:
                    c += 1
                    o.semval = c
        for lst in groups.values():
            mx = max(x.semval for x in lst)
            for x in lst:
                x.semval = mx

    def sem_keys(self):
        return list(self.ENGS) + sorted(self.dma_tot.keys())

    def run_engine(self, e, eng, sems, final_waits=()):
        known = {}
        for o in self.ops[e]:
            waits = {}
            for d in o.wdeps:
                key = d.dma if d.dma is not None else d.eng
                if d.semval > waits.get(key, 0):
                    waits[key] = d.semval
            for key, v in waits.items():
                if known.get(key, 0) >= v:
                    continue
                eng.wait_ge(sems[key], v)
                known[key] = v
            inst = o.fn(eng)
            if o.dma is not None:
                inst.then_inc(sems[o.dma], 16)
            elif o.signal:
                inst.then_inc(sems[e], 1)
        for key in final_waits:
            eng.wait_ge(sems[key], self.dma_tot[key])


def build(debug=None, nblocks=NB, upto=9, cbud=10 ** 9, do_schedule=True):
    nc = bass.Bass("TRN2", target_bir_lowering=False)
    dt = nc.dram_tensor
    x = dt("x", [S, D], F32, kind="ExternalInput").ap()
    norm_gain = dt("norm_gain", [D], F32, kind="ExternalInput").ap()
    w_in = dt("w_in", [D, DIN], F32, kind="ExternalInput").ap()
    pool_w = dt("pool_w", [4, 128, 128], F32, kind="ExternalInput").ap()
    pool_scale = dt("pool_scale", [512], F32, kind="ExternalInput").ap()
    shift_mu = dt("shift_mu", [1664], F32, kind="ExternalInput").ap()
    w0 = dt("w0", [512], F32, kind="ExternalInput").ap()
    w_up = dt("w_up", [64, 512], F32, kind="ExternalInput").ap()
    a0 = dt("a0", [512], F32, kind="ExternalInput").ap()
    a_up = dt("a_up", [64, 512], F32, kind="ExternalInput").ap()
    k_k = dt("k_k", [512], F32, kind="ExternalInput").ap()
    k_a = dt("k_a", [512], F32, kind="ExternalInput").ap()
    r_k = dt("r_k", [512], F32, kind="ExternalInput").ap()
    gn_gain = dt("gn_gain", [512], F32, kind="ExternalInput").ap()
    gn_bias = dt("gn_bias", [512], F32, kind="ExternalInput").ap()
    w_out = dt("w_out", [D, D], F32, kind="ExternalInput").ap()
    final_gain = dt("final_gain", [D], F32, kind="ExternalInput").ap()
    cst_d = dt("cst", [128, CST_W], F32, kind="ExternalInput").ap()
    out = dt("out", [S, D], F32, kind="ExternalOutput").ap()
    dbg_out = {}
    if debug:
        for name, shp in debug.items():
            dbg_out[name] = dt("dbg_" + name, list(shp), F32, kind="ExternalOutput").ap()

    P = Plan()
    P.region_budget = cbud
    P.region_count = 0
    es = contextlib.ExitStack()
    with es:
        def sb(name, shape, dtype):
            return es.enter_context(nc.sbuf_tensor(name, list(shape), dtype))

        def psum(name, shape, dtype):
            return es.enter_context(nc.psum_tensor(name, list(shape), dtype))

        winb = sb("winb", [128, 8, DIN], BF16)
        woutb = sb("woutb", [128, 8, D], BF16)
        cst = sb("cst_sb", [128, CST_W], F32)
        poolw = sb("poolw", [128, 4, 128], BF16)
        lup = sb("lup", [128, 512], BF16)
        identb = sb("identb", [128, 128], BF16)
        bones = sb("bones", [128, 128], BF16)
        blkrk = sb("blkrk", [128, 128], BF16)
        fgain = sb("fgain", [128, D], F32)
        ngv = sb("ngv", [128, 8], F32)
        muv = sb("muv", [128, 13], F32)
        w0v = sb("w0v", [128, 4], F32)
        a0v = sb("a0v", [128, 4], F32)
        kkv = sb("kkv", [128, 4], F32)
        kav = sb("kav", [128, 4], F32)
        omka = sb("omka", [128, 4], F32)
        rkv = sb("rkv", [128, 4], F32)
        gng = sb("gng", [128, 4], F32)
        gnb = sb("gnb", [128, 4], F32)
        pscv = sb("pscv", [128, 4], F32)
        halo = sb("halo", [128, 13], F32)

        xt = [sb(f"xt{i}", [128, D], F32) for i in range(2)]
        hbf = sb("hbf", [128, D], BF16)
        junk = sb("junk", [128, D], BF16)
        ssx = sb("ssx", [128, 2 * S // 128], F32)
        sdx = sb("sdx", [128, 2 * S // 128], F32)
        rsx = sb("rsx", [128, 2 * S // 128], F32)
        hT2 = [sb(f"hT{i}", [128, 8, TB], BF16) for i in range(2)]

        NZS = 5
        zs = [sb(f"zs{i}", [128, TB], F32) for i in range(NZS)]
        dtmp = sb("dtmp", [128, TB], F32)
        UW = 15 + TB
        ua = [sb(f"ua{i}", [128, UW], F32) for i in range(4)]
        pA = sb("pA", [128, UW], F32)
        pB = sb("pB", [128, UW], F32)
        pooled = sb("pooled", [128, TB], BF16)
        sga = sb("sga", [128, 4, TB], BF16)
        sgb2 = [sb(f"sgb{i}", [128, 4, TB], BF16) for i in range(2)]
        lobf = sb("lobf", [128, TB], BF16)
        thg = sb("thg", [128, TB], F32)

        t_sg = sb("t_sg", [128, TB], F32)
        t_a = sb("t_a", [128, TB], F32)
        t_csg = sb("t_csg", [128, TB], F32)
        t_e2 = sb("t_e2", [128, TB], F32)
        t_e1 = sb("t_e1", [128, TB], F32)
        t_e3 = sb("t_e3", [128, TB], F32)
        t_e4 = sb("t_e4", [128, TB], F32)
        t_kkn = sb("t_kkn", [128, TB], F32)
        t_rn = sb("t_rn", [128, TB], F32)
        t_kp = sb("t_kp", [128, TB], F32)
        t_b = sb("t_b", [128, TB], F32)
        kk2bf = sb("kk2bf", [128, TB], BF16)
        rkbf = sb("rkbf", [128, TB], BF16)

        QT2 = [sb(f"QT{i}", [128, 8, NCH, 128], BF16) for i in range(2)]
        PT2 = [sb(f"PT{i}", [128, 4, NCH, 128], BF16) for i in range(2)]
        PPsrc = sb("PPsrc", [128, NCH, 128], BF16)
        vbf = sb("vbf", [128, TB], BF16)
        PPt2 = [sb(f"PPt{i}", [128, NCH, 4, 128], BF16) for i in range(2)]
        Zt2 = [sb(f"Zt{i}", [128, NCH, 4, 128], BF16) for i in range(2)]
        bon2 = [sb(f"bon{i}", [128, 4, TB], BF16) for i in range(2)]
        gam2 = [sb(f"gam{i}", [128, 4, NCH], F32) for i in range(2)]

        MTs = sb("MTs", [128, 8, 128], BF16)
        MM = [sb(f"MM{i}", [128, 2, 4, 64], BF16) for i in range(2)]
        PTch = [sb(f"PTch{i}", [128, 4, 64], BF16) for i in range(4)]
        GXs = sb("GXs", [128, 4, 64], BF16)
        Xs = sb("Xs", [128, 4, 64], F32)
        Sst = sb("Sst", [128, 4, 64], F32)
        Sdec = sb("Sdec", [128, 4, 64], F32)
        Sbf = [sb(f"Sbf{i}", [128, 4, 64], BF16) for i in range(2)]
        ycp = sb("ycp", [128, 512], F32)
        ysq = sb("ysq", [128, 512], F32)
        ynb = [sb(f"ynb{i}", [128, 512], BF16) for i in range(NTT)]
        st1 = sb("st1", [128, 8], F32)
        st2 = sb("st2", [128, 8], F32)
        stm = sb("stm", [128, 8], F32)
        stv = sb("stv", [128, 8], F32)
        t1 = sb("t1", [128, TB], F32)
        ycat = sb("ycat", [128, 8, TB], BF16)
        xe = [sb(f"xe{i}", [128, D], F32) for i in range(2)]

        psNM = [psum(f"psNM{i}", [128, 512], F32) for i in range(2)]
        psT = psum("psT", [128, 512], F32)
        psSm = psum("psSm", [128, 512], F32)
        psM1 = psum("psM1", [128, 512], F32)
        psP = psum("psP", [128, 512], F32)
        psG = psum("psG", [128, 512], F32)
        psY = psum("psY", [128, 512], F32)
        psTb = psT[:, :].bitcast(BF16)
        psYb = psY[:, :].bitcast(BF16)
        psM1b = psM1[:, :].bitcast(BF16)

        dbg_dumps = []

        def dump(name, ap, key):
            if debug and name in dbg_out:
                dbg_dumps.append((name, ap, key))

        def dma(eng, out_ap, in_ap, key, reads=(), writes=(), group=None, slow=False):
            if slow:
                fn = lambda e, o=out_ap, i=in_ap: e.dma_start(out=o, in_=i, allow_slow_non_contiguous=True)
            else:
                fn = lambda e, o=out_ap, i=in_ap: e.dma_start(out=o, in_=i)
            return P.op(eng, fn, reads=reads, writes=writes, dma=key, group=group)

        dma("sync", cst[:, :], cst_d[:, :], "dc", writes=["cst"], group=0)

        def vec_load(tile, src, n, key):
            dma("sync", tile[:, :], src.rearrange("(j p) -> p j", p=128), "dc", writes=[key], group=0, slow=True)

        vec_load(ngv, norm_gain, 8, "ngv")
        vec_load(muv, shift_mu, 13, "muv")
        vec_load(w0v, w0, 4, "w0v")
        vec_load(a0v, a0, 4, "a0v")
        vec_load(kkv, k_k, 4, "kkv")
        vec_load(kav, k_a, 4, "kav")
        vec_load(rkv, r_k, 4, "rkv")
        vec_load(gng, gn_gain, 4, "gng")
        vec_load(gnb, gn_bias, 4, "gnb")
        vec_load(pscv, pool_scale, 4, "pscv")
        dma("sync", fgain[:, :], final_gain.partition_broadcast(128), "dc", writes=["fgain"], group=0)

        for kc in range(8):
            dma("pool", woutb[:, kc, :], w_out[kc * 128:(kc + 1) * 128, :], "dw", writes=[("woutb", kc)], group=0)
        dma("pool", poolw[:, :, :], pool_w.rearrange("g c d -> c g d"), "dw", writes=["poolw"], group=0)
        dma("pool", lup[0:64, :], w_up[:, :], "dw", writes=[("lup", 0)], group=0)
        dma("pool", lup[64:128, :], a_up[:, :], "dw", writes=[("lup", 1)], group=0)

        P.op("pool", lambda e: e.memset(halo[:, :], 0.0), writes=["halo"])
        P.op("pool", lambda e: e.memset(Sst[:, :, :], 0.0), writes=["Sst"])
        P.op("pool", lambda e: e.memset(Sbf[0][:, :, :], 0.0), writes=["Sbf0"])
        for g in range(4):
            P.op("pool", lambda e, g=g: e.memset(ua[g][:, 0:15], 0.0), writes=[("ua", g)])
        P.op("dve", lambda e: e.tensor_copy(out=identb[:, :], in_=cst[:, CO_ID:CO_ID + 128]),
             reads=["cst"], writes=["identb"])
        P.op("dve", lambda e: e.tensor_copy(out=bones[:, :], in_=cst[:, CO_BONES:CO_BONES + 128]),
             reads=["cst"], writes=["bones"])
        neghalf = sb("neghalf", [128, 16], F32)
        omm = sb("omm", [128, 13], F32)
        P.op("pool", lambda e: e.memset(neghalf[:, :], -0.5), writes=["neghalf"])
        half = sb("half", [128, 1], F32)
        P.op("pool", lambda e: e.memset(half[:, :], 0.5), writes=["half"])
        P.op("dve", lambda e: e.tensor_scalar(out=omm[:, :], in0=muv[:, :], scalar1=-1.0, scalar2=1.0,
                                              op0=ALU.mult, op1=ALU.add),
             reads=["muv"], writes=["omm"])
        for vt, vk in ((w0v, "w0v"), (a0v, "a0v"), (pscv, "pscv"), (gng, "gng"), (gnb, "gnb"), (rkv, "rkv")):
            P.op("dve", lambda e, vt=vt: e.tensor_scalar(out=vt[:, :], in0=vt[:, :], scalar1=0.5, scalar2=None,
                                                         op0=ALU.mult),
                 reads=[vk], writes=[vk])
        P.op("dve", lambda e: e.tensor_scalar(out=omka[:, :], in0=kav[:, :], scalar1=-1.0, scalar2=1.0,
                                              op0=ALU.mult, op1=ALU.add),
             reads=["kav"], writes=["omka"])

        cast_engs = ("dve", "act", "dve", "act", "pool")
        qstage = QT2[1][:, :, :, :].rearrange("p h c t -> p (h c t)").bitcast(F32)
        stg = [(qstage[:, 0:WST], [("QT", 1, 0), ("QT", 1, 1)], "dq0"),
               (qstage[:, 1024:1024 + WST], [("QT", 1, 2), ("QT", 1, 3)], "dq1"),
               (xe[0][:, 0:WST], [("xe", 0)], "de0"), (xe[1][:, 0:WST], [("xe", 1)], "de1")]
        n = 0
        for pc in (3, 1, 2, 0):
            for kc in range(8):
                sidx = n % 4
                stg_t, stg_key, stg_dk = stg[sidx]
                dma("sync", stg_t, w_in[kc * 128:(kc + 1) * 128, pc * WST:(pc + 1) * WST],
                    stg_dk, writes=stg_key)
                ce = cast_engs[n % 5]
                o_ap = winb[:, kc, pc * WST:(pc + 1) * WST]
                i_ap = stg_t
                g_ap = ngv[:, kc:kc + 1]
                if ce == "act":
                    fn = lambda e, o=o_ap, i=i_ap, g=g_ap: e.activation(out=o, in_=i, func=AF.Copy, scale=g)
                elif ce == "pool":
                    fn = lambda e, o=o_ap, i=i_ap, g=g_ap: e.tensor_scalar(out=o, in0=i, scalar1=g, scalar2=0.0,
                                                                          op0=ALU.mult, op1=ALU.add)
                else:
                    fn = lambda e, o=o_ap, i=i_ap, g=g_ap: e.tensor_scalar(out=o, in0=i, scalar1=g, scalar2=None,
                                                                          op0=ALU.mult)
                P.op(ce, fn, reads=stg_key + ["ngv"], writes=[("winb", kc, pc)])
                n += 1

        for par in range(2):
            P.op("dve", lambda e, par=par: e.memset(QT2[par][:, :, :, :].rearrange("p h c t -> p (h c t)"), 0.0),
                 writes=[("QT", par, j) for j in range(4)])
        blkrk4 = sb("blkrk4", [128, 4, 128], BF16)
        for j in range(4):
            P.op("dve", lambda e, j=j: e.tensor_scalar(out=blkrk4[:, j, :], in0=cst[:, CO_BONES:CO_BONES + 128],
                                                       scalar1=rkv[:, j:j + 1], scalar2=None, op0=ALU.mult),
                 reads=["cst", "rkv"], writes=["blkrk4"])

        FT_ORDER = [20]
        for j in range(4):
            FT_ORDER += [12 + j, 8 + j, 16 + j, 21 + j]
        for g in range(4):
            FT_ORDER += [g, 4 + g]
        zrot = [0]
        psz_rot = [0]
        sm_rot = [0]
        xrot = [0]
        erot = [0]

        def sm_half():
            h = sm_rot[0] % 2
            sm_rot[0] += 1
            return psT[:, h * 256:(h + 1) * 256], "psT"

        def block_body(tb):
            t0 = tb * TB
            par = tb % 2
            QT, PT, PPt, Zt = QT2[par], PT2[par], PPt2[par], Zt2[par]
            bon, gam, sgb = bon2[par], gam2[par], sgb2[par]
            hT = hT2[par]
            khT = "hT%d" % par
            P.tag = (tb, "A")
            P.section = "A"
            for i in range(NTT):
                col = tb * NTT + i
                xi = xrot[0] % 2
                xrot[0] += 1
                r0 = t0 + i * 128
                dma("sync", xt[xi][:, :], x[r0:r0 + 128, :], f"dx{xi}", writes=[("xt", xi)])
                P.op("act", lambda e, xi=xi, col=col: e.activation(out=hbf[:, :], in_=xt[xi][:, :], func=AF.Square,
                                                                   accum_out=ssx[:, col:col + 1]),
                     reads=[("xt", xi)], writes=["hbf", ("ssx", col)])
                P.op("pool", lambda e, col=col: e.tensor_scalar(out=sdx[:, col:col + 1], in0=ssx[:, col:col + 1],
                                                                scalar1=1.0 / D, scalar2=NORM_EPS,
                                                                op0=ALU.mult, op1=ALU.add),
                     reads=[("ssx", col)], writes=[("sdx", col)])
                P.op("pool", lambda e, col=col: e.tensor_tensor(out=rsx[:, col:col + 1], in0=sdx[:, col:col + 1],
                                                                in1=neghalf[:, 0:1], op=ALU.pow),
                     reads=[("sdx", col), "neghalf"], writes=[("rsx", col)])
                P.op("dve", lambda e, xi=xi, col=col: e.tensor_scalar(out=hbf[:, :], in0=xt[xi][:, :],
                                                                      scalar1=rsx[:, col:col + 1], scalar2=None,
                                                                      op0=ALU.mult),
                     reads=[("xt", xi), ("rsx", col)], writes=["hbf"])
                for kc in range(8):
                    P.op("pe", lambda e, kc=kc: e.transpose(out=psM1b[:, kc * 128:(kc + 1) * 128],
                                                            in_=hbf[:, kc * 128:(kc + 1) * 128],
                                                            identity=identb[:, :]),
                         reads=["hbf", "identb"], writes=["psM1"])
                P.op("dve", lambda e, i=i: e.tensor_copy(
                    out=hT[:, :, i * 128:(i + 1) * 128],
                    in_=psM1b.rearrange("p (k t) -> p k t", t=128)),
                    reads=["psM1"], writes=[khT])

            if upto < 2:
                return
            P.tag = (tb, "B")
            P.section = "B"
            zmap = {}
            for ft in FT_ORDER:
                pz = psz_rot[0] % 2
                psz_rot[0] += 1
                zp = psP[:, 0:TB]
                for kc in range(8):
                    P.op("pe", lambda e, zp=zp, kc=kc, ft=ft: e.matmul(
                        zp, lhsT=winb[:, kc, ft * 128:(ft + 1) * 128], rhs=hT[:, kc, :],
                        start=(kc == 0), stop=(kc == 7)),
                        reads=[("winb", kc, q) for q in sorted({(ft * 128) // WST, (ft * 128 + 127) // WST})] + [khT],
                        writes=["psP"])
                if 8 <= ft <= 20:
                    zi = zrot[0] % NZS
                    zrot[0] += 1
                    zmap[ft] = zi
                    js = ft - 8
                    zz = zs[zi]
                    P.op("act", lambda e, zz=zz, zp=zp: e.activation(out=zz[:, :], in_=zp, func=AF.Copy),
                         reads=["psP"], writes=[("zs", zi)])
                    if upto < 2.2:
                        continue
                    P.op("act", lambda e, zz=zz, js=js: e.activation(out=dtmp[:, 1:TB], in_=zz[:, 0:TB - 1],
                                                                     func=AF.Copy, scale=muv[:, js:js + 1]),
                         reads=[("zs", zi), "muv"], writes=["dtmp"])
                    P.op("pool", lambda e, js=js: e.tensor_scalar(out=dtmp[:, 0:1], in0=halo[:, js:js + 1],
                                                                  scalar1=muv[:, js:js + 1], scalar2=0.0,
                                                                  op0=ALU.mult, op1=ALU.add),
                         reads=[("halo", js), "muv", "dtmp"], writes=["dtmp"])
                    P.op("pool", lambda e, zz=zz, js=js: e.tensor_copy(out=halo[:, js:js + 1], in_=zz[:, TB - 1:TB]),
                         reads=[("zs", zi)], writes=[("halo", js)])
                    P.op("dve", lambda e, zz=zz, js=js: e.scalar_tensor_tensor(
                        out=zz[:, :], in0=zz[:, :], scalar=omm[:, js:js + 1], in1=dtmp[:, :],
                        op0=ALU.mult, op1=ALU.add),
                        reads=[("zs", zi), "dtmp", "omm"], writes=[("zs", zi)])
                    if tb == 0 and debug and ("sh%d" % ft) in dbg_out:
                        dump("sh%d" % ft, zz[:, :], ("zs", zi))
                elif ft >= 21:
                    j = ft - 21
                    P.op("act", lambda e, zp=zp: e.activation(out=thg[:, :], in_=zp, func=AF.Tanh, scale=0.5),
                         reads=["psP"], writes=["thg"])
                    P.op("dve", lambda e, j=j, zp=zp: e.scalar_tensor_tensor(
                        out=sgb[:, j, :], in0=thg[:, :], scalar=1.0, in1=zp, op0=ALU.add, op1=ALU.mult),
                        reads=["psP", "thg"], writes=[("sgb", par, j)])
                elif ft >= 4:
                    g = ft - 4
                    P.op("act", lambda e, zp=zp: e.activation(out=thg[:, :], in_=zp, func=AF.Tanh, scale=0.5),
                         reads=["psP"], writes=["thg"])
                    P.op("dve", lambda e, g=g, zp=zp: e.scalar_tensor_tensor(
                        out=sga[:, g, :], in0=thg[:, :], scalar=1.0, in1=zp, op0=ALU.add, op1=ALU.mult),
                        reads=["psP", "thg"], writes=[("sga", g)])
                else:
                    g = ft
                    P.op("act", lambda e, g=g, zp=zp: e.activation(out=ua[g][:, 15:UW], in_=zp, func=AF.Copy),
                         reads=["psP"], writes=[("ua", g)])

                if upto < 2.4:
                    continue
                if ft == 20:
                    zz = zs[zmap[20]]
                    P.op("act", lambda e, zz=zz: e.activation(out=lobf[0:64, :], in_=zz[0:64, :], func=AF.Tanh),
                         reads=[("zs", zmap[20])], writes=["lobf"])
                    P.op("act", lambda e, zz=zz: e.activation(out=lobf[64:128, :], in_=zz[64:128, :], func=AF.Copy),
                         reads=[("zs", zmap[20])], writes=["lobf"])
                if 16 <= ft <= 19:
                    j = ft - 16
                    zk = zs[zmap[12 + j]]
                    zr = zs[zmap[8 + j]]
                    zv = zs[zmap[16 + j]]
                    kK, kR, kV = ("zs", zmap[12 + j]), ("zs", zmap[8 + j]), ("zs", zmap[16 + j])
                    pw, pwk = sm_half()
                    P.op("pe", lambda e, pw=pw, j=j: e.matmul(pw, lhsT=lup[0:64, j * 128:(j + 1) * 128],
                                                              rhs=lobf[0:64, :], start=True, stop=True),
                         reads=[("lup", 0), "lobf"], writes=[pwk])
                    P.op("act", lambda e, pw=pw, j=j: e.activation(out=t_sg[:, :], in_=pw, func=AF.Tanh,
                                                                   bias=w0v[:, j:j + 1], scale=0.5),
                         reads=[pwk, "w0v"], writes=["t_sg"])
                    P.op("act", lambda e: e.activation(out=t_sg[:, :], in_=t_sg[:, :], func=AF.Identity,
                                                       scale=0.5, bias=half[:, 0:1]),
                         reads=["t_sg", "half"], writes=["t_sg"])
                    pa, pak = sm_half()
                    P.op("pe", lambda e, pa=pa, j=j: e.matmul(pa, lhsT=lup[64:128, j * 128:(j + 1) * 128],
                                                              rhs=lobf[64:128, :], start=True, stop=True),
                         reads=[("lup", 1), "lobf"], writes=[pak])
                    P.op("act", lambda e, pa=pa, j=j: e.activation(out=t_a[:, :], in_=pa, func=AF.Tanh,
                                                                   bias=a0v[:, j:j + 1], scale=0.5),
                         reads=[pak, "a0v"], writes=["t_a"])
                    P.op("act", lambda e: e.activation(out=t_a[:, :], in_=t_a[:, :], func=AF.Identity,
                                                       scale=0.5, bias=half[:, 0:1]),
                         reads=["t_a", "half"], writes=["t_a"])
                    P.op("dve", lambda e: e.tensor_tensor_scan(out=t_csg[:, :], data0=cst[:, CO_RST:CO_RST + TB],
                                                               data1=t_sg[:, :], initial=0.0,
                                                               op0=ALU.mult, op1=ALU.add),
                         reads=["cst", "t_sg"], writes=["t_csg"])
                    P.op("pool", lambda e: e.tensor_tensor(out=t_e2[:, :], in0=t_csg[:, :], in1=t_sg[:, :],
                                                           op=ALU.subtract),
                         reads=["t_csg", "t_sg"], writes=["t_e2"])
                    P.op("act", lambda e: e.activation(out=t_e2[:, :], in_=t_e2[:, :], func=AF.Exp, scale=-C0),
                         reads=["t_e2"], writes=["t_e2"])
                    P.op("act", lambda e: e.activation(out=t_e1[:, :], in_=t_csg[:, :], func=AF.Exp, scale=-C0),
                         reads=["t_csg"], writes=["t_e1"])
                    P.op("act", lambda e: e.activation(out=t_e3[:, :], in_=t_csg[:, :], func=AF.Exp, scale=C0),
                         reads=["t_csg"], writes=["t_e3"])
                    e1v = t_e1[:, :].rearrange("p (c t) -> p c t", t=64)
                    P.op("pool", lambda e, j=j, e1v=e1v: e.tensor_copy(out=gam[:, j, :], in_=e1v[:, :, 63]),
                         reads=["t_e1"], writes=[("gam", par, j)])
                    P.op("pool", lambda e, j=j: e.tensor_tensor(
                        out=t_e4[:, :].rearrange("p (c t) -> p c t", t=64),
                        in0=t_e3[:, :].rearrange("p (c t) -> p c t", t=64),
                        in1=gam[:, j, :].unsqueeze(2).to_broadcast([128, NCH, 64]), op=ALU.mult),
                        reads=["t_e3", ("gam", par, j)], writes=["t_e4"])
                    P.op("act", lambda e, zk=zk, j=j: e.activation(out=kk2bf[:, :], in_=zk[:, :], func=AF.Square,
                                                                   scale=kkv[:, j:j + 1]),
                         reads=[kK, "kkv"], writes=["kk2bf"])
                    pss, pssk = sm_half()
                    P.op("pe", lambda e, pss=pss: e.matmul(pss, lhsT=bones[:, :], rhs=kk2bf[:, :], start=True,
                                                           stop=True),
                         reads=["bones", "kk2bf"], writes=[pssk])
                    P.op("dve", lambda e, pss=pss: e.tensor_scalar(out=t_rn[:, :], in0=pss, scalar1=1e-18,
                                                                   scalar2=None, op0=ALU.max),
                         reads=[pssk], writes=["t_rn"])
                    P.op("act", lambda e: e.activation(out=t_rn[:, :], in_=t_rn[:, :], func=AF.Ln),
                         reads=["t_rn"], writes=["t_rn"])
                    P.op("act", lambda e: e.activation(out=t_rn[:, :], in_=t_rn[:, :], func=AF.Exp, scale=-0.5),
                         reads=["t_rn"], writes=["t_rn"])
                    P.op("dve", lambda e, zk=zk, j=j: e.scalar_tensor_tensor(
                        out=t_kkn[:, :], in0=zk[:, :], scalar=kkv[:, j:j + 1], in1=t_rn[:, :],
                        op0=ALU.mult, op1=ALU.mult),
                        reads=[kK, "kkv", "t_rn"], writes=["t_kkn"])
                    P.op("act", lambda e, j=j: e.activation(out=t_kp[:, :], in_=t_a[:, :], func=AF.Identity,
                                                            scale=kav[:, j:j + 1], bias=omka[:, j:j + 1]),
                         reads=["t_a", "kav", "omka"], writes=["t_kp"])
                    P.op("pool", lambda e, zk=zk: e.tensor_tensor(out=t_kp[:, :], in0=t_kp[:, :], in1=zk[:, :],
                                                                  op=ALU.mult),
                         reads=["t_kp", kK], writes=["t_kp"])
                    P.op("pool", lambda e: e.tensor_tensor(out=t_b[:, :], in0=t_kkn[:, :], in1=t_a[:, :],
                                                           op=ALU.mult),
                         reads=["t_kkn", "t_a"], writes=["t_b"])

                    def v3(t):
                        return t[:, :].rearrange("p (c t) -> p c t", t=64)

                    prods = [
                        ("dve", PT[:, j, :, 0:64], t_kp, "t_e3", t_e3, "t_kp", ("PT", par, j)),
                        ("pool", PT[:, j, :, 64:128], t_b, "t_e3", t_e3, "t_b", ("PT", par, j)),
                        ("dve", PPsrc[:, :, 0:64], t_kp, "t_e4", t_e4, "t_kp", "PPsrc"),
                        ("pool", PPsrc[:, :, 64:128], t_b, "t_e4", t_e4, "t_b", "PPsrc"),
                    ]
                    for e2 in range(2):
                        rs = slice(64 * e2, 64 * e2 + 64)
                        P.op("dve", lambda e, rs=rs, e2=e2, j=j, zr=zr: e.tensor_tensor(
                            out=QT[rs, 2 * j + e2, :, 0:64], in0=v3(zr)[rs], in1=v3(t_e1)[rs], op=ALU.mult),
                            reads=[kR, "t_e1"], writes=[("QT", par, j)])
                        P.op("pool", lambda e, rs=rs, e2=e2, j=j: e.tensor_tensor(
                            out=QT[rs, 2 * j + e2, :, 64:128], in0=v3(t_kkn)[rs], in1=v3(t_e2)[rs], op=ALU.mult),
                            reads=["t_kkn", "t_e2"], writes=[("QT", par, j)])
                    for (en, o_ap, a_t, ek, e_t, akey, okey) in prods:
                        P.op(en, lambda e, o_ap=o_ap, a_t=a_t, e_t=e_t: e.tensor_tensor(
                            out=o_ap, in0=v3(a_t), in1=v3(e_t), op=ALU.mult),
                            reads=[akey, ek], writes=[okey])
                    P.op("dve", lambda e, zr=zr: e.tensor_tensor(out=rkbf[:, :], in0=zr[:, :], in1=t_kp[:, :],
                                                                 op=ALU.mult),
                         reads=[kR, "t_kp"], writes=["rkbf"])
                    P.op("act", lambda e, zv=zv: e.activation(out=vbf[:, :], in_=zv[:, :], func=AF.Copy),
                         reads=[kV], writes=["vbf"])
                    pbo, pbok = sm_half()
                    P.op("pe", lambda e, pbo=pbo, j=j: e.matmul(pbo, lhsT=blkrk4[:, j, :], rhs=rkbf[:, :],
                                                                start=True, stop=True),
                         reads=["blkrk4", "rkbf"], writes=[pbok])
                    P.op("dve", lambda e, pbo=pbo, zv=zv, j=j: e.tensor_tensor(out=bon[:, j, :], in0=pbo,
                                                                               in1=zv[:, :], op=ALU.mult),
                         reads=[pbok, kV], writes=[("bon", par, j)])
                    for c in range(NCH):
                        P.op("pe", lambda e, c=c: e.transpose(out=psTb[:, c * 128:(c + 1) * 128],
                                                              in_=PPsrc[:, c, :], identity=identb[:, :]),
                             reads=["PPsrc", "identb"], writes=["psT"])
                    for c in range(NCH):
                        P.op("pe", lambda e, c=c: e.transpose(out=psTb[0:64, 512 + c * 128:512 + (c + 1) * 128],
                                                              in_=vbf[:, c * 64:(c + 1) * 64],
                                                              identity=identb[:, :]),
                             reads=["vbf", "identb"], writes=["psT"])
                    P.op("act", lambda e, j=j: e.activation(
                        out=PPt[:, :, j, :], in_=psTb[:, 0:512].rearrange("p (c f) -> p c f", f=128),
                        func=AF.Copy),
                        reads=["psT"], writes=[("PPt", par, j)])
                    P.op("act", lambda e, j=j: e.activation(
                        out=Zt[0:64, :, j, :], in_=psTb[0:64, 512:1024].rearrange("p (c f) -> p c f", f=128),
                        func=AF.Copy),
                        reads=["psT"], writes=[("ZtV", par, j)])
                P.in_region = False
                if upto < 2.6:
                    continue
                if 4 <= ft <= 7:
                    g = ft - 4
                    w = 2 << g
                    u = ua[g]
                    src = u
                    bufs = [pA, pB]
                    for lvl in range(g + 1):
                        sh = 1 << lvl
                        dst = bufs[lvl % 2]
                        lo = 2 * sh - 1
                        P.op("pool", lambda e, src=src, dst=dst, sh=sh, lo=lo: e.tensor_tensor(
                            out=dst[:, lo:UW], in0=src[:, lo:UW], in1=src[:, lo - sh:UW - sh], op=ALU.add),
                            reads=[("ua", g), "pA", "pB"], writes=["pA" if dst is pA else "pB"])
                        src = dst
                    skey = "pA" if src is pA else "pB"
                    P.op("dve", lambda e, src=src, u=u, w=w: e.scalar_tensor_tensor(
                        out=pooled[:, :], in0=src[:, 15:UW], scalar=1.0 / w, in1=u[:, 15:UW],
                        op0=ALU.mult, op1=ALU.subtract),
                        reads=[skey, ("ua", g)], writes=["pooled"])
                    if tb == 0:
                        nfix = w - 1
                        P.op("dve", lambda e, src=src, g=g, nfix=nfix: e.tensor_tensor(
                            out=t1[:, 0:nfix], in0=src[:, 15:15 + nfix],
                            in1=cst[:, CO_ICNT + g * 15:CO_ICNT + g * 15 + nfix], op=ALU.mult),
                            reads=[skey, "cst"], writes=["t1"])
                        P.op("dve", lambda e, u=u, nfix=nfix: e.tensor_tensor(
                            out=pooled[:, 0:nfix], in0=t1[:, 0:nfix], in1=u[:, 15:15 + nfix], op=ALU.subtract),
                            reads=["t1", ("ua", g), "pooled"], writes=["pooled"])
                    P.op("pool", lambda e, u=u: e.tensor_copy(out=u[:, 0:15], in_=u[:, TB:TB + 15]),
                         reads=[("ua", g)], writes=[("ua", g)])
                    pm, pmk = sm_half()
                    P.op("pe", lambda e, pm=pm, g=g: e.matmul(pm, lhsT=poolw[:, g, :], rhs=pooled[:, :],
                                                              start=True, stop=True),
                         reads=["poolw", "pooled"], writes=[pmk])
                    P.op("dve", lambda e, pm=pm, g=g: e.scalar_tensor_tensor(
                        out=ycat[:, g, :], in0=pm, scalar=pscv[:, g:g + 1], in1=sga[:, g, :],
                        op0=ALU.mult, op1=ALU.mult),
                        reads=[pmk, "pscv", ("sga", g)], writes=[("ycat", g)])

            if tb == 0:
                dump("QT", None, None)

            if upto < 3:
                return
            P.in_region = True
            P.region_count = 0
            P.tag = (tb, "D")
            P.section = "D"
            mmt4 = cst[:, CO_MMT:CO_MMT + 128].unsqueeze(1).to_broadcast([128, 4, 128])
            ml2 = cst[:, CO_ML:CO_ML + 128].rearrange("p (m t) -> p m t", t=64).unsqueeze(2).to_broadcast(
                [128, 2, 4, 64])
            irep4 = cst[:, CO_IREP:CO_IREP + 64].unsqueeze(1).to_broadcast([128, 4, 64])
            for c in range(NCH):
                cg = tb * NCH + c
                sb_old = Sbf[cg % 2]
                sb_new = Sbf[(cg + 1) % 2]
                kold = "Sbf%d" % (cg % 2)
                knew = "Sbf%d" % ((cg + 1) % 2)
                qk = [("QT", par, j) for j in range(4)]
                pk = [("PT", par, j) for j in range(4)]
                for rnd in range(2):
                    for h in range(4 * rnd, 4 * rnd + 4):
                        j = h // 2
                        P.op("pe", lambda e, h=h, j=j, c=c: e.matmul(
                            psM1[:, (h % 4) * 128:(h % 4 + 1) * 128], lhsT=PT[:, j, c, :], rhs=QT[:, h, c, :],
                            start=True, stop=True),
                            reads=[("QT", par, j), ("PT", par, j)], writes=["psM1"])
                    P.op("dve", lambda e, rnd=rnd: e.tensor_tensor(
                        out=MTs[:, 4 * rnd:4 * rnd + 4, :], in0=psM1[:, :].rearrange("p (h t) -> p h t", t=128),
                        in1=mmt4, op=ALU.mult),
                        reads=["psM1", "cst"], writes=["MTs"])
                HORD = (0, 4, 1, 5, 2, 6, 3, 7)
                pbase = 2 * (cg % 2)

                def hq(h):
                    return slice(64 * (h // 4), 64 * (h // 4) + 64), h % 4

                for h in HORD:
                    j = h // 2
                    ps_, hh = hq(h)
                    P.op("pe", lambda e, h=h, j=j, c=c, ps_=ps_, hh=hh: e.matmul(
                        psG[ps_, hh * 64:(hh + 1) * 64], lhsT=QT[:, h, c, 64:128],
                        rhs=PT[:, j, c, 64:128], start=True, stop=True),
                        reads=[("QT", par, j), ("PT", par, j)], writes=["psG"])
                    P.op("pe", lambda e, h=h, j=j, c=c, ps_=ps_, hh=hh: e.matmul(
                        psG[ps_, 256 + hh * 64:256 + (hh + 1) * 64], lhsT=PT[:, j, c, 64:128],
                        rhs=QT[:, h, c, 64:128], start=True, stop=True),
                        reads=[("QT", par, j), ("PT", par, j)], writes=["psG"])
                P.op("dve", lambda e: e.tensor_tensor(
                    out=MM[0][:, :, :, :], in0=psG[:, :].rearrange("p (m h t) -> p m h t", m=2, t=64),
                    in1=ml2, op=ALU.mult),
                    reads=["psG", "cst"], writes=["MM0"])
                P.op("pool", lambda e, pbase=pbase: e.tensor_tensor(
                    out=PTch[pbase][:, :, :], in0=irep4, in1=MM[0][:, 1, :, :], op=ALU.subtract),
                    reads=["cst", "MM0"], writes=["PTch%d" % pbase])
                M_cur, P_cur = 0, 0
                for lvl in range(1, 6):
                    mi = MM[M_cur]
                    mo = MM[1 - M_cur]
                    kmi, kmo = "MM%d" % M_cur, "MM%d" % (1 - M_cur)
                    for h in HORD:
                        ps_, hh = hq(h)
                        P.op("pe", lambda e, ps_=ps_, hh=hh, mi=mi: e.matmul(
                            psNM[0][ps_, hh * 64:(hh + 1) * 64], lhsT=mi[ps_, 1, hh, :], rhs=mi[ps_, 0, hh, :],
                            start=True, stop=True),
                            reads=[kmi], writes=[("psNM", 0)])
                    if lvl < 5:
                        for h in HORD:
                            ps_, hh = hq(h)
                            P.op("pe", lambda e, ps_=ps_, hh=hh, mi=mi: e.matmul(
                                psNM[0][ps_, 256 + hh * 64:256 + (hh + 1) * 64], lhsT=mi[ps_, 0, hh, :],
                                rhs=mi[ps_, 1, hh, :], start=True, stop=True),
                                reads=[kmi], writes=[("psNM", 0)])
                        if lvl % 2 == 0:
                            P.op("dve", lambda e, mo=mo: e.tensor_copy(
                                out=mo[:, :, :, :].rearrange("p m h t -> p (m h t)"), in_=psNM[0][:, :]),
                                reads=[("psNM", 0)], writes=[kmo])
                        else:
                            P.op("act", lambda e, mo=mo: e.activation(
                                out=mo[:, :, :, :].rearrange("p m h t -> p (m h t)"), in_=psNM[0][:, :],
                                func=AF.Copy),
                                reads=[("psNM", 0)], writes=[kmo])
                    else:
                        P.op("act", lambda e, mo=mo: e.activation(
                            out=mo[:, 0, :, :].rearrange("p h t -> p (h t)"), in_=psNM[0][:, 0:256], func=AF.Copy),
                            reads=[("psNM", 0)], writes=[kmo])
                    M_cur = 1 - M_cur
                    p_in = PTch[pbase + P_cur]
                    p_out = PTch[pbase + 1 - P_cur]
                    kp_in, kp_out = "PTch%d" % (pbase + P_cur), "PTch%d" % (pbase + 1 - P_cur)
                    for h in HORD:
                        ps_, hh = hq(h)
                        P.op("pe", lambda e, ps_=ps_, hh=hh, mo=mo, p_in=p_in: e.matmul(
                            psSm[ps_, hh * 64:(hh + 1) * 64], lhsT=mo[ps_, 0, hh, :], rhs=p_in[ps_, hh, :],
                            start=True, stop=True),
                            reads=[kmo, kp_in], writes=["psSm"])
                    P.op("dve", lambda e, p_in=p_in, p_out=p_out: e.tensor_tensor(
                        out=p_out[:, :, :].rearrange("p h t -> p (h t)"), in0=psSm[:, 0:256],
                        in1=p_in[:, :, :].rearrange("p h t -> p (h t)"), op=ALU.add),
                        reads=["psSm", kp_in], writes=[kp_out])
                    P_cur = 1 - P_cur
                pfin = PTch[pbase + P_cur]
                kpfin = "PTch%d" % (pbase + P_cur)
                P.op("pool", lambda e, c=c: e.tensor_tensor(
                    out=Sdec[:, :, :], in0=Sst[:, :, :],
                    in1=gam[:, :, c].unsqueeze(2).to_broadcast([128, 4, 64]), op=ALU.mult),
                    reads=["Sst"] + [("gam", par, j) for j in range(4)], writes=["Sdec"])
                for h in HORD:
                    j, e2 = h // 2, h % 2
                    ps_, hh = hq(h)
                    P.op("pe", lambda e, h=h, j=j, e2=e2, c=c, ps_=ps_, hh=hh: e.matmul(
                        psNM[1][ps_, hh * 64:(hh + 1) * 64], lhsT=MTs[0:64, h, 64:128],
                        rhs=Zt[0:64, c, j, e2 * 64:(e2 + 1) * 64], start=True, stop=True),
                        reads=["MTs", ("ZtV", par, j)], writes=[("psNM", 1)])
                P.op("act", lambda e: e.activation(out=Xs[:, :, :].rearrange("p h t -> p (h t)"),
                                                   in_=psNM[1][:, 0:256], func=AF.Copy),
                     reads=[("psNM", 1)], writes=["Xs"])
                for h in HORD:
                    j = h // 2
                    ps_, hh = hq(h)
                    P.op("pe", lambda e, h=h, j=j, c=c, sb_old=sb_old, ps_=ps_, hh=hh: e.matmul(
                        psG[ps_, hh * 64:(hh + 1) * 64], lhsT=QT[:, h, c, 64:128],
                        rhs=sb_old[:, j, :], start=True, stop=True),
                        reads=[("QT", par, j), kold], writes=["psG"])
                P.op("dve", lambda e: e.scalar_tensor_tensor(
                    out=GXs[:, :, :].rearrange("p h t -> p (h t)"), in0=psG[:, 0:256], scalar=-1.0,
                    in1=Xs[:, :, :].rearrange("p h t -> p (h t)"), op0=ALU.mult, op1=ALU.subtract),
                    reads=["psG", "Xs"], writes=["GXs"])
                for h in HORD:
                    ps_, hh = hq(h)
                    ub = psSm if h < 4 else psNM[1]
                    ubk = "psSm" if h < 4 else ("psNM", 1)
                    P.op("pe", lambda e, ps_=ps_, hh=hh, pfin=pfin, ub=ub: e.matmul(
                        ub[64:128, hh * 64:(hh + 1) * 64], lhsT=pfin[ps_, hh, :], rhs=GXs[ps_, hh, :],
                        start=True, stop=True),
                        reads=[kpfin, "GXs"], writes=[ubk])
                P.op("dve", lambda e, c=c: e.tensor_copy(
                    out=Zt[64:128, c, 0:2, :].rearrange("p j f -> p (j f)"), in_=psSm[64:128, 0:256]),
                    reads=["psSm"], writes=[("ZtU", par, c)])
                P.op("act", lambda e, c=c: e.activation(
                    out=Zt[64:128, c, 2:4, :].rearrange("p j f -> p (j f)"), in_=psNM[1][64:128, 0:256],
                    func=AF.Copy),
                    reads=[("psNM", 1)], writes=[("ZtU2", par, c)])
                zkeys = [("ZtV", par, j) for j in range(4)] + [("ZtU", par, c), ("ZtU2", par, c)]
                ypb = 64 * (c % 2)
                for h in range(8):
                    j, pb, e2 = h // 2, 64 * (h % 2), h % 2
                    P.op("pe", lambda e, h=h, j=j, pb=pb, c=c, ypb=ypb, sb_old=sb_old: e.matmul(
                        psY[ypb:ypb + 64, h * 64:(h + 1) * 64], lhsT=QT[:, h, c, 0:64],
                        rhs=sb_old[:, j, :], start=True, stop=False),
                        reads=[("QT", par, j), kold], writes=["psY"])
                    P.op("pe", lambda e, h=h, j=j, e2=e2, c=c, ypb=ypb: e.matmul(
                        psY[ypb:ypb + 64, h * 64:(h + 1) * 64], lhsT=MTs[:, h, 0:64],
                        rhs=Zt[:, c, j, e2 * 64:(e2 + 1) * 64], start=False, stop=True),
                        reads=["MTs"] + zkeys, writes=["psY"])
                for h in range(8):
                    j, pb, e2 = h // 2, 64 * (h % 2), h % 2
                    P.op("pe", lambda e, h=h, j=j, pb=pb, e2=e2, c=c: e.matmul(
                        psSm[pb:pb + 64, j * 64:(j + 1) * 64], lhsT=PPt[:, c, j, e2 * 64:(e2 + 1) * 64],
                        rhs=Zt[:, c, j, e2 * 64:(e2 + 1) * 64], start=True, stop=True),
                        reads=[("PPt", par, j)] + zkeys, writes=["psSm"])
                P.op("dve", lambda e: e.tensor_tensor(out=Sst[:, :, :].rearrange("p j v -> p (j v)"),
                                                      in0=psSm[:, 0:256],
                                                      in1=Sdec[:, :, :].rearrange("p j v -> p (j v)"), op=ALU.add),
                     reads=["psSm", "Sdec"], writes=["Sst"])
                P.op("act", lambda e, sb_new=sb_new: e.activation(out=sb_new[:, :, :], in_=Sst[:, :, :],
                                                                  func=AF.Copy),
                     reads=["Sst"], writes=[knew])
                if c % 2 == 1:
                    ti = c // 2
                    P.op("act", lambda e: e.activation(out=ycp[:, :], in_=psY[:, :], func=AF.Copy),
                         reads=["psY"], writes=["ycp"])
                    P.op("act", lambda e: e.activation(out=ysq[:, :], in_=psY[:, :], func=AF.Square),
                         reads=["psY"], writes=["ysq"])
                    P.op("dve", lambda e: e.tensor_reduce(out=st1[:, :],
                                                          in_=ycp[:, :].rearrange("p (h v) -> p h v", v=64),
                                                          axis=AX.X, op=ALU.add),
                         reads=["ycp"], writes=["st1"])
                    P.op("dve", lambda e: e.tensor_reduce(out=st2[:, :],
                                                          in_=ysq[:, :].rearrange("p (h v) -> p h v", v=64),
                                                          axis=AX.X, op=ALU.add),
                         reads=["ysq"], writes=["st2"])
                    P.op("dve", lambda e: e.tensor_scalar(out=stm[:, :], in0=st1[:, :], scalar1=1.0 / 64,
                                                          scalar2=None, op0=ALU.mult),
                         reads=["st1"], writes=["stm"])
                    P.op("dve", lambda e: e.tensor_tensor(out=stv[:, :], in0=stm[:, :], in1=stm[:, :], op=ALU.mult),
                         reads=["stm"], writes=["stv"])
                    P.op("dve", lambda e: e.scalar_tensor_tensor(out=stv[:, :], in0=st2[:, :], scalar=1.0 / 64,
                                                                 in1=stv[:, :], op0=ALU.mult, op1=ALU.subtract),
                         reads=["st2", "stv"], writes=["stv"])
                    P.op("pool", lambda e: e.tensor_scalar(out=stv[:, :], in0=stv[:, :], scalar1=1.0, scalar2=GN_EPS,
                                                           op0=ALU.mult, op1=ALU.add),
                         reads=["stv"], writes=["stv"])
                    P.op("pool", lambda e: e.tensor_tensor(out=stv[:, :], in0=stv[:, :], in1=neghalf[:, 0:8],
                                                           op=ALU.pow),
                         reads=["stv", "neghalf"], writes=["stv"])
                    P.op("pool", lambda e: e.tensor_tensor(
                        out=ycp[:, :].rearrange("p (h v) -> p h v", v=64),
                        in0=ycp[:, :].rearrange("p (h v) -> p h v", v=64),
                        in1=stm[:, :].unsqueeze(2).to_broadcast([128, 8, 64]), op=ALU.subtract),
                        reads=["ycp", "stm"], writes=["ycp"])
                    P.op("pool", lambda e, ti=ti: e.tensor_tensor(
                        out=ynb[ti][:, :].rearrange("p (h v) -> p h v", v=64),
                        in0=ycp[:, :].rearrange("p (h v) -> p h v", v=64),
                        in1=stv[:, :].unsqueeze(2).to_broadcast([128, 8, 64]), op=ALU.mult),
                        reads=["ycp", "stv"], writes=[("ynb", ti)])

            P.in_region = False
            if upto < 4:
                return
            P.tag = (tb, "E")
            P.section = "E"
            for ti in range(NTT):
                for j in range(4):
                    P.op("pe", lambda e, ti=ti, j=j: e.transpose(
                        out=psYb[:, j * TB + ti * 128:j * TB + (ti + 1) * 128],
                        in_=ynb[ti][:, j * 128:(j + 1) * 128], identity=identb[:, :]),
                        reads=[("ynb", ti), "identb"], writes=["psY"])
            for j in range(4):
                P.op("dve", lambda e, j=j: e.tensor_scalar(out=t1[:, :], in0=psYb[:, j * TB:(j + 1) * TB],
                                                           scalar1=gng[:, j:j + 1], scalar2=gnb[:, j:j + 1],
                                                           op0=ALU.mult, op1=ALU.add),
                     reads=["psY", "gng", "gnb"], writes=["t1"])
                P.op("pool", lambda e, j=j: e.tensor_tensor(out=t1[:, :], in0=t1[:, :], in1=bon[:, j, :],
                                                            op=ALU.add),
                     reads=["t1", ("bon", par, j)], writes=["t1"])
                P.op("pool", lambda e, j=j: e.tensor_tensor(out=ycat[:, 4 + j, :], in0=t1[:, :], in1=sgb[:, j, :],
                                                            op=ALU.mult),
                     reads=["t1", ("sgb", par, j)], writes=[("ycat", 4 + j)])

            ykeys = [("ycat", k) for k in range(8)]
            pso = [psM1, psG]
            psok = ["psM1", "psG"]
            for i in range(NTT):
                col = S // 128 + tb * NTT + i
                r0 = t0 + i * 128
                xi = erot[0] % 2
                erot[0] += 1
                xb = xe[xi]
                dma("sync", xb[:, :], x[r0:r0 + 128, :], f"de{xi}", writes=[("xe", xi)])
                for hf in range(2):
                    for kc in range(8):
                        P.op("pe", lambda e, hf=hf, kc=kc, i=i: e.matmul(
                            pso[hf][:, :], lhsT=ycat[:, kc, i * 128:(i + 1) * 128],
                            rhs=woutb[:, kc, hf * 512:(hf + 1) * 512], start=(kc == 0), stop=(kc == 7)),
                            reads=ykeys + [("woutb", kc)], writes=[psok[hf]])
                    P.op("dve", lambda e, hf=hf, xb=xb: e.tensor_tensor(
                        out=xb[:, hf * 512:(hf + 1) * 512], in0=pso[hf][:, :],
                        in1=xb[:, hf * 512:(hf + 1) * 512], op=ALU.add),
                        reads=[psok[hf], ("xe", xi)], writes=[("xe", xi)])
                P.op("act", lambda e, col=col, xb=xb: e.activation(out=junk[:, :], in_=xb[:, :], func=AF.Square,
                                                                   accum_out=ssx[:, col:col + 1]),
                     reads=[("xe", xi)], writes=["junk", ("ssx", col)])
                P.op("pool", lambda e, col=col: e.tensor_scalar(out=sdx[:, col:col + 1], in0=ssx[:, col:col + 1],
                                                                scalar1=1.0 / D, scalar2=NORM_EPS,
                                                                op0=ALU.mult, op1=ALU.add),
                     reads=[("ssx", col)], writes=[("sdx", col)])
                P.op("pool", lambda e, col=col: e.tensor_tensor(out=rsx[:, col:col + 1], in0=sdx[:, col:col + 1],
                                                                in1=neghalf[:, 0:1], op=ALU.pow),
                     reads=[("sdx", col), "neghalf"], writes=[("rsx", col)])
                P.op("dve", lambda e, col=col, xb=xb: e.scalar_tensor_tensor(
                    out=xb[:, :], in0=xb[:, :], scalar=rsx[:, col:col + 1], in1=fgain[:, :],
                    op0=ALU.mult, op1=ALU.mult),
                    reads=[("xe", xi), ("rsx", col), "fgain"], writes=[("xe", xi)])
                dma("sync", out[r0:r0 + 128, :], xb[:, :], f"do{xi}", reads=[("xe", xi)], writes=[("outd", xi)])

        nbl = nblocks if upto >= 1 else 0
        if nbl:
            P.only = {"A"}
            block_body(0)
        for tb in range(nbl):
            P.only = {"B"}
            block_body(tb)
            if tb + 1 < nbl:
                P.only = {"A"}
                block_body(tb + 1)
            P.only = {"D", "E"}
            block_body(tb)
        P.only = None
        P.section = None
        P.tag = None

        for name, ap, key in dbg_dumps:
            if ap is None:
                continue
            dma("sync", dbg_out[name], ap, "ddbg", reads=[key])

        _INFO["sbuf_left"] = nc.sbuf_bytes_remaining
        if do_schedule:
            P.schedule()
            cands = [(P.est_makespan, {e: list(v) for e, v in P.ops.items()})]
            if SCHED_TRIALS > 0:
                import random
                for t in range(SCHED_TRIALS):
                    P.prio_rng = random.Random(1234 + t)
                    P.prio_noise = SCHED_NOISE
                    P.schedule()
                    cands.append((P.est_makespan, {e: list(v) for e, v in P.ops.items()}))
                P.prio_rng = None
            cands.sort(key=lambda c: c[0])
            P.est_makespan, P.ops = cands[min(SCHED_PICK, len(cands) - 1)]
            _INFO["est_us"] = P.est_makespan
        P.finalize()
        keys = P.sem_keys()
        sems = {k: es.enter_context(nc.semaphore("s_" + k)) for k in keys}
        out_keys = [k for k in keys if k.startswith("do") or k == "ddbg"]
        with nc.Block() as block:
            @block.sync
            def _(eng):
                P.run_engine("sync", eng, sems, final_waits=out_keys)

            @block.scalar
            def _(eng):
                P.run_engine("act", eng, sems)

            @block.vector
            def _(eng):
                P.run_engine("dve", eng, sems)

            @block.gpsimd
            def _(eng):
                P.run_engine("pool", eng, sems)

            @block.tensor
            def _(eng):
                P.run_engine("pe", eng, sems)
    return nc


_CACHE = {}
_INFO = {}


def _prep_inputs(inputs):
    f = lambda a: np.ascontiguousarray(np.asarray(a, dtype=np.float32))
    shared = {
        "norm_gain": f(inputs["norm_gain"]).reshape(D),
        "w_in": f(inputs["w_in"]).reshape(D, DIN),
        "pool_w": f(inputs["pool_w"]).reshape(4, 128, 128),
        "pool_scale": f(inputs["pool_scale"]).reshape(512),
        "shift_mu": f(inputs["shift_mu"]).reshape(1664),
        "w0": f(inputs["w0"]).reshape(512),
        "w_up": f(inputs["w_up"]).reshape(64, 512),
        "a0": f(inputs["a0"]).reshape(512),
        "a_up": f(inputs["a_up"]).reshape(64, 512),
        "k_k": f(inputs["k_k"]).reshape(512),
        "k_a": f(inputs["k_a"]).reshape(512),
        "r_k": f(inputs["r_k"]).reshape(512),
        "gn_gain": f(inputs["gn_gain"]).reshape(512),
        "gn_bias": f(inputs["gn_bias"]).reshape(512),
        "w_out": f(inputs["w_out"]).reshape(D, D),
        "final_gain": f(inputs["final_gain"]).reshape(D),
        "cst": _make_consts(),
    }
    xs = f(inputs["x"])
    return [dict(shared, x=xs[b]) for b in range(xs.shape[0])]


def kernel(**inputs):
    in_maps = _prep_inputs(inputs)
    if "nc" not in _CACHE:
        _CACHE["nc"] = build()
    nc = _CACHE["nc"]
    res = run_bass_kernel_spmd(nc, in_maps, core_ids=list(range(8)))
    return np.stack([np.asarray(r["out"], dtype=np.float32) for r in res.results], axis=0)
```

```python
import contextlib
import numpy as np
import concourse.bass as bass
import concourse.mybir as mybir
from concourse.bass_utils import run_bass_kernel_spmd

F32 = mybir.dt.float32
BF16 = mybir.dt.bfloat16
AF = mybir.ActivationFunctionType
ALU = mybir.AluOpType
AX = mybir.AxisListType

S = 2048
D = 1024
DIN = 3200
TB = 256
NB = S // TB
NCH = TB // 64
NTT = TB // 128
C0 = float(np.exp(-0.5))
NORM_EPS = 1e-6
GN_EPS = 64e-5
WST = 800

CO_ID = 0
CO_BONES = 128
CO_MMT = 256
CO_ML = CO_MMT + 128
CO_IREP = CO_ML + 128
CO_RST = CO_IREP + 64
CO_ICNT = CO_RST + TB
CST_W = CO_ICNT + 60


def _make_consts():
    c = np.zeros((128, CST_W), np.float32)
    p = np.arange(128)
    c[:, CO_ID:CO_ID + 128] = np.eye(128, dtype=np.float32)
    c[:, CO_BONES:CO_BONES + 128] = (p[:, None] // 64 == p[None, :] // 64).astype(np.float32)
    s = (p % 64)[:, None]
    t = (p % 64)[None, :]
    colr = (p[None, :] < 64)
    m = np.where(colr, (s <= t), (s < t)).astype(np.float32)
    c[:, CO_MMT:CO_MMT + 128] = m
    tt = (p % 64)[:, None]
    ss = np.arange(64)[None, :]
    c[:, CO_ML:CO_ML + 64] = (ss < tt).astype(np.float32)
    c[:, CO_ML + 64:CO_ML + 128] = (tt < ss).astype(np.float32)
    c[:, CO_IREP:CO_IREP + 64] = (ss == tt).astype(np.float32)
    rst = np.ones(TB, np.float32)
    rst[::64] = 0.0
    c[:, CO_RST:CO_RST + TB] = rst[None, :]
    for g, w in enumerate((2, 4, 8, 16)):
        for tq in range(15):
            c[:, CO_ICNT + g * 15 + tq] = 1.0 / min(tq + 1, w)
    return c


_INFO = {}
SCHED_TRIALS = 40
SCHED_DPEN = 0.0
SCHED_PICK = 0
SCHED_NOISE = 0.01


class _Op:
    __slots__ = ("eng", "fn", "deps", "dma", "semval", "signal", "group", "dur", "lat", "idx", "prio", "succ",
                 "npred", "ready_t", "fin", "tag", "st", "pos", "wdeps", "mode", "sub")


class _Rec:
    def __getattr__(self, name):
        def f(*a, **k):
            return (name, a, k)
        return f


def _free_size(ap):
    n = 1
    for d in list(ap.shape)[1:]:
        n *= int(d)
    return n


def _estimate(eng, fn, dma):
    name, a, k = fn(_Rec())
    out = k.get("out", a[0] if a else None)
    if dma is not None:
        nbytes = _free_size(out) * int(out.shape[0]) * 4
        return 0.06, 2.0 + nbytes / 150e3, None
    if eng == "pe":
        def r32(v):
            return 32 if v <= 32 else (64 if v <= 64 else 128)
        if name == "transpose":
            n = int(k["in_"].shape[0])
            mode = ("T", r32(int(k["in_"].shape[0])), r32(_free_size(k["in_"])))
        else:
            n = _free_size(k["rhs"])
            mode = ("M", r32(int(k["lhsT"].shape[0])), r32(_free_size(k["lhsT"])))
        d = (0.025 + 0.0006 * n) if n >= 256 else (0.03 + 0.0002 * n)
        return d, d + 0.12, mode
    n = _free_size(out)
    if eng == "act":
        d = 0.22 + 0.00075 * n
    elif eng == "dve":
        d = 0.12 + 0.00105 * n
    else:
        d = 0.2 + 0.0021 * n
    return d, d + 0.08, None


class Plan:
    ENGS = ("sync", "act", "dve", "pool", "pe")

    def __init__(self):
        self.ops = {e: [] for e in self.ENGS}
        self.lastw = {}
        self.readers = {}
        self.dma_eng = {}
        self.nops = 0

    def op(self, eng, fn, reads=(), writes=(), dma=None, group=None):
        only = getattr(self, "only", None)
        if only is not None and getattr(self, "section", None) not in only:
            return None
        subonly = getattr(self, "subonly", None)
        if subonly is not None and getattr(self, "sub", None) != subonly:
            return None
        if getattr(self, "in_region", False):
            if self.region_count >= self.region_budget:
                return None
            self.region_count += 1
        o = _Op()
        o.eng = eng
        o.fn = fn
        o.dma = dma
        o.deps = {}
        o.signal = False
        o.semval = None
        o.group = group
        o.dur, o.lat, o.mode = _estimate(eng, fn, dma)
        if dma is not None:
            assert self.dma_eng.setdefault(dma, eng) == eng
        for b in reads:
            w = self.lastw.get(b)
            if w is not None:
                o.deps[w] = True
        for b in writes:
            w = self.lastw.get(b)
            if w is not None and w not in o.deps:
                o.deps[w] = False
            for r in self.readers.get(b, ()):
                if r not in o.deps:
                    o.deps[r] = False
        for b in writes:
            self.lastw[b] = o
            self.readers[b] = []
        for b in reads:
            if b not in writes:
                self.readers.setdefault(b, []).append(o)
        o.idx = self.nops
        o.tag = getattr(self, "tag", None)
        o.sub = getattr(self, "sub", None)
        self.nops += 1
        self.ops[eng].append(o)
        return o

    @staticmethod
    def _needs_wait(o, d, raw):
        if d.dma is not None or o.dma is not None:
            return True
        if d.eng != o.eng:
            return True
        if o.eng == "pe":
            return False
        return True

    def schedule(self):
        allops = []
        for e in self.ENGS:
            allops.extend(self.ops[e])
        for o in allops:
            o.succ = []
            o.npred = len(o.deps)
        for o in allops:
            for d in o.deps:
                d.succ.append(o)
        allops.sort(key=lambda o: o.idx)
        for o in reversed(allops):
            m = 0.0
            for sc in o.succ:
                if sc.prio > m:
                    m = sc.prio
            o.prio = o.lat + m
        rnd = getattr(self, "prio_rng", None)
        if rnd is not None:
            for o in allops:
                o.prio *= 1.0 + self.prio_noise * (rnd.random() - 0.5)
        free = {e: 0.0 for e in self.ENGS}
        ready = {e: [] for e in self.ENGS}
        for o in allops:
            o.ready_t = 0.0
            if o.npred == 0:
                ready[o.eng].append(o)
        order = {e: [] for e in self.ENGS}
        remaining = len(allops)
        SYNC = 0.25
        MODE_SW = 0.18
        pe_mode = None
        while remaining:
            best = None
            for e in self.ENGS:
                lst = ready[e]
                if not lst:
                    continue
                fe = free[e]
                cand = None
                cs = None
                for o in lst:
                    st = o.ready_t if o.ready_t > fe else fe
                    if e == "pe" and o.mode != pe_mode:
                        st += MODE_SW
                    pen = SCHED_DPEN if (o.tag is not None and o.tag[1] in ("D", "E")) else 0.0
                    key = (st + pen, -o.prio, o.idx, st)
                    if cs is None or key < cs:
                        cs = key
                        cand = o
                if best is None or cs < best[0]:
                    best = (cs, cand)
            cs, o = best
            st = cs[3]
            if o.eng == "pe":
                pe_mode = o.mode
            ready[o.eng].remove(o)
            free[o.eng] = st + o.dur
            o.st = st
            o.fin = st + o.lat
            order[o.eng].append(o)
            remaining -= 1
            for sc in o.succ:
                if sc.eng == o.eng == "pe":
                    t = st + o.dur
                else:
                    t = o.fin + (SYNC if sc.eng != o.eng else 0.05)
                if t > sc.ready_t:
                    sc.ready_t = t
                sc.npred -= 1
                if sc.npred == 0:
                    ready[sc.eng].append(sc)
        self.ops = order
        self.est_makespan = max(free.values())

    def finalize(self):
        for e in self.ENGS:
            for i, o in enumerate(self.ops[e]):
                o.pos = i
        for e in self.ENGS:
            for o in self.ops[e]:
                last = {}
                o.wdeps = []
                for d, raw in o.deps.items():
                    if not self._needs_wait(o, d, raw):
                        continue
                    if d.dma is not None:
                        o.wdeps.append(d)
                        continue
                    cur = last.get(d.eng)
                    if cur is None or d.pos > cur.pos:
                        last[d.eng] = d
                for d in last.values():
                    d.signal = True
                    o.wdeps.append(d)
        self.dma_tot = {}
        groups = {}
        for e in self.ENGS:
            c = 0
            for o in self.ops[e]:
                if o.dma is not None:
                    v = self.dma_tot.get(o.dma, 0) + 16
                    self.dma_tot[o.dma] = v
                    o.semval = v
                    if o.group is not None:
                        groups.setdefault((o.dma, o.group), []).append(o)
                elif o.signal:
                    c += 1
                    o.semval = c
        for lst in groups.values():
            mx = max(x.semval for x in lst)
            for x in lst:
                x.semval = mx

    def sem_keys(self):
        return list(self.ENGS) + sorted(self.dma_tot.keys())

    def run_engine(self, e, eng, sems, final_waits=()):
        known = {}
        for o in self.ops[e]:
            waits = {}
            for d in o.wdeps:
                key = d.dma if d.dma is not None else d.eng
                if d.semval > waits.get(key, 0):
                    waits[key] = d.semval
            for key, v in waits.items():
                if known.get(key, 0) >= v:
                    continue
                eng.wait_ge(sems[key], v)
                known[key] = v
            inst = o.fn(eng)
            if o.dma is not None:
                inst.then_inc(sems[o.dma], 16)
            elif o.signal:
                inst.then_inc(sems[e], 1)
        for key in final_waits:
            eng.wait_ge(sems[key], self.dma_tot[key])


def build(debug=None, nblocks=NB, upto=9, cbud=10 ** 9, do_schedule=True):
    nc = bass.Bass("TRN2", target_bir_lowering=False)
    dt = nc.dram_tensor
    x = dt("x", [S, D], F32, kind="ExternalInput").ap()
    norm_gain = dt("norm_gain", [D], F32, kind="ExternalInput").ap()
    w_in = dt("w_in", [D, DIN], F32, kind="ExternalInput").ap()
    pool_w = dt("pool_w", [4, 128, 128], F32, kind="ExternalInput").ap()
    pool_scale = dt("pool_scale", [512], F32, kind="ExternalInput").ap()
    shift_mu = dt("shift_mu", [1664], F32, kind="ExternalInput").ap()
    w0 = dt("w0", [512], F32, kind="ExternalInput").ap()
    w_up = dt("w_up", [64, 512], F32, kind="ExternalInput").ap()
    a0 = dt("a0", [512], F32, kind="ExternalInput").ap()
    a_up = dt("a_up", [64, 512], F32, kind="ExternalInput").ap()
    k_k = dt("k_k", [512], F32, kind="ExternalInput").ap()
    k_a = dt("k_a", [512], F32, kind="ExternalInput").ap()
    r_k = dt("r_k", [512], F32, kind="ExternalInput").ap()
    gn_gain = dt("gn_gain", [512], F32, kind="ExternalInput").ap()
    gn_bias = dt("gn_bias", [512], F32, kind="ExternalInput").ap()
    w_out = dt("w_out", [D, D], F32, kind="ExternalInput").ap()
    final_gain = dt("final_gain", [D], F32, kind="ExternalInput").ap()
    cst_d = dt("cst", [128, CST_W], F32, kind="ExternalInput").ap()
    out = dt("out", [S, D], F32, kind="ExternalOutput").ap()
    dbg_out = {}
    if debug:
        for name, shp in debug.items():
            dbg_out[name] = dt("dbg_" + name, list(shp), F32, kind="ExternalOutput").ap()

    P = Plan()
    P.region_budget = cbud
    P.region_count = 0
    es = contextlib.ExitStack()
    with es:
        def sb(name, shape, dtype):
            return es.enter_context(nc.sbuf_tensor(name, list(shape), dtype))

        def psum(name, shape, dtype):
            return es.enter_context(nc.psum_tensor(name, list(shape), dtype))

        winb = sb("winb", [128, 8, DIN], BF16)
        woutb = sb("woutb", [128, 8, D], BF16)
        cst = sb("cst_sb", [128, CST_W], F32)
        poolw = sb("poolw", [128, 4, 128], BF16)
        lup = sb("lup", [128, 512], BF16)
        identb = sb("identb", [128, 128], BF16)
        bones = sb("bones", [128, 128], BF16)
        blkrk = sb("blkrk", [128, 128], BF16)
        fgain = sb("fgain", [128, D], F32)
        ngv = sb("ngv", [128, 8], F32)
        muv = sb("muv", [128, 13], F32)
        w0v = sb("w0v", [128, 4], F32)
        a0v = sb("a0v", [128, 4], F32)
        kkv = sb("kkv", [128, 4], F32)
        kav = sb("kav", [128, 4], F32)
        omka = sb("omka", [128, 4], F32)
        rkv = sb("rkv", [128, 4], F32)
        gng = sb("gng", [128, 4], F32)
        gnb = sb("gnb", [128, 4], F32)
        pscv = sb("pscv", [128, 4], F32)
        halo = sb("halo", [128, 13], F32)

        xt = [sb(f"xt{i}", [128, D], F32) for i in range(2)]
        hbf = sb("hbf", [128, D], BF16)
        junk = sb("junk", [128, D], BF16)
        ssx = sb("ssx", [128, 2 * S // 128], F32)
        sdx = sb("sdx", [128, 2 * S // 128], F32)
        rsx = sb("rsx", [128, 2 * S // 128], F32)
        hT2 = [sb(f"hT{i}", [128, 8, TB], BF16) for i in range(2)]

        NZS = 5
        zs = [sb(f"zs{i}", [128, TB], F32) for i in range(NZS)]
        dtmp = sb("dtmp", [128, TB], F32)
        UW = 15 + TB
        ua = [sb(f"ua{i}", [128, UW], F32) for i in range(4)]
        pA = sb("pA", [128, UW], F32)
        pB = sb("pB", [128, UW], F32)
        pooled = sb("pooled", [128, TB], BF16)
        sga = sb("sga", [128, 4, TB], BF16)
        sgb2 = [sb(f"sgb{i}", [128, 4, TB], BF16) for i in range(2)]
        lobf = sb("lobf", [128, TB], BF16)
        thg = sb("thg", [128, TB], F32)

        t_sg = sb("t_sg", [128, TB], F32)
        t_a = sb("t_a", [128, TB], F32)
        t_csg = sb("t_csg", [128, TB], F32)
        t_e2 = sb("t_e2", [128, TB], F32)
        t_e1 = sb("t_e1", [128, TB], F32)
        t_e3 = sb("t_e3", [128, TB], F32)
        t_e4 = sb("t_e4", [128, TB], F32)
        t_kkn = sb("t_kkn", [128, TB], F32)
        t_rn = sb("t_rn", [128, TB], F32)
        t_kp = sb("t_kp", [128, TB], F32)
        t_b = sb("t_b", [128, TB], F32)
        kk2bf = sb("kk2bf", [128, TB], BF16)
        rkbf = sb("rkbf", [128, TB], BF16)

        QT2 = [sb(f"QT{i}", [128, 8, NCH, 128], BF16) for i in range(2)]
        PT2 = [sb(f"PT{i}", [128, 4, NCH, 128], BF16) for i in range(2)]
        PPsrc = sb("PPsrc", [128, NCH, 128], BF16)
        vbf = sb("vbf", [128, TB], BF16)
        PPt2 = [sb(f"PPt{i}", [128, NCH, 4, 128], BF16) for i in range(2)]
        Zt2 = [sb(f"Zt{i}", [128, NCH, 4, 128], BF16) for i in range(2)]
        bon2 = [sb(f"bon{i}", [128, 4, TB], BF16) for i in range(2)]
        gam2 = [sb(f"gam{i}", [128, 4, NCH], F32) for i in range(2)]

        MTs = sb("MTs", [128, 8, 128], BF16)
        MM = [sb(f"MM{i}", [128, 2, 4, 64], BF16) for i in range(2)]
        PTch = [sb(f"PTch{i}", [128, 4, 64], BF16) for i in range(4)]
        GXs = sb("GXs", [128, 4, 64], BF16)
        Xs = sb("Xs", [128, 4, 64], F32)
        Sst = sb("Sst", [128, 4, 64], F32)
        Sdec = sb("Sdec", [128, 4, 64], F32)
        Sbf = [sb(f"Sbf{i}", [128, 4, 64], BF16) for i in range(2)]
        ycp = sb("ycp", [128, 512], F32)
        ysq = sb("ysq", [128, 512], F32)
        ynb = [sb(f"ynb{i}", [128, 512], BF16) for i in range(NTT)]
        st1 = sb("st1", [128, 8], F32)
        st2 = sb("st2", [128, 8], F32)
        stm = sb("stm", [128, 8], F32)
        stv = sb("stv", [128, 8], F32)
        t1 = sb("t1", [128, TB], F32)
        ycat = sb("ycat", [128, 8, TB], BF16)
        xe = [sb(f"xe{i}", [128, D], F32) for i in range(2)]

        psNM = [psum(f"psNM{i}", [128, 512], F32) for i in range(2)]
        psT = psum("psT", [128, 512], F32)
        psSm = psum("psSm", [128, 512], F32)
        psM1 = psum("psM1", [128, 512], F32)
        psP = psum("psP", [128, 512], F32)
        psG = psum("psG", [128, 512], F32)
        psY = psum("psY", [128, 512], F32)
        psTb = psT[:, :].bitcast(BF16)
        psYb = psY[:, :].bitcast(BF16)
        psM1b = psM1[:, :].bitcast(BF16)

        dbg_dumps = []

        def dump(name, ap, key):
            if debug and name in dbg_out:
                dbg_dumps.append((name, ap, key))

        def dma(eng, out_ap, in_ap, key, reads=(), writes=(), group=None, slow=False):
            if slow:
                fn = lambda e, o=out_ap, i=in_ap: e.dma_start(out=o, in_=i, allow_slow_non_contiguous=True)
            else:
                fn = lambda e, o=out_ap, i=in_ap: e.dma_start(out=o, in_=i)
            return P.op(eng, fn, reads=reads, writes=writes, dma=key, group=group)

        dma("sync", cst[:, :], cst_d[:, :], "dc", writes=["cst"], group=0)

        def vec_load(tile, src, n, key):
            dma("sync", tile[:, :], src.rearrange("(j p) -> p j", p=128), "dc", writes=[key], group=0, slow=True)

        vec_load(ngv, norm_gain, 8, "ngv")
        vec_load(muv, shift_mu, 13, "muv")
        vec_load(w0v, w0, 4, "w0v")
        vec_load(a0v, a0, 4, "a0v")
        vec_load(kkv, k_k, 4, "kkv")
        vec_load(kav, k_a, 4, "kav")
        vec_load(rkv, r_k, 4, "rkv")
        vec_load(gng, gn_gain, 4, "gng")
        vec_load(gnb, gn_bias, 4, "gnb")
        vec_load(pscv, pool_scale, 4, "pscv")
        dma("sync", fgain[:, :], final_gain.partition_broadcast(128), "dc", writes=["fgain"], group=0)

        for kc in range(8):
            dma("pool", woutb[:, kc, :], w_out[kc * 128:(kc + 1) * 128, :], "dw", writes=[("woutb", kc)], group=0)
        dma("pool", poolw[:, :, :], pool_w.rearrange("g c d -> c g d"), "dw", writes=["poolw"], group=0)
        dma("pool", lup[0:64, :], w_up[:, :], "dw", writes=[("lup", 0)], group=0)
        dma("pool", lup[64:128, :], a_up[:, :], "dw", writes=[("lup", 1)], group=0)

        P.op("pool", lambda e: e.memset(halo[:, :], 0.0), writes=["halo"])
        P.op("pool", lambda e: e.memset(Sst[:, :, :], 0.0), writes=["Sst"])
        P.op("pool", lambda e: e.memset(Sbf[0][:, :, :], 0.0), writes=["Sbf0"])
        for g in range(4):
            P.op("pool", lambda e, g=g: e.memset(ua[g][:, 0:15], 0.0), writes=[("ua", g)])
        P.op("dve", lambda e: e.tensor_copy(out=identb[:, :], in_=cst[:, CO_ID:CO_ID + 128]),
             reads=["cst"], writes=["identb"])
        P.op("dve", lambda e: e.tensor_copy(out=bones[:, :], in_=cst[:, CO_BONES:CO_BONES + 128]),
             reads=["cst"], writes=["bones"])
        neghalf = sb("neghalf", [128, 16], F32)
        omm = sb("omm", [128, 13], F32)
        P.op("pool", lambda e: e.memset(neghalf[:, :], -0.5), writes=["neghalf"])
        half = sb("half", [128, 1], F32)
        P.op("pool", lambda e: e.memset(half[:, :], 0.5), writes=["half"])
        P.op("dve", lambda e: e.tensor_scalar(out=omm[:, :], in0=muv[:, :], scalar1=-1.0, scalar2=1.0,
                                              op0=ALU.mult, op1=ALU.add),
             reads=["muv"], writes=["omm"])
        for vt, vk in ((w0v, "w0v"), (a0v, "a0v"), (pscv, "pscv"), (gng, "gng"), (gnb, "gnb"), (rkv, "rkv")):
            P.op("dve", lambda e, vt=vt: e.tensor_scalar(out=vt[:, :], in0=vt[:, :], scalar1=0.5, scalar2=None,
                                                         op0=ALU.mult),
                 reads=[vk], writes=[vk])
        P.op("dve", lambda e: e.tensor_scalar(out=omka[:, :], in0=kav[:, :], scalar1=-1.0, scalar2=1.0,
                                              op0=ALU.mult, op1=ALU.add),
             reads=["kav"], writes=["omka"])

        cast_engs = ("dve", "act", "dve", "act", "pool")
        qstage = QT2[1][:, :, :, :].rearrange("p h c t -> p (h c t)").bitcast(F32)
        stg = [(qstage[:, 0:WST], [("QT", 1, 0), ("QT", 1, 1)], "dq0"),
               (qstage[:, 1024:1024 + WST], [("QT", 1, 2), ("QT", 1, 3)], "dq1"),
               (xe[0][:, 0:WST], [("xe", 0)], "de0"), (xe[1][:, 0:WST], [("xe", 1)], "de1")]
        n = 0
        for pc in (3, 1, 2, 0):
            for kc in range(8):
                sidx = n % 4
                stg_t, stg_key, stg_dk = stg[sidx]
                dma("sync", stg_t, w_in[kc * 128:(kc + 1) * 128, pc * WST:(pc + 1) * WST],
                    stg_dk, writes=stg_key)
                ce = cast_engs[n % 5]
                o_ap = winb[:, kc, pc * WST:(pc + 1) * WST]
                i_ap = stg_t
                g_ap = ngv[:, kc:kc + 1]
                if ce == "act":
                    fn = lambda e, o=o_ap, i=i_ap, g=g_ap: e.activation(out=o, in_=i, func=AF.Copy, scale=g)
                elif ce == "pool":
                    fn = lambda e, o=o_ap, i=i_ap, g=g_ap: e.tensor_scalar(out=o, in0=i, scalar1=g, scalar2=0.0,
                                                                          op0=ALU.mult, op1=ALU.add)
                else:
                    fn = lambda e, o=o_ap, i=i_ap, g=g_ap: e.tensor_scalar(out=o, in0=i, scalar1=g, scalar2=None,
                                                                          op0=ALU.mult)
                P.op(ce, fn, reads=stg_key + ["ngv"], writes=[("winb", kc, pc)])
                n += 1

        for par in range(2):
            P.op("dve", lambda e, par=par: e.memset(QT2[par][:, :, :, :].rearrange("p h c t -> p (h c t)"), 0.0),
                 writes=[("QT", par, j) for j in range(4)])
        blkrk4 = sb("blkrk4", [128, 4, 128], BF16)
        for j in range(4):
            P.op("dve", lambda e, j=j: e.tensor_scalar(out=blkrk4[:, j, :], in0=cst[:, CO_BONES:CO_BONES + 128],
                                                       scalar1=rkv[:, j:j + 1], scalar2=None, op0=ALU.mult),
                 reads=["cst", "rkv"], writes=["blkrk4"])

        FT_ORDER = [20]
        for j in range(4):
            FT_ORDER += [12 + j, 8 + j, 16 + j, 21 + j]
        for g in range(4):
            FT_ORDER += [g, 4 + g]
        zrot = [0]
        psz_rot = [0]
        sm_rot = [0]
        xrot = [0]
        erot = [0]

        def sm_half():
            h = sm_rot[0] % 2
            sm_rot[0] += 1
            return psT[:, h * 256:(h + 1) * 256], "psT"

        def block_body(tb):
            t0 = tb * TB
            par = tb % 2
            QT, PT, PPt, Zt = QT2[par], PT2[par], PPt2[par], Zt2[par]
            bon, gam, sgb = bon2[par], gam2[par], sgb2[par]
            hT = hT2[par]
            khT = "hT%d" % par
            P.tag = (tb, "A")
            P.section = "A"
            for i in range(NTT):
                col = tb * NTT + i
                xi = xrot[0] % 2
                xrot[0] += 1
                r0 = t0 + i * 128
                dma("sync", xt[xi][:, :], x[r0:r0 + 128, :], f"dx{xi}", writes=[("xt", xi)])
                P.op("act", lambda e, xi=xi, col=col: e.activation(out=hbf[:, :], in_=xt[xi][:, :], func=AF.Square,
                                                                   accum_out=ssx[:, col:col + 1]),
                     reads=[("xt", xi)], writes=["hbf", ("ssx", col)])
                P.op("pool", lambda e, col=col: e.tensor_scalar(out=sdx[:, col:col + 1], in0=ssx[:, col:col + 1],
                                                                scalar1=1.0 / D, scalar2=NORM_EPS,
                                                                op0=ALU.mult, op1=ALU.add),
                     reads=[("ssx", col)], writes=[("sdx", col)])
                P.op("pool", lambda e, col=col: e.tensor_tensor(out=rsx[:, col:col + 1], in0=sdx[:, col:col + 1],
                                                                in1=neghalf[:, 0:1], op=ALU.pow),
                     reads=[("sdx", col), "neghalf"], writes=[("rsx", col)])
                P.op("dve", lambda e, xi=xi, col=col: e.tensor_scalar(out=hbf[:, :], in0=xt[xi][:, :],
                                                                      scalar1=rsx[:, col:col + 1], scalar2=None,
                                                                      op0=ALU.mult),
                     reads=[("xt", xi), ("rsx", col)], writes=["hbf"])
                for kc in range(8):
                    P.op("pe", lambda e, kc=kc: e.transpose(out=psM1b[:, kc * 128:(kc + 1) * 128],
                                                            in_=hbf[:, kc * 128:(kc + 1) * 128],
                                                            identity=identb[:, :]),
                         reads=["hbf", "identb"], writes=["psM1"])
                P.op("dve", lambda e, i=i: e.tensor_copy(
                    out=hT[:, :, i * 128:(i + 1) * 128],
                    in_=psM1b.rearrange("p (k t) -> p k t", t=128)),
                    reads=["psM1"], writes=[khT])

            if upto < 2:
                return
            P.tag = (tb, "B")
            P.section = "B"
            zmap = {}
            for ft in FT_ORDER:
                pz = psz_rot[0] % 2
                psz_rot[0] += 1
                zp = psP[:, 0:TB]
                for kc in range(8):
                    P.op("pe", lambda e, zp=zp, kc=kc, ft=ft: e.matmul(
                        zp, lhsT=winb[:, kc, ft * 128:(ft + 1) * 128], rhs=hT[:, kc, :],
                        start=(kc == 0), stop=(kc == 7)),
                        reads=[("winb", kc, q) for q in sorted({(ft * 128) // WST, (ft * 128 + 127) // WST})] + [khT],
                        writes=["psP"])
                if 8 <= ft <= 20:
                    zi = zrot[0] % NZS
                    zrot[0] += 1
                    zmap[ft] = zi
                    js = ft - 8
                    zz = zs[zi]
                    P.op("act", lambda e, zz=zz, zp=zp: e.activation(out=zz[:, :], in_=zp, func=AF.Copy),
                         reads=["psP"], writes=[("zs", zi)])
                    if upto < 2.2:
                        continue
                    P.op("act", lambda e, zz=zz, js=js: e.activation(out=dtmp[:, 1:TB], in_=zz[:, 0:TB - 1],
                                                                     func=AF.Copy, scale=muv[:, js:js + 1]),
                         reads=[("zs", zi), "muv"], writes=["dtmp"])
                    P.op("pool", lambda e, js=js: e.tensor_scalar(out=dtmp[:, 0:1], in0=halo[:, js:js + 1],
                                                                  scalar1=muv[:, js:js + 1], scalar2=0.0,
                                                                  op0=ALU.mult, op1=ALU.add),
                         reads=[("halo", js), "muv", "dtmp"], writes=["dtmp"])
                    P.op("pool", lambda e, zz=zz, js=js: e.tensor_copy(out=halo[:, js:js + 1], in_=zz[:, TB - 1:TB]),
                         reads=[("zs", zi)], writes=[("halo", js)])
                    P.op("dve", lambda e, zz=zz, js=js: e.scalar_tensor_tensor(
                        out=zz[:, :], in0=zz[:, :], scalar=omm[:, js:js + 1], in1=dtmp[:, :],
                        op0=ALU.mult, op1=ALU.add),
                        reads=[("zs", zi), "dtmp", "omm"], writes=[("zs", zi)])
                    if tb == 0 and debug and ("sh%d" % ft) in dbg_out:
                        dump("sh%d" % ft, zz[:, :], ("zs", zi))
                elif ft >= 21:
                    j = ft - 21
                    P.op("act", lambda e, zp=zp: e.activation(out=thg[:, :], in_=zp, func=AF.Tanh, scale=0.5),
                         reads=["psP"], writes=["thg"])
                    P.op("dve", lambda e, j=j, zp=zp: e.scalar_tensor_tensor(
                        out=sgb[:, j, :], in0=thg[:, :], scalar=1.0, in1=zp, op0=ALU.add, op1=ALU.mult),
                        reads=["psP", "thg"], writes=[("sgb", par, j)])
                elif ft >= 4:
                    g = ft - 4
                    P.op("act", lambda e, zp=zp: e.activation(out=thg[:, :], in_=zp, func=AF.Tanh, scale=0.5),
                         reads=["psP"], writes=["thg"])
                    P.op("dve", lambda e, g=g, zp=zp: e.scalar_tensor_tensor(
                        out=sga[:, g, :], in0=thg[:, :], scalar=1.0, in1=zp, op0=ALU.add, op1=ALU.mult),
                        reads=["psP", "thg"], writes=[("sga", g)])
                else:
                    g = ft
                    P.op("act", lambda e, g=g, zp=zp: e.activation(out=ua[g][:, 15:UW], in_=zp, func=AF.Copy),
                         reads=["psP"], writes=[("ua", g)])

                if upto < 2.4:
                    continue
                if ft == 20:
                    zz = zs[zmap[20]]
                    P.op("act", lambda e, zz=zz: e.activation(out=lobf[0:64, :], in_=zz[0:64, :], func=AF.Tanh),
                         reads=[("zs", zmap[20])], writes=["lobf"])
                    P.op("act", lambda e, zz=zz: e.activation(out=lobf[64:128, :], in_=zz[64:128, :], func=AF.Copy),
                         reads=[("zs", zmap[20])], writes=["lobf"])
                if 16 <= ft <= 19:
                    j = ft - 16
                    zk = zs[zmap[12 + j]]
                    zr = zs[zmap[8 + j]]
                    zv = zs[zmap[16 + j]]
                    kK, kR, kV = ("zs", zmap[12 + j]), ("zs", zmap[8 + j]), ("zs", zmap[16 + j])
                    pw, pwk = sm_half()
                    P.op("pe", lambda e, pw=pw, j=j: e.matmul(pw, lhsT=lup[0:64, j * 128:(j + 1) * 128],
                                                              rhs=lobf[0:64, :], start=True, stop=True),
                         reads=[("lup", 0), "lobf"], writes=[pwk])
                    P.op("act", lambda e, pw=pw, j=j: e.activation(out=t_sg[:, :], in_=pw, func=AF.Tanh,
                                                                   bias=w0v[:, j:j + 1], scale=0.5),
                         reads=[pwk, "w0v"], writes=["t_sg"])
                    P.op("act", lambda e: e.activation(out=t_sg[:, :], in_=t_sg[:, :], func=AF.Identity,
                                                       scale=0.5, bias=half[:, 0:1]),
                         reads=["t_sg", "half"], writes=["t_sg"])
                    pa, pak = sm_half()
                    P.op("pe", lambda e, pa=pa, j=j: e.matmul(pa, lhsT=lup[64:128, j * 128:(j + 1) * 128],
                                                              rhs=lobf[64:128, :], start=True, stop=True),
                         reads=[("lup", 1), "lobf"], writes=[pak])
                    P.op("act", lambda e, pa=pa, j=j: e.activation(out=t_a[:, :], in_=pa, func=AF.Tanh,
                                                                   bias=a0v[:, j:j + 1], scale=0.5),
                         reads=[pak, "a0v"], writes=["t_a"])
                    P.op("act", lambda e: e.activation(out=t_a[:, :], in_=t_a[:, :], func=AF.Identity,
                                                       scale=0.5, bias=half[:, 0:1]),
                         reads=["t_a", "half"], writes=["t_a"])
                    P.op("dve", lambda e: e.tensor_tensor_scan(out=t_csg[:, :], data0=cst[:, CO_RST:CO_RST + TB],
                                                               data1=t_sg[:, :], initial=0.0,
                                                               op0=ALU.mult, op1=ALU.add),
                         reads=["cst", "t_sg"], writes=["t_csg"])
                    P.op("pool", lambda e: e.tensor_tensor(out=t_e2[:, :], in0=t_csg[:, :], in1=t_sg[:, :],
                                                           op=ALU.subtract),
                         reads=["t_csg", "t_sg"], writes=["t_e2"])
                    P.op("act", lambda e: e.activation(out=t_e2[:, :], in_=t_e2[:, :], func=AF.Exp, scale=-C0),
                         reads=["t_e2"], writes=["t_e2"])
                    P.op("act", lambda e: e.activation(out=t_e1[:, :], in_=t_csg[:, :], func=AF.Exp, scale=-C0),
                         reads=["t_csg"], writes=["t_e1"])
                    P.op("act", lambda e: e.activation(out=t_e3[:, :], in_=t_csg[:, :], func=AF.Exp, scale=C0),
                         reads=["t_csg"], writes=["t_e3"])
                    e1v = t_e1[:, :].rearrange("p (c t) -> p c t", t=64)
                    P.op("pool", lambda e, j=j, e1v=e1v: e.tensor_copy(out=gam[:, j, :], in_=e1v[:, :, 63]),
                         reads=["t_e1"], writes=[("gam", par, j)])
                    P.op("pool", lambda e, j=j: e.tensor_tensor(
                        out=t_e4[:, :].rearrange("p (c t) -> p c t", t=64),
                        in0=t_e3[:, :].rearrange("p (c t) -> p c t", t=64),
                        in1=gam[:, j, :].unsqueeze(2).to_broadcast([128, NCH, 64]), op=ALU.mult),
                        reads=["t_e3", ("gam", par, j)], writes=["t_e4"])
                    P.op("act", lambda e, zk=zk, j=j: e.activation(out=kk2bf[:, :], in_=zk[:, :], func=AF.Square,
                                                                   scale=kkv[:, j:j + 1]),
                         reads=[kK, "kkv"], writes=["kk2bf"])
                    pss, pssk = sm_half()
                    P.op("pe", lambda e, pss=pss: e.matmul(pss, lhsT=bones[:, :], rhs=kk2bf[:, :], start=True,
                                                           stop=True),
                         reads=["bones", "kk2bf"], writes=[pssk])
                    P.op("dve", lambda e, pss=pss: e.tensor_scalar(out=t_rn[:, :], in0=pss, scalar1=1e-18,
                                                                   scalar2=None, op0=ALU.max),
                         reads=[pssk], writes=["t_rn"])
                    P.op("act", lambda e: e.activation(out=t_rn[:, :], in_=t_rn[:, :], func=AF.Ln),
                         reads=["t_rn"], writes=["t_rn"])
                    P.op("act", lambda e: e.activation(out=t_rn[:, :], in_=t_rn[:, :], func=AF.Exp, scale=-0.5),
                         reads=["t_rn"], writes=["t_rn"])
                    P.op("dve", lambda e, zk=zk, j=j: e.scalar_tensor_tensor(
                        out=t_kkn[:, :], in0=zk[:, :], scalar=kkv[:, j:j + 1], in1=t_rn[:, :],
                        op0=ALU.mult, op1=ALU.mult),
                        reads=[kK, "kkv", "t_rn"], writes=["t_kkn"])
                    P.op("act", lambda e, j=j: e.activation(out=t_kp[:, :], in_=t_a[:, :], func=AF.Identity,
                                                            scale=kav[:, j:j + 1], bias=omka[:, j:j + 1]),
                         reads=["t_a", "kav", "omka"], writes=["t_kp"])
                    P.op("pool", lambda e, zk=zk: e.tensor_tensor(out=t_kp[:, :], in0=t_kp[:, :], in1=zk[:, :],
                                                                  op=ALU.mult),
                         reads=["t_kp", kK], writes=["t_kp"])
                    P.op("pool", lambda e: e.tensor_tensor(out=t_b[:, :], in0=t_kkn[:, :], in1=t_a[:, :],
                                                           op=ALU.mult),
                         reads=["t_kkn", "t_a"], writes=["t_b"])

                    def v3(t):
                        return t[:, :].rearrange("p (c t) -> p c t", t=64)

                    prods = [
                        ("dve", PT[:, j, :, 0:64], t_kp, "t_e3", t_e3, "t_kp", ("PT", par, j)),
                        ("pool", PT[:, j, :, 64:128], t_b, "t_e3", t_e3, "t_b", ("PT", par, j)),
                        ("dve", PPsrc[:, :, 0:64], t_kp, "t_e4", t_e4, "t_kp", "PPsrc"),
                        ("pool", PPsrc[:, :, 64:128], t_b, "t_e4", t_e4, "t_b", "PPsrc"),
                    ]
                    for e2 in range(2):
                        rs = slice(64 * e2, 64 * e2 + 64)
                        P.op("dve", lambda e, rs=rs, e2=e2, j=j, zr=zr: e.tensor_tensor(
                            out=QT[rs, 2 * j + e2, :, 0:64], in0=v3(zr)[rs], in1=v3(t_e1)[rs], op=ALU.mult),
                            reads=[kR, "t_e1"], writes=[("QT", par, j)])
                        P.op("pool", lambda e, rs=rs, e2=e2, j=j: e.tensor_tensor(
                            out=QT[rs, 2 * j + e2, :, 64:128], in0=v3(t_kkn)[rs], in1=v3(t_e2)[rs], op=ALU.mult),
                            reads=["t_kkn", "t_e2"], writes=[("QT", par, j)])
                    for (en, o_ap, a_t, ek, e_t, akey, okey) in prods:
                        P.op(en, lambda e, o_ap=o_ap, a_t=a_t, e_t=e_t: e.tensor_tensor(
                            out=o_ap, in0=v3(a_t), in1=v3(e_t), op=ALU.mult),
                            reads=[akey, ek], writes=[okey])
                    P.op("dve", lambda e, zr=zr: e.tensor_tensor(out=rkbf[:, :], in0=zr[:, :], in1=t_kp[:, :],
                                                                 op=ALU.mult),
                         reads=[kR, "t_kp"], writes=["rkbf"])
                    P.op("act", lambda e, zv=zv: e.activation(out=vbf[:, :], in_=zv[:, :], func=AF.Copy),
                         reads=[kV], writes=["vbf"])
                    pbo, pbok = sm_half()
                    P.op("pe", lambda e, pbo=pbo, j=j: e.matmul(pbo, lhsT=blkrk4[:, j, :], rhs=rkbf[:, :],
                                                                start=True, stop=True),
                         reads=["blkrk4", "rkbf"], writes=[pbok])
                    P.op("dve", lambda e, pbo=pbo, zv=zv, j=j: e.tensor_tensor(out=bon[:, j, :], in0=pbo,
                                                                               in1=zv[:, :], op=ALU.mult),
                         reads=[pbok, kV], writes=[("bon", par, j)])
                    for c in range(NCH):
                        P.op("pe", lambda e, c=c: e.transpose(out=psTb[:, c * 128:(c + 1) * 128],
                                                              in_=PPsrc[:, c, :], identity=identb[:, :]),
                             reads=["PPsrc", "identb"], writes=["psT"])
                    for c in range(NCH):
                        P.op("pe", lambda e, c=c: e.transpose(out=psTb[0:64, 512 + c * 128:512 + (c + 1) * 128],
                                                              in_=vbf[:, c * 64:(c + 1) * 64],
                                                              identity=identb[:, :]),
                             reads=["vbf", "identb"], writes=["psT"])
                    P.op("act", lambda e, j=j: e.activation(
                        out=PPt[:, :, j, :], in_=psTb[:, 0:512].rearrange("p (c f) -> p c f", f=128),
                        func=AF.Copy),
                        reads=["psT"], writes=[("PPt", par, j)])
                    P.op("act", lambda e, j=j: e.activation(
                        out=Zt[0:64, :, j, :], in_=psTb[0:64, 512:1024].rearrange("p (c f) -> p c f", f=128),
                        func=AF.Copy),
                        reads=["psT"], writes=[("ZtV", par, j)])
                P.in_region = False
                if upto < 2.6:
                    continue
                if 4 <= ft <= 7:
                    g = ft - 4
                    w = 2 << g
                    u = ua[g]
                    src = u
                    bufs = [pA, pB]
                    for lvl in range(g + 1):
                        sh = 1 << lvl
                        dst = bufs[lvl % 2]
                        lo = 2 * sh - 1
                        P.op("pool", lambda e, src=src, dst=dst, sh=sh, lo=lo: e.tensor_tensor(
                            out=dst[:, lo:UW], in0=src[:, lo:UW], in1=src[:, lo - sh:UW - sh], op=ALU.add),
                            reads=[("ua", g), "pA", "pB"], writes=["pA" if dst is pA else "pB"])
                        src = dst
                    skey = "pA" if src is pA else "pB"
                    P.op("dve", lambda e, src=src, u=u, w=w: e.scalar_tensor_tensor(
                        out=pooled[:, :], in0=src[:, 15:UW], scalar=1.0 / w, in1=u[:, 15:UW],
                        op0=ALU.mult, op1=ALU.subtract),
                        reads=[skey, ("ua", g)], writes=["pooled"])
                    if tb == 0:
                        nfix = w - 1
                        P.op("dve", lambda e, src=src, g=g, nfix=nfix: e.tensor_tensor(
                            out=t1[:, 0:nfix], in0=src[:, 15:15 + nfix],
                            in1=cst[:, CO_ICNT + g * 15:CO_ICNT + g * 15 + nfix], op=ALU.mult),
                            reads=[skey, "cst"], writes=["t1"])
                        P.op("dve", lambda e, u=u, nfix=nfix: e.tensor_tensor(
                            out=pooled[:, 0:nfix], in0=t1[:, 0:nfix], in1=u[:, 15:15 + nfix], op=ALU.subtract),
                            reads=["t1", ("ua", g), "pooled"], writes=["pooled"])
                    P.op("pool", lambda e, u=u: e.tensor_copy(out=u[:, 0:15], in_=u[:, TB:TB + 15]),
                         reads=[("ua", g)], writes=[("ua", g)])
                    pm, pmk = sm_half()
                    P.op("pe", lambda e, pm=pm, g=g: e.matmul(pm, lhsT=poolw[:, g, :], rhs=pooled[:, :],
                                                              start=True, stop=True),
                         reads=["poolw", "pooled"], writes=[pmk])
                    P.op("dve", lambda e, pm=pm, g=g: e.scalar_tensor_tensor(
                        out=ycat[:, g, :], in0=pm, scalar=pscv[:, g:g + 1], in1=sga[:, g, :],
                        op0=ALU.mult, op1=ALU.mult),
                        reads=[pmk, "pscv", ("sga", g)], writes=[("ycat", g)])

            if tb == 0:
                dump("QT", None, None)

            if upto < 3:
                return
            P.in_region = True
            P.region_count = 0
            P.tag = (tb, "D")
            P.section = "D"
            mmt4 = cst[:, CO_MMT:CO_MMT + 128].unsqueeze(1).to_broadcast([128, 4, 128])
            ml2 = cst[:, CO_ML:CO_ML + 128].rearrange("p (m t) -> p m t", t=64).unsqueeze(2).to_broadcast(
                [128, 2, 4, 64])
            irep4 = cst[:, CO_IREP:CO_IREP + 64].unsqueeze(1).to_broadcast([128, 4, 64])
            for c in range(NCH):
                cg = tb * NCH + c
                sb_old = Sbf[cg % 2]
                sb_new = Sbf[(cg + 1) % 2]
                kold = "Sbf%d" % (cg % 2)
                knew = "Sbf%d" % ((cg + 1) % 2)
                qk = [("QT", par, j) for j in range(4)]
                pk = [("PT", par, j) for j in range(4)]
                P.sub = ("seq", c)
                for rnd in range(2):
                    for h in range(4 * rnd, 4 * rnd + 4):
                        j = h // 2
                        P.op("pe", lambda e, h=h, j=j, c=c: e.matmul(
                            psM1[:, (h % 4) * 128:(h % 4 + 1) * 128], lhsT=PT[:, j, c, :], rhs=QT[:, h, c, :],
                            start=True, stop=True),
                            reads=[("QT", par, j), ("PT", par, j)], writes=["psM1"])
                    P.op("dve", lambda e, rnd=rnd: e.tensor_tensor(
                        out=MTs[:, 4 * rnd:4 * rnd + 4, :], in0=psM1[:, :].rearrange("p (h t) -> p h t", t=128),
                        in1=mmt4, op=ALU.mult),
                        reads=["psM1", "cst"], writes=["MTs"])
                HORD = (0, 4, 1, 5, 2, 6, 3, 7)
                pbase = 2 * (cg % 2)
                P.sub = ("head", c)

                def hq(h):
                    return slice(64 * (h // 4), 64 * (h // 4) + 64), h % 4

                for h in HORD:
                    j = h // 2
                    ps_, hh = hq(h)
                    P.op("pe", lambda e, h=h, j=j, c=c, ps_=ps_, hh=hh: e.matmul(
                        psG[ps_, hh * 64:(hh + 1) * 64], lhsT=QT[:, h, c, 64:128],
                        rhs=PT[:, j, c, 64:128], start=True, stop=True),
                        reads=[("QT", par, j), ("PT", par, j)], writes=["psG"])
                    P.op("pe", lambda e, h=h, j=j, c=c, ps_=ps_, hh=hh: e.matmul(
                        psG[ps_, 256 + hh * 64:256 + (hh + 1) * 64], lhsT=PT[:, j, c, 64:128],
                        rhs=QT[:, h, c, 64:128], start=True, stop=True),
                        reads=[("QT", par, j), ("PT", par, j)], writes=["psG"])
                P.op("dve", lambda e: e.tensor_tensor(
                    out=MM[0][:, :, :, :], in0=psG[:, :].rearrange("p (m h t) -> p m h t", m=2, t=64),
                    in1=ml2, op=ALU.mult),
                    reads=["psG", "cst"], writes=["MM0"])
                P.op("pool", lambda e, pbase=pbase: e.tensor_tensor(
                    out=PTch[pbase][:, :, :], in0=irep4, in1=MM[0][:, 1, :, :], op=ALU.subtract),
                    reads=["cst", "MM0"], writes=["PTch%d" % pbase])
                M_cur, P_cur = 0, 0
                P.sub = ("lvl", c)
                for lvl in range(1, 6):
                    mi = MM[M_cur]
                    mo = MM[1 - M_cur]
                    kmi, kmo = "MM%d" % M_cur, "MM%d" % (1 - M_cur)
                    for h in HORD:
                        ps_, hh = hq(h)
                        P.op("pe", lambda e, ps_=ps_, hh=hh, mi=mi: e.matmul(
                            psNM[0][ps_, hh * 64:(hh + 1) * 64], lhsT=mi[ps_, 1, hh, :], rhs=mi[ps_, 0, hh, :],
                            start=True, stop=True),
                            reads=[kmi], writes=[("psNM", 0)])
                    if lvl < 5:
                        for h in HORD:
                            ps_, hh = hq(h)
                            P.op("pe", lambda e, ps_=ps_, hh=hh, mi=mi: e.matmul(
                                psNM[0][ps_, 256 + hh * 64:256 + (hh + 1) * 64], lhsT=mi[ps_, 0, hh, :],
                                rhs=mi[ps_, 1, hh, :], start=True, stop=True),
                                reads=[kmi], writes=[("psNM", 0)])
                        P.op("act", lambda e, mo=mo: e.activation(
                            out=mo[:, :, :, :].rearrange("p m h t -> p (m h t)"), in_=psNM[0][:, :], func=AF.Copy),
                            reads=[("psNM", 0)], writes=[kmo])
                    else:
                        P.op("act", lambda e, mo=mo: e.activation(
                            out=mo[:, 0, :, :].rearrange("p h t -> p (h t)"), in_=psNM[0][:, 0:256], func=AF.Copy),
                            reads=[("psNM", 0)], writes=[kmo])
                    M_cur = 1 - M_cur
                    p_in = PTch[pbase + P_cur]
                    p_out = PTch[pbase + 1 - P_cur]
                    kp_in, kp_out = "PTch%d" % (pbase + P_cur), "PTch%d" % (pbase + 1 - P_cur)
                    for h in HORD:
                        ps_, hh = hq(h)
                        P.op("pe", lambda e, ps_=ps_, hh=hh, mo=mo, p_in=p_in: e.matmul(
                            psSm[ps_, hh * 64:(hh + 1) * 64], lhsT=mo[ps_, 0, hh, :], rhs=p_in[ps_, hh, :],
                            start=True, stop=True),
                            reads=[kmo, kp_in], writes=["psSm"])
                    P.op("dve", lambda e, p_in=p_in, p_out=p_out: e.tensor_tensor(
                        out=p_out[:, :, :].rearrange("p h t -> p (h t)"), in0=psSm[:, 0:256],
                        in1=p_in[:, :, :].rearrange("p h t -> p (h t)"), op=ALU.add),
                        reads=["psSm", kp_in], writes=[kp_out])
                    P_cur = 1 - P_cur
                pfin = PTch[pbase + P_cur]
                kpfin = "PTch%d" % (pbase + P_cur)
                P.sub = ("seq", c)
                P.op("pool", lambda e, c=c: e.tensor_tensor(
                    out=Sdec[:, :, :], in0=Sst[:, :, :],
                    in1=gam[:, :, c].unsqueeze(2).to_broadcast([128, 4, 64]), op=ALU.mult),
                    reads=["Sst"] + [("gam", par, j) for j in range(4)], writes=["Sdec"])
                for h in HORD:
                    j, e2 = h // 2, h % 2
                    ps_, hh = hq(h)
                    P.op("pe", lambda e, h=h, j=j, e2=e2, c=c, ps_=ps_, hh=hh: e.matmul(
                        psNM[1][ps_, hh * 64:(hh + 1) * 64], lhsT=MTs[0:64, h, 64:128],
                        rhs=Zt[0:64, c, j, e2 * 64:(e2 + 1) * 64], start=True, stop=True),
                        reads=["MTs", ("ZtV", par, j)], writes=[("psNM", 1)])
                P.op("act", lambda e: e.activation(out=Xs[:, :, :].rearrange("p h t -> p (h t)"),
                                                   in_=psNM[1][:, 0:256], func=AF.Copy),
                     reads=[("psNM", 1)], writes=["Xs"])
                for h in HORD:
                    j = h // 2
                    ps_, hh = hq(h)
                    P.op("pe", lambda e, h=h, j=j, c=c, sb_old=sb_old, ps_=ps_, hh=hh: e.matmul(
                        psG[ps_, hh * 64:(hh + 1) * 64], lhsT=QT[:, h, c, 64:128],
                        rhs=sb_old[:, j, :], start=True, stop=True),
                        reads=[("QT", par, j), kold], writes=["psG"])
                P.op("dve", lambda e: e.scalar_tensor_tensor(
                    out=GXs[:, :, :].rearrange("p h t -> p (h t)"), in0=psG[:, 0:256], scalar=-1.0,
                    in1=Xs[:, :, :].rearrange("p h t -> p (h t)"), op0=ALU.mult, op1=ALU.subtract),
                    reads=["psG", "Xs"], writes=["GXs"])
                for h in HORD:
                    ps_, hh = hq(h)
                    ub = psSm if h < 4 else psNM[1]
                    ubk = "psSm" if h < 4 else ("psNM", 1)
                    P.op("pe", lambda e, ps_=ps_, hh=hh, pfin=pfin, ub=ub: e.matmul(
                        ub[64:128, hh * 64:(hh + 1) * 64], lhsT=pfin[ps_, hh, :], rhs=GXs[ps_, hh, :],
                        start=True, stop=True),
                        reads=[kpfin, "GXs"], writes=[ubk])
                P.op("dve", lambda e, c=c: e.tensor_copy(
                    out=Zt[64:128, c, 0:2, :].rearrange("p j f -> p (j f)"), in_=psSm[64:128, 0:256]),
                    reads=["psSm"], writes=[("ZtU", par, c)])
                P.op("act", lambda e, c=c: e.activation(
                    out=Zt[64:128, c, 2:4, :].rearrange("p j f -> p (j f)"), in_=psNM[1][64:128, 0:256],
                    func=AF.Copy),
                    reads=[("psNM", 1)], writes=[("ZtU2", par, c)])
                zkeys = [("ZtV", par, j) for j in range(4)] + [("ZtU", par, c), ("ZtU2", par, c)]
                ypb = 64 * (c % 2)
                for h in range(8):
                    j, pb, e2 = h // 2, 64 * (h % 2), h % 2
                    P.op("pe", lambda e, h=h, j=j, pb=pb, c=c, ypb=ypb, sb_old=sb_old: e.matmul(
                        psY[ypb:ypb + 64, h * 64:(h + 1) * 64], lhsT=QT[:, h, c, 0:64],
                        rhs=sb_old[:, j, :], start=True, stop=False),
                        reads=[("QT", par, j), kold], writes=["psY"])
                    P.op("pe", lambda e, h=h, j=j, e2=e2, c=c, ypb=ypb: e.matmul(
                        psY[ypb:ypb + 64, h * 64:(h + 1) * 64], lhsT=MTs[:, h, 0:64],
                        rhs=Zt[:, c, j, e2 * 64:(e2 + 1) * 64], start=False, stop=True),
                        reads=["MTs"] + zkeys, writes=["psY"])
                for h in range(8):
                    j, pb, e2 = h // 2, 64 * (h % 2), h % 2
                    P.op("pe", lambda e, h=h, j=j, pb=pb, e2=e2, c=c: e.matmul(
                        psSm[pb:pb + 64, j * 64:(j + 1) * 64], lhsT=PPt[:, c, j, e2 * 64:(e2 + 1) * 64],
                        rhs=Zt[:, c, j, e2 * 64:(e2 + 1) * 64], start=True, stop=True),
                        reads=[("PPt", par, j)] + zkeys, writes=["psSm"])
                P.op("dve", lambda e: e.tensor_tensor(out=Sst[:, :, :].rearrange("p j v -> p (j v)"),
                                                      in0=psSm[:, 0:256],
                                                      in1=Sdec[:, :, :].rearrange("p j v -> p (j v)"), op=ALU.add),
                     reads=["psSm", "Sdec"], writes=["Sst"])
                P.op("act", lambda e, sb_new=sb_new: e.activation(out=sb_new[:, :, :], in_=Sst[:, :, :],
                                                                  func=AF.Copy),
                     reads=["Sst"], writes=[knew])
                if c % 2 == 1:
                    ti = c // 2
                    P.op("act", lambda e: e.activation(out=ycp[:, :], in_=psY[:, :], func=AF.Copy),
                         reads=["psY"], writes=["ycp"])
                    P.op("act", lambda e: e.activation(out=ysq[:, :], in_=psY[:, :], func=AF.Square),
                         reads=["psY"], writes=["ysq"])
                    P.op("dve", lambda e: e.tensor_reduce(out=st1[:, :],
                                                          in_=ycp[:, :].rearrange("p (h v) -> p h v", v=64),
                                                          axis=AX.X, op=ALU.add),
                         reads=["ycp"], writes=["st1"])
                    P.op("dve", lambda e: e.tensor_reduce(out=st2[:, :],
                                                          in_=ysq[:, :].rearrange("p (h v) -> p h v", v=64),
                                                          axis=AX.X, op=ALU.add),
                         reads=["ysq"], writes=["st2"])
                    P.op("dve", lambda e: e.tensor_scalar(out=stm[:, :], in0=st1[:, :], scalar1=1.0 / 64,
                                                          scalar2=None, op0=ALU.mult),
                         reads=["st1"], writes=["stm"])
                    P.op("dve", lambda e: e.tensor_tensor(out=stv[:, :], in0=stm[:, :], in1=stm[:, :], op=ALU.mult),
                         reads=["stm"], writes=["stv"])
                    P.op("dve", lambda e: e.scalar_tensor_tensor(out=stv[:, :], in0=st2[:, :], scalar=1.0 / 64,
                                                                 in1=stv[:, :], op0=ALU.mult, op1=ALU.subtract),
                         reads=["st2", "stv"], writes=["stv"])
                    P.op("pool", lambda e: e.tensor_scalar(out=stv[:, :], in0=stv[:, :], scalar1=1.0, scalar2=GN_EPS,
                                                           op0=ALU.mult, op1=ALU.add),
                         reads=["stv"], writes=["stv"])
                    P.op("pool", lambda e: e.tensor_tensor(out=stv[:, :], in0=stv[:, :], in1=neghalf[:, 0:8],
                                                           op=ALU.pow),
                         reads=["stv", "neghalf"], writes=["stv"])
                    P.op("pool", lambda e: e.tensor_tensor(
                        out=ycp[:, :].rearrange("p (h v) -> p h v", v=64),
                        in0=ycp[:, :].rearrange("p (h v) -> p h v", v=64),
                        in1=stm[:, :].unsqueeze(2).to_broadcast([128, 8, 64]), op=ALU.subtract),
                        reads=["ycp", "stm"], writes=["ycp"])
                    P.op("pool", lambda e, ti=ti: e.tensor_tensor(
                        out=ynb[ti][:, :].rearrange("p (h v) -> p h v", v=64),
                        in0=ycp[:, :].rearrange("p (h v) -> p h v", v=64),
                        in1=stv[:, :].unsqueeze(2).to_broadcast([128, 8, 64]), op=ALU.mult),
                        reads=["ycp", "stv"], writes=[("ynb", ti)])

            P.in_region = False
            if upto < 4:
                return
            P.tag = (tb, "E")
            P.section = "E"
            P.sub = None
            for ti in range(NTT):
                for j in range(4):
                    P.op("pe", lambda e, ti=ti, j=j: e.transpose(
                        out=psYb[:, j * TB + ti * 128:j * TB + (ti + 1) * 128],
                        in_=ynb[ti][:, j * 128:(j + 1) * 128], identity=identb[:, :]),
                        reads=[("ynb", ti), "identb"], writes=["psY"])
            for j in range(4):
                P.op("dve", lambda e, j=j: e.tensor_scalar(out=t1[:, :], in0=psYb[:, j * TB:(j + 1) * TB],
                                                           scalar1=gng[:, j:j + 1], scalar2=gnb[:, j:j + 1],
                                                           op0=ALU.mult, op1=ALU.add),
                     reads=["psY", "gng", "gnb"], writes=["t1"])
                P.op("pool", lambda e, j=j: e.tensor_tensor(out=t1[:, :], in0=t1[:, :], in1=bon[:, j, :],
                                                            op=ALU.add),
                     reads=["t1", ("bon", par, j)], writes=["t1"])
                P.op("pool", lambda e, j=j: e.tensor_tensor(out=ycat[:, 4 + j, :], in0=t1[:, :], in1=sgb[:, j, :],
                                                            op=ALU.mult),
                     reads=["t1", ("sgb", par, j)], writes=[("ycat", 4 + j)])

            ykeys = [("ycat", k) for k in range(8)]
            pso = [psM1, psG]
            psok = ["psM1", "psG"]
            for i in range(NTT):
                col = S // 128 + tb * NTT + i
                r0 = t0 + i * 128
                xi = erot[0] % 2
                erot[0] += 1
                xb = xe[xi]
                dma("sync", xb[:, :], x[r0:r0 + 128, :], f"de{xi}", writes=[("xe", xi)])
                for hf in range(2):
                    for kc in range(8):
                        P.op("pe", lambda e, hf=hf, kc=kc, i=i: e.matmul(
                            pso[hf][:, :], lhsT=ycat[:, kc, i * 128:(i + 1) * 128],
                            rhs=woutb[:, kc, hf * 512:(hf + 1) * 512], start=(kc == 0), stop=(kc == 7)),
                            reads=ykeys + [("woutb", kc)], writes=[psok[hf]])
                    P.op("dve", lambda e, hf=hf, xb=xb: e.tensor_tensor(
                        out=xb[:, hf * 512:(hf + 1) * 512], in0=pso[hf][:, :],
                        in1=xb[:, hf * 512:(hf + 1) * 512], op=ALU.add),
                        reads=[psok[hf], ("xe", xi)], writes=[("xe", xi)])
                P.op("act", lambda e, col=col, xb=xb: e.activation(out=junk[:, :], in_=xb[:, :], func=AF.Square,
                                                                   accum_out=ssx[:, col:col + 1]),
                     reads=[("xe", xi)], writes=["junk", ("ssx", col)])
                P.op("pool", lambda e, col=col: e.tensor_scalar(out=sdx[:, col:col + 1], in0=ssx[:, col:col + 1],
                                                                scalar1=1.0 / D, scalar2=NORM_EPS,
                                                                op0=ALU.mult, op1=ALU.add),
                     reads=[("ssx", col)], writes=[("sdx", col)])
                P.op("pool", lambda e, col=col: e.tensor_tensor(out=rsx[:, col:col + 1], in0=sdx[:, col:col + 1],
                                                                in1=neghalf[:, 0:1], op=ALU.pow),
                     reads=[("sdx", col), "neghalf"], writes=[("rsx", col)])
                P.op("dve", lambda e, col=col, xb=xb: e.scalar_tensor_tensor(
                    out=xb[:, :], in0=xb[:, :], scalar=rsx[:, col:col + 1], in1=fgain[:, :],
                    op0=ALU.mult, op1=ALU.mult),
                    reads=[("xe", xi), ("rsx", col), "fgain"], writes=[("xe", xi)])
                dma("sync", out[r0:r0 + 128, :], xb[:, :], f"do{xi}", reads=[("xe", xi)], writes=[("outd", xi)])

        nbl = nblocks if upto >= 1 else 0
        if nbl:
            P.only = {"A"}
            block_body(0)
        for tb in range(nbl):
            P.only = {"B"}
            block_body(tb)
            if tb + 1 < nbl:
                P.only = {"A"}
                block_body(tb + 1)
            P.only = {"D"}
            for sub in [("head", 0), ("lvl", 0)]:
                P.subonly = sub
                block_body(tb)
            for c_ in range(NCH):
                order = ([("head", c_ + 1)] if c_ + 1 < NCH else []) + [("seq", c_)] + \
                    ([("lvl", c_ + 1)] if c_ + 1 < NCH else [])
                for sub in order:
                    P.subonly = sub
                    block_body(tb)
            P.subonly = None
            P.sub = None
            P.only = {"E"}
            block_body(tb)
        P.only = None
        P.section = None
        P.tag = None

        for name, ap, key in dbg_dumps:
            if ap is None:
                continue
            dma("sync", dbg_out[name], ap, "ddbg", reads=[key])

        _INFO["sbuf_left"] = nc.sbuf_bytes_remaining
        if do_schedule:
            P.schedule()
            cands = [(P.est_makespan, {e: list(v) for e, v in P.ops.items()})]
            if SCHED_TRIALS > 0:
                import random
                for t in range(SCHED_TRIALS):
                    P.prio_rng = random.Random(1234 + t)
                    P.prio_noise = SCHED_NOISE
                    P.schedule()
                    cands.append((P.est_makespan, {e: list(v) for e, v in P.ops.items()}))
                P.prio_rng = None
            cands.sort(key=lambda c: c[0])
            P.est_makespan, P.ops = cands[min(SCHED_PICK, len(cands) - 1)]
            _INFO["est_us"] = P.est_makespan
        P.finalize()
        keys = P.sem_keys()
        sems = {k: es.enter_context(nc.semaphore("s_" + k)) for k in keys}
        out_keys = [k for k in keys if k.startswith("do") or k == "ddbg"]
        with nc.Block() as block:
            @block.sync
            def _(eng):
                P.run_engine("sync", eng, sems, final_waits=out_keys)

            @block.scalar
            def _(eng):
                P.run_engine("act", eng, sems)

            @block.vector
            def _(eng):
                P.run_engine("dve", eng, sems)

            @block.gpsimd
            def _(eng):
                P.run_engine("pool", eng, sems)

            @block.tensor
            def _(eng):
                P.run_engine("pe", eng, sems)
    return nc


_CACHE = {}
_INFO = {}


def _prep_inputs(inputs):
    f = lambda a: np.ascontiguousarray(np.asarray(a, dtype=np.float32))
    shared = {
        "norm_gain": f(inputs["norm_gain"]).reshape(D),
        "w_in": f(inputs["w_in"]).reshape(D, DIN),
        "pool_w": f(inputs["pool_w"]).reshape(4, 128, 128),
        "pool_scale": f(inputs["pool_scale"]).reshape(512),
        "shift_mu": f(inputs["shift_mu"]).reshape(1664),
        "w0": f(inputs["w0"]).reshape(512),
        "w_up": f(inputs["w_up"]).reshape(64, 512),
        "a0": f(inputs["a0"]).reshape(512),
        "a_up": f(inputs["a_up"]).reshape(64, 512),
        "k_k": f(inputs["k_k"]).reshape(512),
        "k_a": f(inputs["k_a"]).reshape(512),
        "r_k": f(inputs["r_k"]).reshape(512),
        "gn_gain": f(inputs["gn_gain"]).reshape(512),
        "gn_bias": f(inputs["gn_bias"]).reshape(512),
        "w_out": f(inputs["w_out"]).reshape(D, D),
        "final_gain": f(inputs["final_gain"]).reshape(D),
        "cst": _make_consts(),
    }
    xs = f(inputs["x"])
    return [dict(shared, x=xs[b]) for b in range(xs.shape[0])]


def kernel(**inputs):
    in_maps = _prep_inputs(inputs)
    if "nc" not in _CACHE:
        _CACHE["nc"] = build()
    nc = _CACHE["nc"]
    res = run_bass_kernel_spmd(nc, in_maps, core_ids=list(range(8)))
    return np.stack([np.asarray(r["out"], dtype=np.float32) for r in res.results], axis=0)
```

```python
import contextlib
import numpy as np
import concourse.bass as bass
import concourse.mybir as mybir
from concourse.bass_utils import run_bass_kernel_spmd

F32 = mybir.dt.float32
BF16 = mybir.dt.bfloat16
AF = mybir.ActivationFunctionType
ALU = mybir.AluOpType
AX = mybir.AxisListType

S = 2048
D = 1024
DIN = 3200
TB = 256
NB = S // TB
NCH = TB // 64
NTT = TB // 128
C0 = float(np.exp(-0.5))
NORM_EPS = 1e-6
GN_EPS = 64e-5
WST = 800

CO_ID = 0
CO_BONES = 128
CO_MMT = 256
CO_ML = CO_MMT + 128
CO_IREP = CO_ML + 128
CO_RST = CO_IREP + 64
CO_ICNT = CO_RST + TB
CST_W = CO_ICNT + 60


def _make_consts():
    c = np.zeros((128, CST_W), np.float32)
    p = np.arange(128)
    c[:, CO_ID:CO_ID + 128] = np.eye(128, dtype=np.float32)
    c[:, CO_BONES:CO_BONES + 128] = (p[:, None] // 64 == p[None, :] // 64).astype(np.float32)
    s = (p % 64)[:, None]
    t = (p % 64)[None, :]
    colr = (p[None, :] < 64)
    m = np.where(colr, (s <= t), (s < t)).astype(np.float32)
    c[:, CO_MMT:CO_MMT + 128] = m
    tt = (p % 64)[:, None]
    ss = np.arange(64)[None, :]
    c[:, CO_ML:CO_ML + 64] = (ss < tt).astype(np.float32)
    c[:, CO_ML + 64:CO_ML + 128] = (tt < ss).astype(np.float32)
    c[:, CO_IREP:CO_IREP + 64] = (ss == tt).astype(np.float32)
    rst = np.ones(TB, np.float32)
    rst[::64] = 0.0
    c[:, CO_RST:CO_RST + TB] = rst[None, :]
    for g, w in enumerate((2, 4, 8, 16)):
        for tq in range(15):
            c[:, CO_ICNT + g * 15 + tq] = 1.0 / min(tq + 1, w)
    return c


_INFO = {}
SCHED_TRIALS = 40
SCHED_PICK = 0
SCHED_NOISE = 0.01


class _Op:
    __slots__ = ("eng", "fn", "deps", "dma", "semval", "signal", "group", "dur", "lat", "idx", "prio", "succ",
                 "npred", "ready_t", "fin", "tag", "st", "pos", "wdeps", "mode")


class _Rec:
    def __getattr__(self, name):
        def f(*a, **k):
            return (name, a, k)
        return f


def _free_size(ap):
    n = 1
    for d in list(ap.shape)[1:]:
        n *= int(d)
    return n


def _estimate(eng, fn, dma):
    name, a, k = fn(_Rec())
    out = k.get("out", a[0] if a else None)
    if dma is not None:
        nbytes = _free_size(out) * int(out.shape[0]) * 4
        return 0.06, 2.0 + nbytes / 150e3, None
    if eng == "pe":
        def r32(v):
            return 32 if v <= 32 else (64 if v <= 64 else 128)
        if name == "transpose":
            n = int(k["in_"].shape[0])
            mode = ("T", r32(int(k["in_"].shape[0])), r32(_free_size(k["in_"])))
        else:
            n = _free_size(k["rhs"])
            mode = ("M", r32(int(k["lhsT"].shape[0])), r32(_free_size(k["lhsT"])))
        d = (0.025 + 0.0006 * n) if n >= 256 else (0.03 + 0.0002 * n)
        return d, d + 0.12, mode
    n = _free_size(out)
    if eng == "act":
        d = 0.22 + 0.00075 * n
    elif eng == "dve":
        d = 0.12 + 0.00105 * n
    else:
        d = 0.2 + 0.0021 * n
    return d, d + 0.08, None


class Plan:
    ENGS = ("sync", "act", "dve", "pool", "pe")

    def __init__(self):
        self.ops = {e: [] for e in self.ENGS}
        self.lastw = {}
        self.readers = {}
        self.dma_eng = {}
        self.nops = 0

    def op(self, eng, fn, reads=(), writes=(), dma=None, group=None):
        only = getattr(self, "only", None)
        if only is not None and getattr(self, "section", None) not in only:
            return None
        if getattr(self, "in_region", False):
            if self.region_count >= self.region_budget:
                return None
            self.region_count += 1
        o = _Op()
        o.eng = eng
        o.fn = fn
        o.dma = dma
        o.deps = {}
        o.signal = False
        o.semval = None
        o.group = group
        o.dur, o.lat, o.mode = _estimate(eng, fn, dma)
        if dma is not None:
            assert self.dma_eng.setdefault(dma, eng) == eng
        for b in reads:
            w = self.lastw.get(b)
            if w is not None:
                o.deps[w] = True
        for b in writes:
            w = self.lastw.get(b)
            if w is not None and w not in o.deps:
                o.deps[w] = False
            for r in self.readers.get(b, ()):
                if r not in o.deps:
                    o.deps[r] = False
        for b in writes:
            self.lastw[b] = o
            self.readers[b] = []
        for b in reads:
            if b not in writes:
                self.readers.setdefault(b, []).append(o)
        o.idx = self.nops
        o.tag = getattr(self, "tag", None)
        self.nops += 1
        self.ops[eng].append(o)
        return o

    @staticmethod
    def _needs_wait(o, d, raw):
        if d.dma is not None or o.dma is not None:
            return True
        if d.eng != o.eng:
            return True
        if o.eng == "pe":
            return False
        return True

    def schedule(self):
        allops = []
        for e in self.ENGS:
            allops.extend(self.ops[e])
        for o in allops:
            o.succ = []
            o.npred = len(o.deps)
        for o in allops:
            for d in o.deps:
                d.succ.append(o)
        allops.sort(key=lambda o: o.idx)
        for o in reversed(allops):
            m = 0.0
            for sc in o.succ:
                if sc.prio > m:
                    m = sc.prio
            o.prio = o.lat + m
        rnd = getattr(self, "prio_rng", None)
        if rnd is not None:
            for o in allops:
                o.prio *= 1.0 + self.prio_noise * (rnd.random() - 0.5)
        free = {e: 0.0 for e in self.ENGS}
        ready = {e: [] for e in self.ENGS}
        for o in allops:
            o.ready_t = 0.0
            if o.npred == 0:
                ready[o.eng].append(o)
        order = {e: [] for e in self.ENGS}
        remaining = len(allops)
        SYNC = 0.35
        MODE_SW = 0.30
        pe_mode = None
        while remaining:
            best = None
            for e in self.ENGS:
                lst = ready[e]
                if not lst:
                    continue
                fe = free[e]
                cand = None
                cs = None
                for o in lst:
                    st = o.ready_t if o.ready_t > fe else fe
                    if e == "pe" and o.mode != pe_mode:
                        st += MODE_SW
                    key = (st, -o.prio, o.idx)
                    if cs is None or key < cs:
                        cs = key
                        cand = o
                if best is None or cs < best[0]:
                    best = (cs, cand)
            cs, o = best
            st = cs[0]
            if o.eng == "pe":
                pe_mode = o.mode
            ready[o.eng].remove(o)
            free[o.eng] = st + o.dur
            o.st = st
            o.fin = st + o.lat
            order[o.eng].append(o)
            remaining -= 1
            for sc in o.succ:
                if sc.eng == o.eng == "pe":
                    t = st + o.dur
                else:
                    t = o.fin + (SYNC if sc.eng != o.eng else 0.05)
                if t > sc.ready_t:
                    sc.ready_t = t
                sc.npred -= 1
                if sc.npred == 0:
                    ready[sc.eng].append(sc)
        self.ops = order
        self.est_makespan = max(free.values())

    def finalize(self):
        for e in self.ENGS:
            for i, o in enumerate(self.ops[e]):
                o.pos = i
        for e in self.ENGS:
            for o in self.ops[e]:
                last = {}
                o.wdeps = []
                for d, raw in o.deps.items():
                    if not self._needs_wait(o, d, raw):
                        continue
                    if d.dma is not None:
                        o.wdeps.append(d)
                        continue
                    cur = last.get(d.eng)
                    if cur is None or d.pos > cur.pos:
                        last[d.eng] = d
                for d in last.values():
                    d.signal = True
                    o.wdeps.append(d)
        self.dma_tot = {}
        groups = {}
        for e in self.ENGS:
            c = 0
            for o in self.ops[e]:
                if o.dma is not None:
                    v = self.dma_tot.get(o.dma, 0) + 16
                    self.dma_tot[o.dma] = v
                    o.semval = v
                    if o.group is not None:
                        groups.setdefault((o.dma, o.group), []).append(o)
                elif o.signal:
                    c += 1
                    o.semval = c
        for lst in groups.values():
            mx = max(x.semval for x in lst)
            for x in lst:
                x.semval = mx

    def sem_keys(self):
        return list(self.ENGS) + sorted(self.dma_tot.keys())

    def run_engine(self, e, eng, sems, final_waits=()):
        known = {}
        for o in self.ops[e]:
            waits = {}
            for d in o.wdeps:
                key = d.dma if d.dma is not None else d.eng
                if d.semval > waits.get(key, 0):
                    waits[key] = d.semval
            for key, v in waits.items():
                if known.get(key, 0) >= v:
                    continue
                eng.wait_ge(sems[key], v)
                known[key] = v
            inst = o.fn(eng)
            if o.dma is not None:
                inst.then_inc(sems[o.dma], 16)
            elif o.signal:
                inst.then_inc(sems[e], 1)
        for key in final_waits:
            eng.wait_ge(sems[key], self.dma_tot[key])


def build(debug=None, nblocks=NB, upto=9, cbud=10 ** 9, do_schedule=True):
    nc = bass.Bass("TRN2", target_bir_lowering=False)
    dt = nc.dram_tensor
    x = dt("x", [S, D], F32, kind="ExternalInput").ap()
    norm_gain = dt("norm_gain", [D], F32, kind="ExternalInput").ap()
    w_in = dt("w_in", [D, DIN], F32, kind="ExternalInput").ap()
    pool_w = dt("pool_w", [4, 128, 128], F32, kind="ExternalInput").ap()
    pool_scale = dt("pool_scale", [512], F32, kind="ExternalInput").ap()
    shift_mu = dt("shift_mu", [1664], F32, kind="ExternalInput").ap()
    w0 = dt("w0", [512], F32, kind="ExternalInput").ap()
    w_up = dt("w_up", [64, 512], F32, kind="ExternalInput").ap()
    a0 = dt("a0", [512], F32, kind="ExternalInput").ap()
    a_up = dt("a_up", [64, 512], F32, kind="ExternalInput").ap()
    k_k = dt("k_k", [512], F32, kind="ExternalInput").ap()
    k_a = dt("k_a", [512], F32, kind="ExternalInput").ap()
    r_k = dt("r_k", [512], F32, kind="ExternalInput").ap()
    gn_gain = dt("gn_gain", [512], F32, kind="ExternalInput").ap()
    gn_bias = dt("gn_bias", [512], F32, kind="ExternalInput").ap()
    w_out = dt("w_out", [D, D], F32, kind="ExternalInput").ap()
    final_gain = dt("final_gain", [D], F32, kind="ExternalInput").ap()
    cst_d = dt("cst", [128, CST_W], F32, kind="ExternalInput").ap()
    out = dt("out", [S, D], F32, kind="ExternalOutput").ap()
    dbg_out = {}
    if debug:
        for name, shp in debug.items():
            dbg_out[name] = dt("dbg_" + name, list(shp), F32, kind="ExternalOutput").ap()

    P = Plan()
    P.region_budget = cbud
    P.region_count = 0
    es = contextlib.ExitStack()
    with es:
        def sb(name, shape, dtype):
            return es.enter_context(nc.sbuf_tensor(name, list(shape), dtype))

        def psum(name, shape, dtype):
            return es.enter_context(nc.psum_tensor(name, list(shape), dtype))

        winb = sb("winb", [128, 8, DIN], BF16)
        woutb = sb("woutb", [128, 8, D], BF16)
        cst = sb("cst_sb", [128, CST_W], F32)
        poolw = sb("poolw", [128, 4, 128], BF16)
        lup = sb("lup", [128, 512], BF16)
        identb = sb("identb", [128, 128], BF16)
        bones = sb("bones", [128, 128], BF16)
        blkrk = sb("blkrk", [128, 128], BF16)
        fgain = sb("fgain", [128, D], F32)
        ngv = sb("ngv", [128, 8], F32)
        muv = sb("muv", [128, 13], F32)
        w0v = sb("w0v", [128, 4], F32)
        a0v = sb("a0v", [128, 4], F32)
        kkv = sb("kkv", [128, 4], F32)
        kav = sb("kav", [128, 4], F32)
        omka = sb("omka", [128, 4], F32)
        rkv = sb("rkv", [128, 4], F32)
        gng = sb("gng", [128, 4], F32)
        gnb = sb("gnb", [128, 4], F32)
        pscv = sb("pscv", [128, 4], F32)
        halo = sb("halo", [128, 13], F32)

        xt = [sb(f"xt{i}", [128, D], F32) for i in range(2)]
        hbf = sb("hbf", [128, D], BF16)
        junk = sb("junk", [128, D], BF16)
        ssx = sb("ssx", [128, 2 * S // 128], F32)
        sdx = sb("sdx", [128, 2 * S // 128], F32)
        rsx = sb("rsx", [128, 2 * S // 128], F32)
        hT2 = [sb(f"hT{i}", [128, 8, TB], BF16) for i in range(2)]

        NZS = 5
        zs = [sb(f"zs{i}", [128, TB], F32) for i in range(NZS)]
        dtmp = sb("dtmp", [128, TB], F32)
        UW = 15 + TB
        ua = [sb(f"ua{i}", [128, UW], F32) for i in range(4)]
        pA = sb("pA", [128, UW], F32)
        pB = sb("pB", [128, UW], F32)
        pooled = sb("pooled", [128, TB], BF16)
        sga = sb("sga", [128, 4, TB], BF16)
        sgb2 = [sb(f"sgb{i}", [128, 4, TB], BF16) for i in range(2)]
        lobf = sb("lobf", [128, TB], BF16)
        thg = sb("thg", [128, TB], F32)

        t_sg = sb("t_sg", [128, TB], F32)
        t_a = sb("t_a", [128, TB], F32)
        t_csg = sb("t_csg", [128, TB], F32)
        t_e2 = sb("t_e2", [128, TB], F32)
        t_e1 = sb("t_e1", [128, TB], F32)
        t_e3 = sb("t_e3", [128, TB], F32)
        t_e4 = sb("t_e4", [128, TB], F32)
        t_kkn = sb("t_kkn", [128, TB], F32)
        t_rn = sb("t_rn", [128, TB], F32)
        t_kp = sb("t_kp", [128, TB], F32)
        t_b = sb("t_b", [128, TB], F32)
        kk2bf = sb("kk2bf", [128, TB], BF16)
        rkbf = sb("rkbf", [128, TB], BF16)

        QT2 = [sb(f"QT{i}", [128, 8, NCH, 128], BF16) for i in range(2)]
        PT2 = [sb(f"PT{i}", [128, 4, NCH, 128], BF16) for i in range(2)]
        PPsrc = sb("PPsrc", [128, NCH, 128], BF16)
        vbf = sb("vbf", [128, TB], BF16)
        PPt2 = [sb(f"PPt{i}", [128, NCH, 4, 128], BF16) for i in range(2)]
        Zt2 = [sb(f"Zt{i}", [128, NCH, 4, 128], BF16) for i in range(2)]
        bon2 = [sb(f"bon{i}", [128, 4, TB], BF16) for i in range(2)]
        gam2 = [sb(f"gam{i}", [128, 4, NCH], F32) for i in range(2)]

        MTs = sb("MTs", [128, 8, 128], BF16)
        MM = [sb(f"MM{i}", [128, 2, 4, 64], BF16) for i in range(2)]
        PTch = [sb(f"PTch{i}", [128, 4, 64], BF16) for i in range(4)]
        GXs = sb("GXs", [128, 4, 64], BF16)
        Xs = sb("Xs", [128, 4, 64], F32)
        Sst = sb("Sst", [128, 4, 64], F32)
        Sdec = sb("Sdec", [128, 4, 64], F32)
        Sbf = [sb(f"Sbf{i}", [128, 4, 64], BF16) for i in range(2)]
        ycp = sb("ycp", [128, 512], F32)
        ysq = sb("ysq", [128, 512], F32)
        ynb = [sb(f"ynb{i}", [128, 512], BF16) for i in range(NTT)]
        st1 = sb("st1", [128, 8], F32)
        st2 = sb("st2", [128, 8], F32)
        stm = sb("stm", [128, 8], F32)
        stv = sb("stv", [128, 8], F32)
        t1 = sb("t1", [128, TB], F32)
        ycat = sb("ycat", [128, 8, TB], BF16)
        xe = [sb(f"xe{i}", [128, D], F32) for i in range(2)]

        psNM = [psum(f"psNM{i}", [128, 512], F32) for i in range(2)]
        psT = psum("psT", [128, 512], F32)
        psSm = psum("psSm", [128, 512], F32)
        psM1 = psum("psM1", [128, 512], F32)
        psP = psum("psP", [128, 512], F32)
        psG = psum("psG", [128, 512], F32)
        psY = psum("psY", [128, 512], F32)
        psTb = psT[:, :].bitcast(BF16)
        psYb = psY[:, :].bitcast(BF16)
        psM1b = psM1[:, :].bitcast(BF16)

        dbg_dumps = []

        def dump(name, ap, key):
            if debug and name in dbg_out:
                dbg_dumps.append((name, ap, key))

        def dma(eng, out_ap, in_ap, key, reads=(), writes=(), group=None, slow=False):
            if slow:
                fn = lambda e, o=out_ap, i=in_ap: e.dma_start(out=o, in_=i, allow_slow_non_contiguous=True)
            else:
                fn = lambda e, o=out_ap, i=in_ap: e.dma_start(out=o, in_=i)
            return P.op(eng, fn, reads=reads, writes=writes, dma=key, group=group)

        dma("sync", cst[:, :], cst_d[:, :], "dc", writes=["cst"], group=0)

        def vec_load(tile, src, n, key):
            dma("sync", tile[:, :], src.rearrange("(j p) -> p j", p=128), "dc", writes=[key], group=0, slow=True)

        vec_load(ngv, norm_gain, 8, "ngv")
        vec_load(muv, shift_mu, 13, "muv")
        vec_load(w0v, w0, 4, "w0v")
        vec_load(a0v, a0, 4, "a0v")
        vec_load(kkv, k_k, 4, "kkv")
        vec_load(kav, k_a, 4, "kav")
        vec_load(rkv, r_k, 4, "rkv")
        vec_load(gng, gn_gain, 4, "gng")
        vec_load(gnb, gn_bias, 4, "gnb")
        vec_load(pscv, pool_scale, 4, "pscv")
        dma("sync", fgain[:, :], final_gain.partition_broadcast(128), "dc", writes=["fgain"], group=0)

        for kc in range(8):
            dma("pool", woutb[:, kc, :], w_out[kc * 128:(kc + 1) * 128, :], "dw", writes=[("woutb", kc)], group=0)
        dma("pool", poolw[:, :, :], pool_w.rearrange("g c d -> c g d"), "dw", writes=["poolw"], group=0)
        dma("pool", lup[0:64, :], w_up[:, :], "dw", writes=[("lup", 0)], group=0)
        dma("pool", lup[64:128, :], a_up[:, :], "dw", writes=[("lup", 1)], group=0)

        P.op("pool", lambda e: e.memset(halo[:, :], 0.0), writes=["halo"])
        P.op("pool", lambda e: e.memset(Sst[:, :, :], 0.0), writes=["Sst"])
        P.op("pool", lambda e: e.memset(Sbf[0][:, :, :], 0.0), writes=["Sbf0"])
        for g in range(4):
            P.op("pool", lambda e, g=g: e.memset(ua[g][:, 0:15], 0.0), writes=[("ua", g)])
        P.op("dve", lambda e: e.tensor_copy(out=identb[:, :], in_=cst[:, CO_ID:CO_ID + 128]),
             reads=["cst"], writes=["identb"])
        P.op("dve", lambda e: e.tensor_copy(out=bones[:, :], in_=cst[:, CO_BONES:CO_BONES + 128]),
             reads=["cst"], writes=["bones"])
        neghalf = sb("neghalf", [128, 16], F32)
        omm = sb("omm", [128, 13], F32)
        P.op("pool", lambda e: e.memset(neghalf[:, :], -0.5), writes=["neghalf"])
        half = sb("half", [128, 1], F32)
        P.op("pool", lambda e: e.memset(half[:, :], 0.5), writes=["half"])
        P.op("dve", lambda e: e.tensor_scalar(out=omm[:, :], in0=muv[:, :], scalar1=-1.0, scalar2=1.0,
                                              op0=ALU.mult, op1=ALU.add),
             reads=["muv"], writes=["omm"])
        for vt, vk in ((w0v, "w0v"), (a0v, "a0v"), (pscv, "pscv"), (gng, "gng"), (gnb, "gnb"), (rkv, "rkv")):
            P.op("dve", lambda e, vt=vt: e.tensor_scalar(out=vt[:, :], in0=vt[:, :], scalar1=0.5, scalar2=None,
                                                         op0=ALU.mult),
                 reads=[vk], writes=[vk])
        P.op("dve", lambda e: e.tensor_scalar(out=omka[:, :], in0=kav[:, :], scalar1=-1.0, scalar2=1.0,
                                              op0=ALU.mult, op1=ALU.add),
             reads=["kav"], writes=["omka"])

        cast_engs = ("dve", "act", "dve", "act", "pool")
        qstage = QT2[1][:, :, :, :].rearrange("p h c t -> p (h c t)").bitcast(F32)
        stg = [(qstage[:, 0:WST], [("QT", 1, 0), ("QT", 1, 1)], "dq0"),
               (qstage[:, 1024:1024 + WST], [("QT", 1, 2), ("QT", 1, 3)], "dq1"),
               (xe[0][:, 0:WST], [("xe", 0)], "de0"), (xe[1][:, 0:WST], [("xe", 1)], "de1")]
        n = 0
        for pc in (3, 1, 2, 0):
            for kc in range(8):
                sidx = n % 4
                stg_t, stg_key, stg_dk = stg[sidx]
                dma("sync", stg_t, w_in[kc * 128:(kc + 1) * 128, pc * WST:(pc + 1) * WST],
                    stg_dk, writes=stg_key)
                ce = cast_engs[n % 5]
                o_ap = winb[:, kc, pc * WST:(pc + 1) * WST]
                i_ap = stg_t
                g_ap = ngv[:, kc:kc + 1]
                if ce == "act":
                    fn = lambda e, o=o_ap, i=i_ap, g=g_ap: e.activation(out=o, in_=i, func=AF.Copy, scale=g)
                elif ce == "pool":
                    fn = lambda e, o=o_ap, i=i_ap, g=g_ap: e.tensor_scalar(out=o, in0=i, scalar1=g, scalar2=0.0,
                                                                          op0=ALU.mult, op1=ALU.add)
                else:
                    fn = lambda e, o=o_ap, i=i_ap, g=g_ap: e.tensor_scalar(out=o, in0=i, scalar1=g, scalar2=None,
                                                                          op0=ALU.mult)
                P.op(ce, fn, reads=stg_key + ["ngv"], writes=[("winb", kc, pc)])
                n += 1

        for par in range(2):
            P.op("dve", lambda e, par=par: e.memset(QT2[par][:, :, :, :].rearrange("p h c t -> p (h c t)"), 0.0),
                 writes=[("QT", par, j) for j in range(4)])
        blkrk4 = sb("blkrk4", [128, 4, 128], BF16)
        for j in range(4):
            P.op("dve", lambda e, j=j: e.tensor_scalar(out=blkrk4[:, j, :], in0=cst[:, CO_BONES:CO_BONES + 128],
                                                       scalar1=rkv[:, j:j + 1], scalar2=None, op0=ALU.mult),
                 reads=["cst", "rkv"], writes=["blkrk4"])

        FT_ORDER = [20]
        for j in range(4):
            FT_ORDER += [12 + j, 8 + j, 16 + j, 21 + j]
        for g in range(4):
            FT_ORDER += [g, 4 + g]
        zrot = [0]
        psz_rot = [0]
        sm_rot = [0]
        xrot = [0]
        erot = [0]

        def sm_half():
            h = sm_rot[0] % 2
            sm_rot[0] += 1
            return psT[:, h * 256:(h + 1) * 256], "psT"

        def block_body(tb):
            t0 = tb * TB
            par = tb % 2
            QT, PT, PPt, Zt = QT2[par], PT2[par], PPt2[par], Zt2[par]
            bon, gam, sgb = bon2[par], gam2[par], sgb2[par]
            hT = hT2[par]
            khT = "hT%d" % par
            P.tag = (tb, "A")
            P.section = "A"
            for i in range(NTT):
                col = tb * NTT + i
                xi = xrot[0] % 2
                xrot[0] += 1
                r0 = t0 + i * 128
                dma("sync", xt[xi][:, :], x[r0:r0 + 128, :], f"dx{xi}", writes=[("xt", xi)])
                P.op("act", lambda e, xi=xi, col=col: e.activation(out=hbf[:, :], in_=xt[xi][:, :], func=AF.Square,
                                                                   accum_out=ssx[:, col:col + 1]),
                     reads=[("xt", xi)], writes=["hbf", ("ssx", col)])
                P.op("pool", lambda e, col=col: e.tensor_scalar(out=sdx[:, col:col + 1], in0=ssx[:, col:col + 1],
                                                                scalar1=1.0 / D, scalar2=NORM_EPS,
                                                                op0=ALU.mult, op1=ALU.add),
                     reads=[("ssx", col)], writes=[("sdx", col)])
                P.op("pool", lambda e, col=col: e.tensor_tensor(out=rsx[:, col:col + 1], in0=sdx[:, col:col + 1],
                                                                in1=neghalf[:, 0:1], op=ALU.pow),
                     reads=[("sdx", col), "neghalf"], writes=[("rsx", col)])
                P.op("dve", lambda e, xi=xi, col=col: e.tensor_scalar(out=hbf[:, :], in0=xt[xi][:, :],
                                                                      scalar1=rsx[:, col:col + 1], scalar2=None,
                                                                      op0=ALU.mult),
                     reads=[("xt", xi), ("rsx", col)], writes=["hbf"])
                for kc in range(8):
                    P.op("pe", lambda e, kc=kc: e.transpose(out=psM1b[:, kc * 128:(kc + 1) * 128],
                                                            in_=hbf[:, kc * 128:(kc + 1) * 128],
                                                            identity=identb[:, :]),
                         reads=["hbf", "identb"], writes=["psM1"])
                P.op("dve", lambda e, i=i: e.tensor_copy(
                    out=hT[:, :, i * 128:(i + 1) * 128],
                    in_=psM1b.rearrange("p (k t) -> p k t", t=128)),
                    reads=["psM1"], writes=[khT])

            if upto < 2:
                return
            P.tag = (tb, "B")
            P.section = "B"
            zmap = {}
            for ft in FT_ORDER:
                pz = psz_rot[0] % 2
                psz_rot[0] += 1
                zp = psP[:, 0:TB]
                for kc in range(8):
                    P.op("pe", lambda e, zp=zp, kc=kc, ft=ft: e.matmul(
                        zp, lhsT=winb[:, kc, ft * 128:(ft + 1) * 128], rhs=hT[:, kc, :],
                        start=(kc == 0), stop=(kc == 7)),
                        reads=[("winb", kc, q) for q in sorted({(ft * 128) // WST, (ft * 128 + 127) // WST})] + [khT],
                        writes=["psP"])
                if 8 <= ft <= 20:
                    zi = zrot[0] % NZS
                    zrot[0] += 1
                    zmap[ft] = zi
                    js = ft - 8
                    zz = zs[zi]
                    P.op("act", lambda e, zz=zz, zp=zp: e.activation(out=zz[:, :], in_=zp, func=AF.Copy),
                         reads=["psP"], writes=[("zs", zi)])
                    if upto < 2.2:
                        continue
                    P.op("act", lambda e, zz=zz, js=js: e.activation(out=dtmp[:, 1:TB], in_=zz[:, 0:TB - 1],
                                                                     func=AF.Copy, scale=muv[:, js:js + 1]),
                         reads=[("zs", zi), "muv"], writes=["dtmp"])
                    P.op("pool", lambda e, js=js: e.tensor_scalar(out=dtmp[:, 0:1], in0=halo[:, js:js + 1],
                                                                  scalar1=muv[:, js:js + 1], scalar2=0.0,
                                                                  op0=ALU.mult, op1=ALU.add),
                         reads=[("halo", js), "muv", "dtmp"], writes=["dtmp"])
                    P.op("pool", lambda e, zz=zz, js=js: e.tensor_copy(out=halo[:, js:js + 1], in_=zz[:, TB - 1:TB]),
                         reads=[("zs", zi)], writes=[("halo", js)])
                    P.op("dve", lambda e, zz=zz, js=js: e.scalar_tensor_tensor(
                        out=zz[:, :], in0=zz[:, :], scalar=omm[:, js:js + 1], in1=dtmp[:, :],
                        op0=ALU.mult, op1=ALU.add),
                        reads=[("zs", zi), "dtmp", "omm"], writes=[("zs", zi)])
                    if tb == 0 and debug and ("sh%d" % ft) in dbg_out:
                        dump("sh%d" % ft, zz[:, :], ("zs", zi))
                elif ft >= 21:
                    j = ft - 21
                    P.op("act", lambda e, zp=zp: e.activation(out=thg[:, :], in_=zp, func=AF.Tanh, scale=0.5),
                         reads=["psP"], writes=["thg"])
                    P.op("dve", lambda e, j=j, zp=zp: e.scalar_tensor_tensor(
                        out=sgb[:, j, :], in0=thg[:, :], scalar=1.0, in1=zp, op0=ALU.add, op1=ALU.mult),
                        reads=["psP", "thg"], writes=[("sgb", par, j)])
                elif ft >= 4:
                    g = ft - 4
                    P.op("act", lambda e, zp=zp: e.activation(out=thg[:, :], in_=zp, func=AF.Tanh, scale=0.5),
                         reads=["psP"], writes=["thg"])
                    P.op("dve", lambda e, g=g, zp=zp: e.scalar_tensor_tensor(
                        out=sga[:, g, :], in0=thg[:, :], scalar=1.0, in1=zp, op0=ALU.add, op1=ALU.mult),
                        reads=["psP", "thg"], writes=[("sga", g)])
                else:
                    g = ft
                    P.op("act", lambda e, g=g, zp=zp: e.activation(out=ua[g][:, 15:UW], in_=zp, func=AF.Copy),
                         reads=["psP"], writes=[("ua", g)])

                if upto < 2.4:
                    continue
                if ft == 20:
                    zz = zs[zmap[20]]
                    P.op("act", lambda e, zz=zz: e.activation(out=lobf[0:64, :], in_=zz[0:64, :], func=AF.Tanh),
                         reads=[("zs", zmap[20])], writes=["lobf"])
                    P.op("act", lambda e, zz=zz: e.activation(out=lobf[64:128, :], in_=zz[64:128, :], func=AF.Copy),
                         reads=[("zs", zmap[20])], writes=["lobf"])
                if 16 <= ft <= 19:
                    j = ft - 16
                    zk = zs[zmap[12 + j]]
                    zr = zs[zmap[8 + j]]
                    zv = zs[zmap[16 + j]]
                    kK, kR, kV = ("zs", zmap[12 + j]), ("zs", zmap[8 + j]), ("zs", zmap[16 + j])
                    pw, pwk = sm_half()
                    P.op("pe", lambda e, pw=pw, j=j: e.matmul(pw, lhsT=lup[0:64, j * 128:(j + 1) * 128],
                                                              rhs=lobf[0:64, :], start=True, stop=True),
                         reads=[("lup", 0), "lobf"], writes=[pwk])
                    P.op("act", lambda e, pw=pw, j=j: e.activation(out=t_sg[:, :], in_=pw, func=AF.Tanh,
                                                                   bias=w0v[:, j:j + 1], scale=0.5),
                         reads=[pwk, "w0v"], writes=["t_sg"])
                    P.op("act", lambda e: e.activation(out=t_sg[:, :], in_=t_sg[:, :], func=AF.Identity,
                                                       scale=0.5, bias=half[:, 0:1]),
                         reads=["t_sg", "half"], writes=["t_sg"])
                    pa, pak = sm_half()
                    P.op("pe", lambda e, pa=pa, j=j: e.matmul(pa, lhsT=lup[64:128, j * 128:(j + 1) * 128],
                                                              rhs=lobf[64:128, :], start=True, stop=True),
                         reads=[("lup", 1), "lobf"], writes=[pak])
                    P.op("act", lambda e, pa=pa, j=j: e.activation(out=t_a[:, :], in_=pa, func=AF.Tanh,
                                                                   bias=a0v[:, j:j + 1], scale=0.5),
                         reads=[pak, "a0v"], writes=["t_a"])
                    P.op("act", lambda e: e.activation(out=t_a[:, :], in_=t_a[:, :], func=AF.Identity,
                                                       scale=0.5, bias=half[:, 0:1]),
                         reads=["t_a", "half"], writes=["t_a"])
                    P.op("dve", lambda e: e.tensor_tensor_scan(out=t_csg[:, :], data0=cst[:, CO_RST:CO_RST + TB],
                                                               data1=t_sg[:, :], initial=0.0,
                                                               op0=ALU.mult, op1=ALU.add),
                         reads=["cst", "t_sg"], writes=["t_csg"])
                    P.op("pool", lambda e: e.tensor_tensor(out=t_e2[:, :], in0=t_csg[:, :], in1=t_sg[:, :],
                                                           op=ALU.subtract),
                         reads=["t_csg", "t_sg"], writes=["t_e2"])
                    P.op("act", lambda e: e.activation(out=t_e2[:, :], in_=t_e2[:, :], func=AF.Exp, scale=-C0),
                         reads=["t_e2"], writes=["t_e2"])
                    P.op("act", lambda e: e.activation(out=t_e1[:, :], in_=t_csg[:, :], func=AF.Exp, scale=-C0),
                         reads=["t_csg"], writes=["t_e1"])
                    P.op("act", lambda e: e.activation(out=t_e3[:, :], in_=t_csg[:, :], func=AF.Exp, scale=C0),
                         reads=["t_csg"], writes=["t_e3"])
                    e1v = t_e1[:, :].rearrange("p (c t) -> p c t", t=64)
                    P.op("pool", lambda e, j=j, e1v=e1v: e.tensor_copy(out=gam[:, j, :], in_=e1v[:, :, 63]),
                         reads=["t_e1"], writes=[("gam", par, j)])
                    P.op("pool", lambda e, j=j: e.tensor_tensor(
                        out=t_e4[:, :].rearrange("p (c t) -> p c t", t=64),
                        in0=t_e3[:, :].rearrange("p (c t) -> p c t", t=64),
                        in1=gam[:, j, :].unsqueeze(2).to_broadcast([128, NCH, 64]), op=ALU.mult),
                        reads=["t_e3", ("gam", par, j)], writes=["t_e4"])
                    P.op("act", lambda e, zk=zk, j=j: e.activation(out=kk2bf[:, :], in_=zk[:, :], func=AF.Square,
                                                                   scale=kkv[:, j:j + 1]),
                         reads=[kK, "kkv"], writes=["kk2bf"])
                    pss, pssk = sm_half()
                    P.op("pe", lambda e, pss=pss: e.matmul(pss, lhsT=bones[:, :], rhs=kk2bf[:, :], start=True,
                                                           stop=True),
                         reads=["bones", "kk2bf"], writes=[pssk])
                    P.op("dve", lambda e, pss=pss: e.tensor_scalar(out=t_rn[:, :], in0=pss, scalar1=1e-18,
                                                                   scalar2=None, op0=ALU.max),
                         reads=[pssk], writes=["t_rn"])
                    P.op("act", lambda e: e.activation(out=t_rn[:, :], in_=t_rn[:, :], func=AF.Ln),
                         reads=["t_rn"], writes=["t_rn"])
                    P.op("act", lambda e: e.activation(out=t_rn[:, :], in_=t_rn[:, :], func=AF.Exp, scale=-0.5),
                         reads=["t_rn"], writes=["t_rn"])
                    P.op("dve", lambda e, zk=zk, j=j: e.scalar_tensor_tensor(
                        out=t_kkn[:, :], in0=zk[:, :], scalar=kkv[:, j:j + 1], in1=t_rn[:, :],
                        op0=ALU.mult, op1=ALU.mult),
                        reads=[kK, "kkv", "t_rn"], writes=["t_kkn"])
                    P.op("act", lambda e, j=j: e.activation(out=t_kp[:, :], in_=t_a[:, :], func=AF.Identity,
                                                            scale=kav[:, j:j + 1], bias=omka[:, j:j + 1]),
                         reads=["t_a", "kav", "omka"], writes=["t_kp"])
                    P.op("pool", lambda e, zk=zk: e.tensor_tensor(out=t_kp[:, :], in0=t_kp[:, :], in1=zk[:, :],
                                                                  op=ALU.mult),
                         reads=["t_kp", kK], writes=["t_kp"])
                    P.op("pool", lambda e: e.tensor_tensor(out=t_b[:, :], in0=t_kkn[:, :], in1=t_a[:, :],
                                                           op=ALU.mult),
                         reads=["t_kkn", "t_a"], writes=["t_b"])

                    def v3(t):
                        return t[:, :].rearrange("p (c t) -> p c t", t=64)

                    prods = [
                        ("dve", PT[:, j, :, 0:64], t_kp, "t_e3", t_e3, "t_kp", ("PT", par, j)),
                        ("pool", PT[:, j, :, 64:128], t_b, "t_e3", t_e3, "t_b", ("PT", par, j)),
                        ("dve", PPsrc[:, :, 0:64], t_kp, "t_e4", t_e4, "t_kp", "PPsrc"),
                        ("pool", PPsrc[:, :, 64:128], t_b, "t_e4", t_e4, "t_b", "PPsrc"),
                    ]
                    for e2 in range(2):
                        rs = slice(64 * e2, 64 * e2 + 64)
                        P.op("dve", lambda e, rs=rs, e2=e2, j=j, zr=zr: e.tensor_tensor(
                            out=QT[rs, 2 * j + e2, :, 0:64], in0=v3(zr)[rs], in1=v3(t_e1)[rs], op=ALU.mult),
                            reads=[kR, "t_e1"], writes=[("QT", par, j)])
                        P.op("pool", lambda e, rs=rs, e2=e2, j=j: e.tensor_tensor(
                            out=QT[rs, 2 * j + e2, :, 64:128], in0=v3(t_kkn)[rs], in1=v3(t_e2)[rs], op=ALU.mult),
                            reads=["t_kkn", "t_e2"], writes=[("QT", par, j)])
                    for (en, o_ap, a_t, ek, e_t, akey, okey) in prods:
                        P.op(en, lambda e, o_ap=o_ap, a_t=a_t, e_t=e_t: e.tensor_tensor(
                            out=o_ap, in0=v3(a_t), in1=v3(e_t), op=ALU.mult),
                            reads=[akey, ek], writes=[okey])
                    P.op("dve", lambda e, zr=zr: e.tensor_tensor(out=rkbf[:, :], in0=zr[:, :], in1=t_kp[:, :],
                                                                 op=ALU.mult),
                         reads=[kR, "t_kp"], writes=["rkbf"])
                    P.op("act", lambda e, zv=zv: e.activation(out=vbf[:, :], in_=zv[:, :], func=AF.Copy),
                         reads=[kV], writes=["vbf"])
                    pbo, pbok = sm_half()
                    P.op("pe", lambda e, pbo=pbo, j=j: e.matmul(pbo, lhsT=blkrk4[:, j, :], rhs=rkbf[:, :],
                                                                start=True, stop=True),
                         reads=["blkrk4", "rkbf"], writes=[pbok])
                    P.op("dve", lambda e, pbo=pbo, zv=zv, j=j: e.tensor_tensor(out=bon[:, j, :], in0=pbo,
                                                                               in1=zv[:, :], op=ALU.mult),
                         reads=[pbok, kV], writes=[("bon", par, j)])
                    for c in range(NCH):
                        P.op("pe", lambda e, c=c: e.transpose(out=psTb[:, c * 128:(c + 1) * 128],
                                                              in_=PPsrc[:, c, :], identity=identb[:, :]),
                             reads=["PPsrc", "identb"], writes=["psT"])
                    for c in range(NCH):
                        P.op("pe", lambda e, c=c: e.transpose(out=psTb[0:64, 512 + c * 128:512 + (c + 1) * 128],
                                                              in_=vbf[:, c * 64:(c + 1) * 64],
                                                              identity=identb[:, :]),
                             reads=["vbf", "identb"], writes=["psT"])
                    P.op("act", lambda e, j=j: e.activation(
                        out=PPt[:, :, j, :], in_=psTb[:, 0:512].rearrange("p (c f) -> p c f", f=128),
                        func=AF.Copy),
                        reads=["psT"], writes=[("PPt", par, j)])
                    P.op("act", lambda e, j=j: e.activation(
                        out=Zt[0:64, :, j, :], in_=psTb[0:64, 512:1024].rearrange("p (c f) -> p c f", f=128),
                        func=AF.Copy),
                        reads=["psT"], writes=[("ZtV", par, j)])
                P.in_region = False
                if upto < 2.6:
                    continue
                if 4 <= ft <= 7:
                    g = ft - 4
                    w = 2 << g
                    u = ua[g]
                    src = u
                    bufs = [pA, pB]
                    for lvl in range(g + 1):
                        sh = 1 << lvl
                        dst = bufs[lvl % 2]
                        lo = 2 * sh - 1
                        P.op("pool", lambda e, src=src, dst=dst, sh=sh, lo=lo: e.tensor_tensor(
                            out=dst[:, lo:UW], in0=src[:, lo:UW], in1=src[:, lo - sh:UW - sh], op=ALU.add),
                            reads=[("ua", g), "pA", "pB"], writes=["pA" if dst is pA else "pB"])
                        src = dst
                    skey = "pA" if src is pA else "pB"
                    P.op("dve", lambda e, src=src, u=u, w=w: e.scalar_tensor_tensor(
                        out=pooled[:, :], in0=src[:, 15:UW], scalar=1.0 / w, in1=u[:, 15:UW],
                        op0=ALU.mult, op1=ALU.subtract),
                        reads=[skey, ("ua", g)], writes=["pooled"])
                    if tb == 0:
                        nfix = w - 1
                        P.op("dve", lambda e, src=src, g=g, nfix=nfix: e.tensor_tensor(
                            out=t1[:, 0:nfix], in0=src[:, 15:15 + nfix],
                            in1=cst[:, CO_ICNT + g * 15:CO_ICNT + g * 15 + nfix], op=ALU.mult),
                            reads=[skey, "cst"], writes=["t1"])
                        P.op("dve", lambda e, u=u, nfix=nfix: e.tensor_tensor(
                            out=pooled[:, 0:nfix], in0=t1[:, 0:nfix], in1=u[:, 15:15 + nfix], op=ALU.subtract),
                            reads=["t1", ("ua", g), "pooled"], writes=["pooled"])
                    P.op("pool", lambda e, u=u: e.tensor_copy(out=u[:, 0:15], in_=u[:, TB:TB + 15]),
                         reads=[("ua", g)], writes=[("ua", g)])
                    pm, pmk = sm_half()
                    P.op("pe", lambda e, pm=pm, g=g: e.matmul(pm, lhsT=poolw[:, g, :], rhs=pooled[:, :],
                                                              start=True, stop=True),
                         reads=["poolw", "pooled"], writes=[pmk])
                    P.op("dve", lambda e, pm=pm, g=g: e.scalar_tensor_tensor(
                        out=ycat[:, g, :], in0=pm, scalar=pscv[:, g:g + 1], in1=sga[:, g, :],
                        op0=ALU.mult, op1=ALU.mult),
                        reads=[pmk, "pscv", ("sga", g)], writes=[("ycat", g)])

            if tb == 0:
                dump("QT", None, None)

            if upto < 3:
                return
            P.in_region = True
            P.region_count = 0
            P.tag = (tb, "D")
            P.section = "D"
            mmt4 = cst[:, CO_MMT:CO_MMT + 128].unsqueeze(1).to_broadcast([128, 4, 128])
            ml2 = cst[:, CO_ML:CO_ML + 128].rearrange("p (m t) -> p m t", t=64).unsqueeze(2).to_broadcast(
                [128, 2, 4, 64])
            irep4 = cst[:, CO_IREP:CO_IREP + 64].unsqueeze(1).to_broadcast([128, 4, 64])
            for c in range(NCH):
                cg = tb * NCH + c
                sb_old = Sbf[cg % 2]
                sb_new = Sbf[(cg + 1) % 2]
                kold = "Sbf%d" % (cg % 2)
                knew = "Sbf%d" % ((cg + 1) % 2)
                qk = [("QT", par, j) for j in range(4)]
                pk = [("PT", par, j) for j in range(4)]
                for rnd in range(2):
                    for h in range(4 * rnd, 4 * rnd + 4):
                        j = h // 2
                        P.op("pe", lambda e, h=h, j=j, c=c: e.matmul(
                            psM1[:, (h % 4) * 128:(h % 4 + 1) * 128], lhsT=PT[:, j, c, :], rhs=QT[:, h, c, :],
                            start=True, stop=True),
                            reads=[("QT", par, j), ("PT", par, j)], writes=["psM1"])
                    P.op("dve", lambda e, rnd=rnd: e.tensor_tensor(
                        out=MTs[:, 4 * rnd:4 * rnd + 4, :], in0=psM1[:, :].rearrange("p (h t) -> p h t", t=128),
                        in1=mmt4, op=ALU.mult),
                        reads=["psM1", "cst"], writes=["MTs"])
                HORD = (0, 4, 1, 5, 2, 6, 3, 7)
                pbase = 2 * (cg % 2)

                def hq(h):
                    return slice(64 * (h // 4), 64 * (h // 4) + 64), h % 4

                for h in HORD:
                    j = h // 2
                    ps_, hh = hq(h)
                    P.op("pe", lambda e, h=h, j=j, c=c, ps_=ps_, hh=hh: e.matmul(
                        psG[ps_, hh * 64:(hh + 1) * 64], lhsT=QT[:, h, c, 64:128],
                        rhs=PT[:, j, c, 64:128], start=True, stop=True),
                        reads=[("QT", par, j), ("PT", par, j)], writes=["psG"])
                    P.op("pe", lambda e, h=h, j=j, c=c, ps_=ps_, hh=hh: e.matmul(
                        psG[ps_, 256 + hh * 64:256 + (hh + 1) * 64], lhsT=PT[:, j, c, 64:128],
                        rhs=QT[:, h, c, 64:128], start=True, stop=True),
                        reads=[("QT", par, j), ("PT", par, j)], writes=["psG"])
                P.op("dve", lambda e: e.tensor_tensor(
                    out=MM[0][:, :, :, :], in0=psG[:, :].rearrange("p (m h t) -> p m h t", m=2, t=64),
                    in1=ml2, op=ALU.mult),
                    reads=["psG", "cst"], writes=["MM0"])
                P.op("pool", lambda e, pbase=pbase: e.tensor_tensor(
                    out=PTch[pbase][:, :, :], in0=irep4, in1=MM[0][:, 1, :, :], op=ALU.subtract),
                    reads=["cst", "MM0"], writes=["PTch%d" % pbase])
                M_cur, P_cur = 0, 0
                for lvl in range(1, 6):
                    mi = MM[M_cur]
                    mo = MM[1 - M_cur]
                    kmi, kmo = "MM%d" % M_cur, "MM%d" % (1 - M_cur)
                    for h in HORD:
                        ps_, hh = hq(h)
                        P.op("pe", lambda e, ps_=ps_, hh=hh, mi=mi: e.matmul(
                            psNM[0][ps_, hh * 64:(hh + 1) * 64], lhsT=mi[ps_, 1, hh, :], rhs=mi[ps_, 0, hh, :],
                            start=True, stop=True),
                            reads=[kmi], writes=[("psNM", 0)])
                    if lvl < 5:
                        for h in HORD:
                            ps_, hh = hq(h)
                            P.op("pe", lambda e, ps_=ps_, hh=hh, mi=mi: e.matmul(
                                psNM[0][ps_, 256 + hh * 64:256 + (hh + 1) * 64], lhsT=mi[ps_, 0, hh, :],
                                rhs=mi[ps_, 1, hh, :], start=True, stop=True),
                                reads=[kmi], writes=[("psNM", 0)])
                        P.op("act", lambda e, mo=mo: e.activation(
                            out=mo[:, :, :, :].rearrange("p m h t -> p (m h t)"), in_=psNM[0][:, :], func=AF.Copy),
                            reads=[("psNM", 0)], writes=[kmo])
                    else:
                        P.op("act", lambda e, mo=mo: e.activation(
                            out=mo[:, 0, :, :].rearrange("p h t -> p (h t)"), in_=psNM[0][:, 0:256], func=AF.Copy),
                            reads=[("psNM", 0)], writes=[kmo])
                    M_cur = 1 - M_cur
                    p_in = PTch[pbase + P_cur]
                    p_out = PTch[pbase + 1 - P_cur]
                    kp_in, kp_out = "PTch%d" % (pbase + P_cur), "PTch%d" % (pbase + 1 - P_cur)
                    for h in HORD:
                        ps_, hh = hq(h)
                        P.op("pe", lambda e, ps_=ps_, hh=hh, mo=mo, p_in=p_in: e.matmul(
                            psSm[ps_, hh * 64:(hh + 1) * 64], lhsT=mo[ps_, 0, hh, :], rhs=p_in[ps_, hh, :],
                            start=True, stop=True),
                            reads=[kmo, kp_in], writes=["psSm"])
                    P.op("dve", lambda e, p_in=p_in, p_out=p_out: e.tensor_tensor(
                        out=p_out[:, :, :].rearrange("p h t -> p (h t)"), in0=psSm[:, 0:256],
                        in1=p_in[:, :, :].rearrange("p h t -> p (h t)"), op=ALU.add),
                        reads=["psSm", kp_in], writes=[kp_out])
                    P_cur = 1 - P_cur
                pfin = PTch[pbase + P_cur]
                kpfin = "PTch%d" % (pbase + P_cur)
                P.op("pool", lambda e, c=c: e.tensor_tensor(
                    out=Sdec[:, :, :], in0=Sst[:, :, :],
                    in1=gam[:, :, c].unsqueeze(2).to_broadcast([128, 4, 64]), op=ALU.mult),
                    reads=["Sst"] + [("gam", par, j) for j in range(4)], writes=["Sdec"])
                for h in HORD:
                    j, e2 = h // 2, h % 2
                    ps_, hh = hq(h)
                    P.op("pe", lambda e, h=h, j=j, e2=e2, c=c, ps_=ps_, hh=hh: e.matmul(
                        psNM[1][ps_, hh * 64:(hh + 1) * 64], lhsT=MTs[0:64, h, 64:128],
                        rhs=Zt[0:64, c, j, e2 * 64:(e2 + 1) * 64], start=True, stop=True),
                        reads=["MTs", ("ZtV", par, j)], writes=[("psNM", 1)])
                P.op("act", lambda e: e.activation(out=Xs[:, :, :].rearrange("p h t -> p (h t)"),
                                                   in_=psNM[1][:, 0:256], func=AF.Copy),
                     reads=[("psNM", 1)], writes=["Xs"])
                for h in HORD:
                    j = h // 2
                    ps_, hh = hq(h)
                    P.op("pe", lambda e, h=h, j=j, c=c, sb_old=sb_old, ps_=ps_, hh=hh: e.matmul(
                        psG[ps_, hh * 64:(hh + 1) * 64], lhsT=QT[:, h, c, 64:128],
                        rhs=sb_old[:, j, :], start=True, stop=True),
                        reads=[("QT", par, j), kold], writes=["psG"])
                P.op("dve", lambda e: e.scalar_tensor_tensor(
                    out=GXs[:, :, :].rearrange("p h t -> p (h t)"), in0=psG[:, 0:256], scalar=-1.0,
                    in1=Xs[:, :, :].rearrange("p h t -> p (h t)"), op0=ALU.mult, op1=ALU.subtract),
                    reads=["psG", "Xs"], writes=["GXs"])
                for h in HORD:
                    ps_, hh = hq(h)
                    ub = psSm if h < 4 else psNM[1]
                    ubk = "psSm" if h < 4 else ("psNM", 1)
                    P.op("pe", lambda e, ps_=ps_, hh=hh, pfin=pfin, ub=ub: e.matmul(
                        ub[64:128, hh * 64:(hh + 1) * 64], lhsT=pfin[ps_, hh, :], rhs=GXs[ps_, hh, :],
                        start=True, stop=True),
                        reads=[kpfin, "GXs"], writes=[ubk])
                P.op("dve", lambda e, c=c: e.tensor_copy(
                    out=Zt[64:128, c, 0:2, :].rearrange("p j f -> p (j f)"), in_=psSm[64:128, 0:256]),
                    reads=["psSm"], writes=[("ZtU", par, c)])
                P.op("act", lambda e, c=c: e.activation(
                    out=Zt[64:128, c, 2:4, :].rearrange("p j f -> p (j f)"), in_=psNM[1][64:128, 0:256],
                    func=AF.Copy),
                    reads=[("psNM", 1)], writes=[("ZtU2", par, c)])
                zkeys = [("ZtV", par, j) for j in range(4)] + [("ZtU", par, c), ("ZtU2", par, c)]
                ypb = 64 * (c % 2)
                for h in range(8):
                    j, pb, e2 = h // 2, 64 * (h % 2), h % 2
                    P.op("pe", lambda e, h=h, j=j, pb=pb, c=c, ypb=ypb, sb_old=sb_old: e.matmul(
                        psY[ypb:ypb + 64, h * 64:(h + 1) * 64], lhsT=QT[:, h, c, 0:64],
                        rhs=sb_old[:, j, :], start=True, stop=False),
                        reads=[("QT", par, j), kold], writes=["psY"])
                    P.op("pe", lambda e, h=h, j=j, e2=e2, c=c, ypb=ypb: e.matmul(
                        psY[ypb:ypb + 64, h * 64:(h + 1) * 64], lhsT=MTs[:, h, 0:64],
                        rhs=Zt[:, c, j, e2 * 64:(e2 + 1) * 64], start=False, stop=True),
                        reads=["MTs"] + zkeys, writes=["psY"])
                for h in range(8):
                    j, pb, e2 = h // 2, 64 * (h % 2), h % 2
                    P.op("pe", lambda e, h=h, j=j, pb=pb, e2=e2, c=c: e.matmul(
                        psSm[pb:pb + 64, j * 64:(j + 1) * 64], lhsT=PPt[:, c, j, e2 * 64:(e2 + 1) * 64],
                        rhs=Zt[:, c, j, e2 * 64:(e2 + 1) * 64], start=True, stop=True),
                        reads=[("PPt", par, j)] + zkeys, writes=["psSm"])
                P.op("dve", lambda e: e.tensor_tensor(out=Sst[:, :, :].rearrange("p j v -> p (j v)"),
                                                      in0=psSm[:, 0:256],
                                                      in1=Sdec[:, :, :].rearrange("p j v -> p (j v)"), op=ALU.add),
                     reads=["psSm", "Sdec"], writes=["Sst"])
                P.op("act", lambda e, sb_new=sb_new: e.activation(out=sb_new[:, :, :], in_=Sst[:, :, :],
                                                                  func=AF.Copy),
                     reads=["Sst"], writes=[knew])
                if c % 2 == 1:
                    ti = c // 2
                    P.op("act", lambda e: e.activation(out=ycp[:, :], in_=psY[:, :], func=AF.Copy),
                         reads=["psY"], writes=["ycp"])
                    P.op("act", lambda e: e.activation(out=ysq[:, :], in_=psY[:, :], func=AF.Square),
                         reads=["psY"], writes=["ysq"])
                    P.op("dve", lambda e: e.tensor_reduce(out=st1[:, :],
                                                          in_=ycp[:, :].rearrange("p (h v) -> p h v", v=64),
                                                          axis=AX.X, op=ALU.add),
                         reads=["ycp"], writes=["st1"])
                    P.op("dve", lambda e: e.tensor_reduce(out=st2[:, :],
                                                          in_=ysq[:, :].rearrange("p (h v) -> p h v", v=64),
                                                          axis=AX.X, op=ALU.add),
                         reads=["ysq"], writes=["st2"])
                    P.op("dve", lambda e: e.tensor_scalar(out=stm[:, :], in0=st1[:, :], scalar1=1.0 / 64,
                                                          scalar2=None, op0=ALU.mult),
                         reads=["st1"], writes=["stm"])
                    P.op("dve", lambda e: e.tensor_tensor(out=stv[:, :], in0=stm[:, :], in1=stm[:, :], op=ALU.mult),
                         reads=["stm"], writes=["stv"])
                    P.op("dve", lambda e: e.scalar_tensor_tensor(out=stv[:, :], in0=st2[:, :], scalar=1.0 / 64,
                                                                 in1=stv[:, :], op0=ALU.mult, op1=ALU.subtract),
                         reads=["st2", "stv"], writes=["stv"])
                    P.op("pool", lambda e: e.tensor_scalar(out=stv[:, :], in0=stv[:, :], scalar1=1.0, scalar2=GN_EPS,
                                                           op0=ALU.mult, op1=ALU.add),
                         reads=["stv"], writes=["stv"])
                    P.op("pool", lambda e: e.tensor_tensor(out=stv[:, :], in0=stv[:, :], in1=neghalf[:, 0:8],
                                                           op=ALU.pow),
                         reads=["stv", "neghalf"], writes=["stv"])
                    P.op("pool", lambda e: e.tensor_tensor(
                        out=ycp[:, :].rearrange("p (h v) -> p h v", v=64),
                        in0=ycp[:, :].rearrange("p (h v) -> p h v", v=64),
                        in1=stm[:, :].unsqueeze(2).to_broadcast([128, 8, 64]), op=ALU.subtract),
                        reads=["ycp", "stm"], writes=["ycp"])
                    P.op("pool", lambda e, ti=ti: e.tensor_tensor(
                        out=ynb[ti][:, :].rearrange("p (h v) -> p h v", v=64),
                        in0=ycp[:, :].rearrange("p (h v) -> p h v", v=64),
                        in1=stv[:, :].unsqueeze(2).to_broadcast([128, 8, 64]), op=ALU.mult),
                        reads=["ycp", "stv"], writes=[("ynb", ti)])

            P.in_region = False
            if upto < 4:
                return
            P.tag = (tb, "E")
            P.section = "E"
            for ti in range(NTT):
                for j in range(4):
                    P.op("pe", lambda e, ti=ti, j=j: e.transpose(
                        out=psYb[:, j * TB + ti * 128:j * TB + (ti + 1) * 128],
                        in_=ynb[ti][:, j * 128:(j + 1) * 128], identity=identb[:, :]),
                        reads=[("ynb", ti), "identb"], writes=["psY"])
            for j in range(4):
                P.op("dve", lambda e, j=j: e.tensor_scalar(out=t1[:, :], in0=psYb[:, j * TB:(j + 1) * TB],
                                                           scalar1=gng[:, j:j + 1], scalar2=gnb[:, j:j + 1],
                                                           op0=ALU.mult, op1=ALU.add),
                     reads=["psY", "gng", "gnb"], writes=["t1"])
                P.op("pool", lambda e, j=j: e.tensor_tensor(out=t1[:, :], in0=t1[:, :], in1=bon[:, j, :],
                                                            op=ALU.add),
                     reads=["t1", ("bon", par, j)], writes=["t1"])
                P.op("pool", lambda e, j=j: e.tensor_tensor(out=ycat[:, 4 + j, :], in0=t1[:, :], in1=sgb[:, j, :],
                                                            op=ALU.mult),
                     reads=["t1", ("sgb", par, j)], writes=[("ycat", 4 + j)])

            ykeys = [("ycat", k) for k in range(8)]
            pso = [psM1, psG]
            psok = ["psM1", "psG"]
            for i in range(NTT):
                col = S // 128 + tb * NTT + i
                r0 = t0 + i * 128
                xi = erot[0] % 2
                erot[0] += 1
                xb = xe[xi]
                dma("sync", xb[:, :], x[r0:r0 + 128, :], f"de{xi}", writes=[("xe", xi)])
                for hf in range(2):
                    for kc in range(8):
                        P.op("pe", lambda e, hf=hf, kc=kc, i=i: e.matmul(
                            pso[hf][:, :], lhsT=ycat[:, kc, i * 128:(i + 1) * 128],
                            rhs=woutb[:, kc, hf * 512:(hf + 1) * 512], start=(kc == 0), stop=(kc == 7)),
                            reads=ykeys + [("woutb", kc)], writes=[psok[hf]])
                    P.op("dve", lambda e, hf=hf, xb=xb: e.tensor_tensor(
                        out=xb[:, hf * 512:(hf + 1) * 512], in0=pso[hf][:, :],
                        in1=xb[:, hf * 512:(hf + 1) * 512], op=ALU.add),
                        reads=[psok[hf], ("xe", xi)], writes=[("xe", xi)])
                P.op("act", lambda e, col=col, xb=xb: e.activation(out=junk[:, :], in_=xb[:, :], func=AF.Square,
                                                                   accum_out=ssx[:, col:col + 1]),
                     reads=[("xe", xi)], writes=["junk", ("ssx", col)])
                P.op("pool", lambda e, col=col: e.tensor_scalar(out=sdx[:, col:col + 1], in0=ssx[:, col:col + 1],
                                                                scalar1=1.0 / D, scalar2=NORM_EPS,
                                                                op0=ALU.mult, op1=ALU.add),
                     reads=[("ssx", col)], writes=[("sdx", col)])
                P.op("pool", lambda e, col=col: e.tensor_tensor(out=rsx[:, col:col + 1], in0=sdx[:, col:col + 1],
                                                                in1=neghalf[:, 0:1], op=ALU.pow),
                     reads=[("sdx", col), "neghalf"], writes=[("rsx", col)])
                P.op("dve", lambda e, col=col, xb=xb: e.scalar_tensor_tensor(
                    out=xb[:, :], in0=xb[:, :], scalar=rsx[:, col:col + 1], in1=fgain[:, :],
                    op0=ALU.mult, op1=ALU.mult),
                    reads=[("xe", xi), ("rsx", col), "fgain"], writes=[("xe", xi)])
                dma("sync", out[r0:r0 + 128, :], xb[:, :], f"do{xi}", reads=[("xe", xi)], writes=[("outd", xi)])

        nbl = nblocks if upto >= 1 else 0
        if nbl:
            P.only = {"A"}
            block_body(0)
        for tb in range(nbl):
            P.only = {"B"}
            block_body(tb)
            if tb + 1 < nbl:
                P.only = {"A"}
                block_body(tb + 1)
            P.only = {"D", "E"}
            block_body(tb)
        P.only = None
        P.section = None
        P.tag = None

        for name, ap, key in dbg_dumps:
            if ap is None:
                continue
            dma("sync", dbg_out[name], ap, "ddbg", reads=[key])

        _INFO["sbuf_left"] = nc.sbuf_bytes_remaining
        if do_schedule:
            P.schedule()
            cands = [(P.est_makespan, {e: list(v) for e, v in P.ops.items()})]
            if SCHED_TRIALS > 0:
                import random
                for t in range(SCHED_TRIALS):
                    P.prio_rng = random.Random(1234 + t)
                    P.prio_noise = SCHED_NOISE
                    P.schedule()
                    cands.append((P.est_makespan, {e: list(v) for e, v in P.ops.items()}))
                P.prio_rng = None
            cands.sort(key=lambda c: c[0])
            P.est_makespan, P.ops = cands[min(SCHED_PICK, len(cands) - 1)]
            _INFO["est_us"] = P.est_makespan
        P.finalize()
        keys = P.sem_keys()
        sems = {k: es.enter_context(nc.semaphore("s_" + k)) for k in keys}
        out_keys = [k for k in keys if k.startswith("do") or k == "ddbg"]
        with nc.Block() as block:
            @block.sync
            def _(eng):
                P.run_engine("sync", eng, sems, final_waits=out_keys)

            @block.scalar
            def _(eng):
                P.run_engine("act", eng, sems)

            @block.vector
            def _(eng):
                P.run_engine("dve", eng, sems)

            @block.gpsimd
            def _(eng):
                P.run_engine("pool", eng, sems)

            @block.tensor
            def _(eng):
                P.run_engine("pe", eng, sems)
    return nc


_CACHE = {}
_INFO = {}


def _prep_inputs(inputs):
    f = lambda a: np.ascontiguousarray(np.asarray(a, dtype=np.float32))
    shared = {
        "norm_gain": f(inputs["norm_gain"]).reshape(D),
        "w_in": f(inputs["w_in"]).reshape(D, DIN),
        "pool_w": f(inputs["pool_w"]).reshape(4, 128, 128),
        "pool_scale": f(inputs["pool_scale"]).reshape(512),
        "shift_mu": f(inputs["shift_mu"]).reshape(1664),
        "w0": f(inputs["w0"]).reshape(512),
        "w_up": f(inputs["w_up"]).reshape(64, 512),
        "a0": f(inputs["a0"]).reshape(512),
        "a_up": f(inputs["a_up"]).reshape(64, 512),
        "k_k": f(inputs["k_k"]).reshape(512),
        "k_a": f(inputs["k_a"]).reshape(512),
        "r_k": f(inputs["r_k"]).reshape(512),
        "gn_gain": f(inputs["gn_gain"]).reshape(512),
        "gn_bias": f(inputs["gn_bias"]).reshape(512),
        "w_out": f(inputs["w_out"]).reshape(D, D),
        "final_gain": f(inputs["final_gain"]).reshape(D),
        "cst": _make_consts(),
    }
    xs = f(inputs["x"])
    return [dict(shared, x=xs[b]) for b in range(xs.shape[0])]


def kernel(**inputs):
    in_maps = _prep_inputs(inputs)
    if "nc" not in _CACHE:
        _CACHE["nc"] = build()
    nc = _CACHE["nc"]
    res = run_bass_kernel_spmd(nc, in_maps, core_ids=list(range(8)))
    return np.stack([np.asarray(r["out"], dtype=np.float32) for r in res.results], axis=0)
```

```python
import contextlib
import numpy as np
import concourse.bass as bass
import concourse.mybir as mybir
from concourse.bass_utils import run_bass_kernel_spmd

F32 = mybir.dt.float32
BF16 = mybir.dt.bfloat16
AF = mybir.ActivationFunctionType
ALU = mybir.AluOpType
AX = mybir.AxisListType

S = 2048
D = 1024
DIN = 3200
TB = 256
NB = S // TB
NCH = TB // 64
NTT = TB // 128
C0 = float(np.exp(-0.5))
NORM_EPS = 1e-6
GN_EPS = 64e-5
WST = 800

CO_ID = 0
CO_BONES = 128
CO_MMT = 256
CO_ML = CO_MMT + 128
CO_IREP = CO_ML + 128
CO_RST = CO_IREP + 64
CO_ICNT = CO_RST + TB
CST_W = CO_ICNT + 60


def _make_consts():
    c = np.zeros((128, CST_W), np.float32)
    p = np.arange(128)
    c[:, CO_ID:CO_ID + 128] = np.eye(128, dtype=np.float32)
    c[:, CO_BONES:CO_BONES + 128] = (p[:, None] // 64 == p[None, :] // 64).astype(np.float32)
    s = (p % 64)[:, None]
    t = (p % 64)[None, :]
    colr = (p[None, :] < 64)
    m = np.where(colr, (s <= t), (s < t)).astype(np.float32)
    c[:, CO_MMT:CO_MMT + 128] = m
    tt = (p % 64)[:, None]
    ss = np.arange(64)[None, :]
    c[:, CO_ML:CO_ML + 64] = (ss < tt).astype(np.float32)
    c[:, CO_ML + 64:CO_ML + 128] = (tt < ss).astype(np.float32)
    c[:, CO_IREP:CO_IREP + 64] = (ss == tt).astype(np.float32)
    rst = np.ones(TB, np.float32)
    rst[::64] = 0.0
    c[:, CO_RST:CO_RST + TB] = rst[None, :]
    for g, w in enumerate((2, 4, 8, 16)):
        for tq in range(15):
            c[:, CO_ICNT + g * 15 + tq] = 1.0 / min(tq + 1, w)
    return c


_INFO = {}
SCHED_TRIALS = 60
SCHED_PICK = 0
SCHED_NOISE = 0.02


class _Op:
    __slots__ = ("eng", "fn", "deps", "dma", "semval", "signal", "group", "dur", "lat", "idx", "prio", "succ",
                 "npred", "ready_t", "fin", "tag", "st", "pos", "wdeps", "mode")


class _Rec:
    def __getattr__(self, name):
        def f(*a, **k):
            return (name, a, k)
        return f


def _free_size(ap):
    n = 1
    for d in list(ap.shape)[1:]:
        n *= int(d)
    return n


def _estimate(eng, fn, dma):
    name, a, k = fn(_Rec())
    out = k.get("out", a[0] if a else None)
    if dma is not None:
        nbytes = _free_size(out) * int(out.shape[0]) * 4
        return 0.06, 2.0 + nbytes / 150e3, None
    if eng == "pe":
        def r32(v):
            return 32 if v <= 32 else (64 if v <= 64 else 128)
        if name == "transpose":
            n = int(k["in_"].shape[0])
            mode = ("T", r32(int(k["in_"].shape[0])), r32(_free_size(k["in_"])))
        else:
            n = _free_size(k["rhs"])
            mode = ("M", r32(int(k["lhsT"].shape[0])), r32(_free_size(k["lhsT"])))
        d = (0.025 + 0.0006 * n) if n >= 256 else (0.03 + 0.0002 * n)
        return d, d + 0.12, mode
    n = _free_size(out)
    if eng == "act":
        d = 0.22 + 0.00075 * n
    elif eng == "dve":
        d = 0.12 + 0.00105 * n
    else:
        d = 0.2 + 0.0021 * n
    return d, d + 0.08, None


class Plan:
    ENGS = ("sync", "act", "dve", "pool", "pe")

    def __init__(self):
        self.ops = {e: [] for e in self.ENGS}
        self.lastw = {}
        self.readers = {}
        self.dma_eng = {}
        self.nops = 0

    def op(self, eng, fn, reads=(), writes=(), dma=None, group=None):
        only = getattr(self, "only", None)
        if only is not None and getattr(self, "section", None) not in only:
            return None
        if getattr(self, "in_region", False):
            if self.region_count >= self.region_budget:
                return None
            self.region_count += 1
        o = _Op()
        o.eng = eng
        o.fn = fn
        o.dma = dma
        o.deps = {}
        o.signal = False
        o.semval = None
        o.group = group
        o.dur, o.lat, o.mode = _estimate(eng, fn, dma)
        if dma is not None:
            assert self.dma_eng.setdefault(dma, eng) == eng
        for b in reads:
            w = self.lastw.get(b)
            if w is not None:
                o.deps[w] = True
        for b in writes:
            w = self.lastw.get(b)
            if w is not None and w not in o.deps:
                o.deps[w] = False
            for r in self.readers.get(b, ()):
                if r not in o.deps:
                    o.deps[r] = False
        for b in writes:
            self.lastw[b] = o
            self.readers[b] = []
        for b in reads:
            if b not in writes:
                self.readers.setdefault(b, []).append(o)
        o.idx = self.nops
        o.tag = getattr(self, "tag", None)
        self.nops += 1
        self.ops[eng].append(o)
        return o

    @staticmethod
    def _needs_wait(o, d, raw):
        if d.dma is not None or o.dma is not None:
            return True
        if d.eng != o.eng:
            return True
        if o.eng == "pe":
            return False
        return True

    def schedule(self):
        allops = []
        for e in self.ENGS:
            allops.extend(self.ops[e])
        for o in allops:
            o.succ = []
            o.npred = len(o.deps)
        for o in allops:
            for d in o.deps:
                d.succ.append(o)
        allops.sort(key=lambda o: o.idx)
        for o in reversed(allops):
            m = 0.0
            for sc in o.succ:
                if sc.prio > m:
                    m = sc.prio
            o.prio = o.lat + m
        rnd = getattr(self, "prio_rng", None)
        if rnd is not None:
            for o in allops:
                o.prio *= 1.0 + self.prio_noise * (rnd.random() - 0.5)
        free = {e: 0.0 for e in self.ENGS}
        ready = {e: [] for e in self.ENGS}
        for o in allops:
            o.ready_t = 0.0
            if o.npred == 0:
                ready[o.eng].append(o)
        order = {e: [] for e in self.ENGS}
        remaining = len(allops)
        SYNC = 0.35
        MODE_SW = 0.29
        pe_mode = None
        while remaining:
            best = None
            for e in self.ENGS:
                lst = ready[e]
                if not lst:
                    continue
                fe = free[e]
                cand = None
                cs = None
                for o in lst:
                    st = o.ready_t if o.ready_t > fe else fe
                    if e == "pe" and o.mode != pe_mode:
                        st += MODE_SW
                    key = (st, -o.prio, o.idx)
                    if cs is None or key < cs:
                        cs = key
                        cand = o
                if best is None or cs < best[0]:
                    best = (cs, cand)
            cs, o = best
            st = cs[0]
            if o.eng == "pe":
                pe_mode = o.mode
            ready[o.eng].remove(o)
            free[o.eng] = st + o.dur
            o.st = st
            o.fin = st + o.lat
            order[o.eng].append(o)
            remaining -= 1
            for sc in o.succ:
                if sc.eng == o.eng == "pe":
                    t = st + o.dur
                else:
                    t = o.fin + (SYNC if sc.eng != o.eng else 0.05)
                if t > sc.ready_t:
                    sc.ready_t = t
                sc.npred -= 1
                if sc.npred == 0:
                    ready[sc.eng].append(sc)
        self.ops = order
        self.est_makespan = max(free.values())

    def finalize(self):
        for e in self.ENGS:
            for i, o in enumerate(self.ops[e]):
                o.pos = i
        for e in self.ENGS:
            for o in self.ops[e]:
                last = {}
                o.wdeps = []
                for d, raw in o.deps.items():
                    if not self._needs_wait(o, d, raw):
                        continue
                    if d.dma is not None:
                        o.wdeps.append(d)
                        continue
                    cur = last.get(d.eng)
                    if cur is None or d.pos > cur.pos:
                        last[d.eng] = d
                for d in last.values():
                    d.signal = True
                    o.wdeps.append(d)
        self.dma_tot = {}
        groups = {}
        for e in self.ENGS:
            c = 0
            for o in self.ops[e]:
                if o.dma is not None:
                    v = self.dma_tot.get(o.dma, 0) + 16
                    self.dma_tot[o.dma] = v
                    o.semval = v
                    if o.group is not None:
                        groups.setdefault((o.dma, o.group), []).append(o)
                elif o.signal:
                    c += 1
                    o.semval = c
        for lst in groups.values():
            mx = max(x.semval for x in lst)
            for x in lst:
                x.semval = mx

    def sem_keys(self):
        return list(self.ENGS) + sorted(self.dma_tot.keys())

    def run_engine(self, e, eng, sems, final_waits=()):
        known = {}
        for o in self.ops[e]:
            waits = {}
            for d in o.wdeps:
                key = d.dma if d.dma is not None else d.eng
                if d.semval > waits.get(key, 0):
                    waits[key] = d.semval
            for key, v in waits.items():
                if known.get(key, 0) >= v:
                    continue
                eng.wait_ge(sems[key], v)
                known[key] = v
            inst = o.fn(eng)
            if o.dma is not None:
                inst.then_inc(sems[o.dma], 16)
            elif o.signal:
                inst.then_inc(sems[e], 1)
        for key in final_waits:
            eng.wait_ge(sems[key], self.dma_tot[key])


def build(debug=None, nblocks=NB, upto=9, cbud=10 ** 9, do_schedule=True):
    nc = bass.Bass("TRN2", target_bir_lowering=False)
    dt = nc.dram_tensor
    x = dt("x", [S, D], F32, kind="ExternalInput").ap()
    norm_gain = dt("norm_gain", [D], F32, kind="ExternalInput").ap()
    w_in = dt("w_in", [D, DIN], F32, kind="ExternalInput").ap()
    pool_w = dt("pool_w", [4, 128, 128], F32, kind="ExternalInput").ap()
    pool_scale = dt("pool_scale", [512], F32, kind="ExternalInput").ap()
    shift_mu = dt("shift_mu", [1664], F32, kind="ExternalInput").ap()
    w0 = dt("w0", [512], F32, kind="ExternalInput").ap()
    w_up = dt("w_up", [64, 512], F32, kind="ExternalInput").ap()
    a0 = dt("a0", [512], F32, kind="ExternalInput").ap()
    a_up = dt("a_up", [64, 512], F32, kind="ExternalInput").ap()
    k_k = dt("k_k", [512], F32, kind="ExternalInput").ap()
    k_a = dt("k_a", [512], F32, kind="ExternalInput").ap()
    r_k = dt("r_k", [512], F32, kind="ExternalInput").ap()
    gn_gain = dt("gn_gain", [512], F32, kind="ExternalInput").ap()
    gn_bias = dt("gn_bias", [512], F32, kind="ExternalInput").ap()
    w_out = dt("w_out", [D, D], F32, kind="ExternalInput").ap()
    final_gain = dt("final_gain", [D], F32, kind="ExternalInput").ap()
    cst_d = dt("cst", [128, CST_W], F32, kind="ExternalInput").ap()
    out = dt("out", [S, D], F32, kind="ExternalOutput").ap()
    dbg_out = {}
    if debug:
        for name, shp in debug.items():
            dbg_out[name] = dt("dbg_" + name, list(shp), F32, kind="ExternalOutput").ap()

    P = Plan()
    P.region_budget = cbud
    P.region_count = 0
    es = contextlib.ExitStack()
    with es:
        def sb(name, shape, dtype):
            return es.enter_context(nc.sbuf_tensor(name, list(shape), dtype))

        def psum(name, shape, dtype):
            return es.enter_context(nc.psum_tensor(name, list(shape), dtype))

        winb = sb("winb", [128, 8, DIN], BF16)
        woutb = sb("woutb", [128, 8, D], BF16)
        cst = sb("cst_sb", [128, CST_W], F32)
        poolw = sb("poolw", [128, 4, 128], BF16)
        lup = sb("lup", [128, 512], BF16)
        identb = sb("identb", [128, 128], BF16)
        bones = sb("bones", [128, 128], BF16)
        blkrk = sb("blkrk", [128, 128], BF16)
        fgain = sb("fgain", [128, D], F32)
        ngv = sb("ngv", [128, 8], F32)
        muv = sb("muv", [128, 13], F32)
        w0v = sb("w0v", [128, 4], F32)
        a0v = sb("a0v", [128, 4], F32)
        kkv = sb("kkv", [128, 4], F32)
        kav = sb("kav", [128, 4], F32)
        omka = sb("omka", [128, 4], F32)
        rkv = sb("rkv", [128, 4], F32)
        gng = sb("gng", [128, 4], F32)
        gnb = sb("gnb", [128, 4], F32)
        pscv = sb("pscv", [128, 4], F32)
        halo = sb("halo", [128, 13], F32)

        xt = [sb(f"xt{i}", [128, D], F32) for i in range(2)]
        hbf = sb("hbf", [128, D], BF16)
        junk = sb("junk", [128, D], BF16)
        ssx = sb("ssx", [128, 2 * S // 128], F32)
        sdx = sb("sdx", [128, 2 * S // 128], F32)
        rsx = sb("rsx", [128, 2 * S // 128], F32)
        hT2 = [sb(f"hT{i}", [128, 8, TB], BF16) for i in range(2)]

        NZS = 5
        zs = [sb(f"zs{i}", [128, TB], F32) for i in range(NZS)]
        dtmp = sb("dtmp", [128, TB], F32)
        UW = 15 + TB
        ua = [sb(f"ua{i}", [128, UW], F32) for i in range(4)]
        pA = sb("pA", [128, UW], F32)
        pB = sb("pB", [128, UW], F32)
        pooled = sb("pooled", [128, TB], BF16)
        sga = sb("sga", [128, 4, TB], BF16)
        sgb2 = [sb(f"sgb{i}", [128, 4, TB], BF16) for i in range(2)]
        lobf = sb("lobf", [128, TB], BF16)
        thg = sb("thg", [128, TB], F32)

        t_sg = sb("t_sg", [128, TB], F32)
        t_a = sb("t_a", [128, TB], F32)
        t_csg = sb("t_csg", [128, TB], F32)
        t_e2 = sb("t_e2", [128, TB], F32)
        t_e1 = sb("t_e1", [128, TB], F32)
        t_e3 = sb("t_e3", [128, TB], F32)
        t_e4 = sb("t_e4", [128, TB], F32)
        t_kkn = sb("t_kkn", [128, TB], F32)
        t_rn = sb("t_rn", [128, TB], F32)
        t_kp = sb("t_kp", [128, TB], F32)
        t_b = sb("t_b", [128, TB], F32)
        kk2bf = sb("kk2bf", [128, TB], BF16)
        rkbf = sb("rkbf", [128, TB], BF16)

        QT2 = [sb(f"QT{i}", [128, 8, NCH, 128], BF16) for i in range(2)]
        PT2 = [sb(f"PT{i}", [128, 4, NCH, 128], BF16) for i in range(2)]
        PPsrc = sb("PPsrc", [128, NCH, 128], BF16)
        vbf = sb("vbf", [128, TB], BF16)
        PPt2 = [sb(f"PPt{i}", [128, NCH, 4, 128], BF16) for i in range(2)]
        Zt2 = [sb(f"Zt{i}", [128, NCH, 4, 128], BF16) for i in range(2)]
        bon2 = [sb(f"bon{i}", [128, 4, TB], BF16) for i in range(2)]
        gam2 = [sb(f"gam{i}", [128, 4, NCH], F32) for i in range(2)]

        MTs = sb("MTs", [128, 8, 128], BF16)
        MM = [sb(f"MM{i}", [128, 2, 4, 64], BF16) for i in range(2)]
        PTch = [sb(f"PTch{i}", [128, 4, 64], BF16) for i in range(4)]
        GXs = sb("GXs", [128, 4, 64], BF16)
        Xs = sb("Xs", [128, 4, 64], F32)
        Sst = sb("Sst", [128, 4, 64], F32)
        Sdec = sb("Sdec", [128, 4, 64], F32)
        Sbf = [sb(f"Sbf{i}", [128, 4, 64], BF16) for i in range(2)]
        ycp = sb("ycp", [128, 512], F32)
        ysq = sb("ysq", [128, 512], F32)
        ynb = [sb(f"ynb{i}", [128, 512], BF16) for i in range(NTT)]
        st1 = sb("st1", [128, 8], F32)
        st2 = sb("st2", [128, 8], F32)
        stm = sb("stm", [128, 8], F32)
        stv = sb("stv", [128, 8], F32)
        t1 = sb("t1", [128, TB], F32)
        ycat = sb("ycat", [128, 8, TB], BF16)
        xe = [sb(f"xe{i}", [128, D], F32) for i in range(2)]

        psNM = [psum(f"psNM{i}", [128, 512], F32) for i in range(2)]
        psT = psum("psT", [128, 512], F32)
        psSm = psum("psSm", [128, 512], F32)
        psM1 = psum("psM1", [128, 512], F32)
        psP = psum("psP", [128, 512], F32)
        psG = psum("psG", [128, 512], F32)
        psY = psum("psY", [128, 512], F32)
        psTb = psT[:, :].bitcast(BF16)
        psYb = psY[:, :].bitcast(BF16)
        psM1b = psM1[:, :].bitcast(BF16)

        dbg_dumps = []

        def dump(name, ap, key):
            if debug and name in dbg_out:
                dbg_dumps.append((name, ap, key))

        def dma(eng, out_ap, in_ap, key, reads=(), writes=(), group=None, slow=False):
            if slow:
                fn = lambda e, o=out_ap, i=in_ap: e.dma_start(out=o, in_=i, allow_slow_non_contiguous=True)
            else:
                fn = lambda e, o=out_ap, i=in_ap: e.dma_start(out=o, in_=i)
            return P.op(eng, fn, reads=reads, writes=writes, dma=key, group=group)

        dma("sync", cst[:, :], cst_d[:, :], "dc", writes=["cst"], group=0)

        def vec_load(tile, src, n, key):
            dma("sync", tile[:, :], src.rearrange("(j p) -> p j", p=128), "dc", writes=[key], group=0, slow=True)

        vec_load(ngv, norm_gain, 8, "ngv")
        vec_load(muv, shift_mu, 13, "muv")
        vec_load(w0v, w0, 4, "w0v")
        vec_load(a0v, a0, 4, "a0v")
        vec_load(kkv, k_k, 4, "kkv")
        vec_load(kav, k_a, 4, "kav")
        vec_load(rkv, r_k, 4, "rkv")
        vec_load(gng, gn_gain, 4, "gng")
        vec_load(gnb, gn_bias, 4, "gnb")
        vec_load(pscv, pool_scale, 4, "pscv")
        dma("sync", fgain[:, :], final_gain.partition_broadcast(128), "dc", writes=["fgain"], group=0)

        for kc in range(8):
            dma("pool", woutb[:, kc, :], w_out[kc * 128:(kc + 1) * 128, :], "dw", writes=[("woutb", kc)], group=0)
        dma("pool", poolw[:, :, :], pool_w.rearrange("g c d -> c g d"), "dw", writes=["poolw"], group=0)
        dma("pool", lup[0:64, :], w_up[:, :], "dw", writes=[("lup", 0)], group=0)
        dma("pool", lup[64:128, :], a_up[:, :], "dw", writes=[("lup", 1)], group=0)

        P.op("pool", lambda e: e.memset(halo[:, :], 0.0), writes=["halo"])
        P.op("pool", lambda e: e.memset(Sst[:, :, :], 0.0), writes=["Sst"])
        P.op("pool", lambda e: e.memset(Sbf[0][:, :, :], 0.0), writes=["Sbf0"])
        for g in range(4):
            P.op("pool", lambda e, g=g: e.memset(ua[g][:, 0:15], 0.0), writes=[("ua", g)])
        P.op("dve", lambda e: e.tensor_copy(out=identb[:, :], in_=cst[:, CO_ID:CO_ID + 128]),
             reads=["cst"], writes=["identb"])
        P.op("dve", lambda e: e.tensor_copy(out=bones[:, :], in_=cst[:, CO_BONES:CO_BONES + 128]),
             reads=["cst"], writes=["bones"])
        neghalf = sb("neghalf", [128, 16], F32)
        omm = sb("omm", [128, 13], F32)
        P.op("pool", lambda e: e.memset(neghalf[:, :], -0.5), writes=["neghalf"])
        half = sb("half", [128, 1], F32)
        P.op("pool", lambda e: e.memset(half[:, :], 0.5), writes=["half"])
        P.op("dve", lambda e: e.tensor_scalar(out=omm[:, :], in0=muv[:, :], scalar1=-1.0, scalar2=1.0,
                                              op0=ALU.mult, op1=ALU.add),
             reads=["muv"], writes=["omm"])
        for vt, vk in ((w0v, "w0v"), (a0v, "a0v"), (pscv, "pscv"), (gng, "gng"), (gnb, "gnb"), (rkv, "rkv")):
            P.op("dve", lambda e, vt=vt: e.tensor_scalar(out=vt[:, :], in0=vt[:, :], scalar1=0.5, scalar2=None,
                                                         op0=ALU.mult),
                 reads=[vk], writes=[vk])
        P.op("dve", lambda e: e.tensor_scalar(out=omka[:, :], in0=kav[:, :], scalar1=-1.0, scalar2=1.0,
                                              op0=ALU.mult, op1=ALU.add),
             reads=["kav"], writes=["omka"])

        cast_engs = ("dve", "act", "dve", "act", "pool")
        qstage = QT2[1][:, :, :, :].rearrange("p h c t -> p (h c t)").bitcast(F32)
        stg = [(qstage[:, 0:WST], [("QT", 1, 0), ("QT", 1, 1)], "dq0"),
               (qstage[:, 1024:1024 + WST], [("QT", 1, 2), ("QT", 1, 3)], "dq1"),
               (xe[0][:, 0:WST], [("xe", 0)], "de0"), (xe[1][:, 0:WST], [("xe", 1)], "de1")]
        n = 0
        for pc in (3, 1, 2, 0):
            for kc in range(8):
                sidx = n % 4
                stg_t, stg_key, stg_dk = stg[sidx]
                dma("sync", stg_t, w_in[kc * 128:(kc + 1) * 128, pc * WST:(pc + 1) * WST],
                    stg_dk, writes=stg_key)
                ce = cast_engs[n % 5]
                o_ap = winb[:, kc, pc * WST:(pc + 1) * WST]
                i_ap = stg_t
                g_ap = ngv[:, kc:kc + 1]
                if ce == "act":
                    fn = lambda e, o=o_ap, i=i_ap, g=g_ap: e.activation(out=o, in_=i, func=AF.Copy, scale=g)
                elif ce == "pool":
                    fn = lambda e, o=o_ap, i=i_ap, g=g_ap: e.tensor_scalar(out=o, in0=i, scalar1=g, scalar2=0.0,
                                                                          op0=ALU.mult, op1=ALU.add)
                else:
                    fn = lambda e, o=o_ap, i=i_ap, g=g_ap: e.tensor_scalar(out=o, in0=i, scalar1=g, scalar2=None,
                                                                          op0=ALU.mult)
                P.op(ce, fn, reads=stg_key + ["ngv"], writes=[("winb", kc, pc)])
                n += 1

        for par in range(2):
            P.op("dve", lambda e, par=par: e.memset(QT2[par][:, :, :, :].rearrange("p h c t -> p (h c t)"), 0.0),
                 writes=[("QT", par, j) for j in range(4)])
        blkrk4 = sb("blkrk4", [128, 4, 128], BF16)
        for j in range(4):
            P.op("dve", lambda e, j=j: e.tensor_scalar(out=blkrk4[:, j, :], in0=cst[:, CO_BONES:CO_BONES + 128],
                                                       scalar1=rkv[:, j:j + 1], scalar2=None, op0=ALU.mult),
                 reads=["cst", "rkv"], writes=["blkrk4"])

        FT_ORDER = [20]
        for j in range(4):
            FT_ORDER += [12 + j, 8 + j, 16 + j, 21 + j]
        for g in range(4):
            FT_ORDER += [g, 4 + g]
        zrot = [0]
        psz_rot = [0]
        sm_rot = [0]
        xrot = [0]
        erot = [0]

        def sm_half():
            h = sm_rot[0] % 2
            sm_rot[0] += 1
            return psT[:, h * 256:(h + 1) * 256], "psT"

        def block_body(tb):
            t0 = tb * TB
            par = tb % 2
            QT, PT, PPt, Zt = QT2[par], PT2[par], PPt2[par], Zt2[par]
            bon, gam, sgb = bon2[par], gam2[par], sgb2[par]
            hT = hT2[par]
            khT = "hT%d" % par
            P.tag = (tb, "A")
            P.section = "A"
            for i in range(NTT):
                col = tb * NTT + i
                xi = xrot[0] % 2
                xrot[0] += 1
                r0 = t0 + i * 128
                dma("sync", xt[xi][:, :], x[r0:r0 + 128, :], f"dx{xi}", writes=[("xt", xi)])
                P.op("act", lambda e, xi=xi, col=col: e.activation(out=hbf[:, :], in_=xt[xi][:, :], func=AF.Square,
                                                                   accum_out=ssx[:, col:col + 1]),
                     reads=[("xt", xi)], writes=["hbf", ("ssx", col)])
                P.op("pool", lambda e, col=col: e.tensor_scalar(out=sdx[:, col:col + 1], in0=ssx[:, col:col + 1],
                                                                scalar1=1.0 / D, scalar2=NORM_EPS,
                                                                op0=ALU.mult, op1=ALU.add),
                     reads=[("ssx", col)], writes=[("sdx", col)])
                P.op("pool", lambda e, col=col: e.tensor_tensor(out=rsx[:, col:col + 1], in0=sdx[:, col:col + 1],
                                                                in1=neghalf[:, 0:1], op=ALU.pow),
                     reads=[("sdx", col), "neghalf"], writes=[("rsx", col)])
                P.op("dve", lambda e, xi=xi, col=col: e.tensor_scalar(out=hbf[:, :], in0=xt[xi][:, :],
                                                                      scalar1=rsx[:, col:col + 1], scalar2=None,
                                                                      op0=ALU.mult),
                     reads=[("xt", xi), ("rsx", col)], writes=["hbf"])
                for kc in range(8):
                    P.op("pe", lambda e, kc=kc: e.transpose(out=psM1b[:, kc * 128:(kc + 1) * 128],
                                                            in_=hbf[:, kc * 128:(kc + 1) * 128],
                                                            identity=identb[:, :]),
                         reads=["hbf", "identb"], writes=["psM1"])
                P.op("dve", lambda e, i=i: e.tensor_copy(
                    out=hT[:, :, i * 128:(i + 1) * 128],
                    in_=psM1b.rearrange("p (k t) -> p k t", t=128)),
                    reads=["psM1"], writes=[khT])

            if upto < 2:
                return
            P.tag = (tb, "B")
            P.section = "B"
            zmap = {}
            for ft in FT_ORDER:
                pz = psz_rot[0] % 2
                psz_rot[0] += 1
                zp = psP[:, 0:TB]
                for kc in range(8):
                    P.op("pe", lambda e, zp=zp, kc=kc, ft=ft: e.matmul(
                        zp, lhsT=winb[:, kc, ft * 128:(ft + 1) * 128], rhs=hT[:, kc, :],
                        start=(kc == 0), stop=(kc == 7)),
                        reads=[("winb", kc, q) for q in sorted({(ft * 128) // WST, (ft * 128 + 127) // WST})] + [khT],
                        writes=["psP"])
                if 8 <= ft <= 20:
                    zi = zrot[0] % NZS
                    zrot[0] += 1
                    zmap[ft] = zi
                    js = ft - 8
                    zz = zs[zi]
                    P.op("act", lambda e, zz=zz, zp=zp: e.activation(out=zz[:, :], in_=zp, func=AF.Copy),
                         reads=["psP"], writes=[("zs", zi)])
                    if upto < 2.2:
                        continue
                    P.op("act", lambda e, zz=zz, js=js: e.activation(out=dtmp[:, 1:TB], in_=zz[:, 0:TB - 1],
                                                                     func=AF.Copy, scale=muv[:, js:js + 1]),
                         reads=[("zs", zi), "muv"], writes=["dtmp"])
                    P.op("pool", lambda e, js=js: e.tensor_scalar(out=dtmp[:, 0:1], in0=halo[:, js:js + 1],
                                                                  scalar1=muv[:, js:js + 1], scalar2=0.0,
                                                                  op0=ALU.mult, op1=ALU.add),
                         reads=[("halo", js), "muv", "dtmp"], writes=["dtmp"])
                    P.op("pool", lambda e, zz=zz, js=js: e.tensor_copy(out=halo[:, js:js + 1], in_=zz[:, TB - 1:TB]),
                         reads=[("zs", zi)], writes=[("halo", js)])
                    P.op("dve", lambda e, zz=zz, js=js: e.scalar_tensor_tensor(
                        out=zz[:, :], in0=zz[:, :], scalar=omm[:, js:js + 1], in1=dtmp[:, :],
                        op0=ALU.mult, op1=ALU.add),
                        reads=[("zs", zi), "dtmp", "omm"], writes=[("zs", zi)])
                    if tb == 0 and debug and ("sh%d" % ft) in dbg_out:
                        dump("sh%d" % ft, zz[:, :], ("zs", zi))
                elif ft >= 21:
                    j = ft - 21
                    P.op("act", lambda e, zp=zp: e.activation(out=thg[:, :], in_=zp, func=AF.Tanh, scale=0.5),
                         reads=["psP"], writes=["thg"])
                    P.op("dve", lambda e, j=j, zp=zp: e.scalar_tensor_tensor(
                        out=sgb[:, j, :], in0=thg[:, :], scalar=1.0, in1=zp, op0=ALU.add, op1=ALU.mult),
                        reads=["psP", "thg"], writes=[("sgb", par, j)])
                elif ft >= 4:
                    g = ft - 4
                    P.op("act", lambda e, zp=zp: e.activation(out=thg[:, :], in_=zp, func=AF.Tanh, scale=0.5),
                         reads=["psP"], writes=["thg"])
                    P.op("dve", lambda e, g=g, zp=zp: e.scalar_tensor_tensor(
                        out=sga[:, g, :], in0=thg[:, :], scalar=1.0, in1=zp, op0=ALU.add, op1=ALU.mult),
                        reads=["psP", "thg"], writes=[("sga", g)])
                else:
                    g = ft
                    P.op("act", lambda e, g=g, zp=zp: e.activation(out=ua[g][:, 15:UW], in_=zp, func=AF.Copy),
                         reads=["psP"], writes=[("ua", g)])

                if upto < 2.4:
                    continue
                if ft == 20:
                    zz = zs[zmap[20]]
                    P.op("act", lambda e, zz=zz: e.activation(out=lobf[0:64, :], in_=zz[0:64, :], func=AF.Tanh),
                         reads=[("zs", zmap[20])], writes=["lobf"])
                    P.op("act", lambda e, zz=zz: e.activation(out=lobf[64:128, :], in_=zz[64:128, :], func=AF.Copy),
                         reads=[("zs", zmap[20])], writes=["lobf"])
                if 16 <= ft <= 19:
                    j = ft - 16
                    zk = zs[zmap[12 + j]]
                    zr = zs[zmap[8 + j]]
                    zv = zs[zmap[16 + j]]
                    kK, kR, kV = ("zs", zmap[12 + j]), ("zs", zmap[8 + j]), ("zs", zmap[16 + j])
                    pw, pwk = sm_half()
                    P.op("pe", lambda e, pw=pw, j=j: e.matmul(pw, lhsT=lup[0:64, j * 128:(j + 1) * 128],
                                                              rhs=lobf[0:64, :], start=True, stop=True),
                         reads=[("lup", 0), "lobf"], writes=[pwk])
                    P.op("act", lambda e, pw=pw, j=j: e.activation(out=t_sg[:, :], in_=pw, func=AF.Tanh,
                                                                   bias=w0v[:, j:j + 1], scale=0.5),
                         reads=[pwk, "w0v"], writes=["t_sg"])
                    P.op("act", lambda e: e.activation(out=t_sg[:, :], in_=t_sg[:, :], func=AF.Identity,
                                                       scale=0.5, bias=half[:, 0:1]),
                         reads=["t_sg", "half"], writes=["t_sg"])
                    pa, pak = sm_half()
                    P.op("pe", lambda e, pa=pa, j=j: e.matmul(pa, lhsT=lup[64:128, j * 128:(j + 1) * 128],
                                                              rhs=lobf[64:128, :], start=True, stop=True),
                         reads=[("lup", 1), "lobf"], writes=[pak])
                    P.op("act", lambda e, pa=pa, j=j: e.activation(out=t_a[:, :], in_=pa, func=AF.Tanh,
                                                                   bias=a0v[:, j:j + 1], scale=0.5),
                         reads=[pak, "a0v"], writes=["t_a"])
                    P.op("act", lambda e: e.activation(out=t_a[:, :], in_=t_a[:, :], func=AF.Identity,
                                                       scale=0.5, bias=half[:, 0:1]),
                         reads=["t_a", "half"], writes=["t_a"])
                    P.op("dve", lambda e: e.tensor_tensor_scan(out=t_csg[:, :], data0=cst[:, CO_RST:CO_RST + TB],
                                                               data1=t_sg[:, :], initial=0.0,
                                                               op0=ALU.mult, op1=ALU.add),
                         reads=["cst", "t_sg"], writes=["t_csg"])
                    P.op("pool", lambda e: e.tensor_tensor(out=t_e2[:, :], in0=t_csg[:, :], in1=t_sg[:, :],
                                                           op=ALU.subtract),
                         reads=["t_csg", "t_sg"], writes=["t_e2"])
                    P.op("act", lambda e: e.activation(out=t_e2[:, :], in_=t_e2[:, :], func=AF.Exp, scale=-C0),
                         reads=["t_e2"], writes=["t_e2"])
                    P.op("act", lambda e: e.activation(out=t_e1[:, :], in_=t_csg[:, :], func=AF.Exp, scale=-C0),
                         reads=["t_csg"], writes=["t_e1"])
                    P.op("act", lambda e: e.activation(out=t_e3[:, :], in_=t_csg[:, :], func=AF.Exp, scale=C0),
                         reads=["t_csg"], writes=["t_e3"])
                    e1v = t_e1[:, :].rearrange("p (c t) -> p c t", t=64)
                    P.op("pool", lambda e, j=j, e1v=e1v: e.tensor_copy(out=gam[:, j, :], in_=e1v[:, :, 63]),
                         reads=["t_e1"], writes=[("gam", par, j)])
                    P.op("pool", lambda e, j=j: e.tensor_tensor(
                        out=t_e4[:, :].rearrange("p (c t) -> p c t", t=64),
                        in0=t_e3[:, :].rearrange("p (c t) -> p c t", t=64),
                        in1=gam[:, j, :].unsqueeze(2).to_broadcast([128, NCH, 64]), op=ALU.mult),
                        reads=["t_e3", ("gam", par, j)], writes=["t_e4"])
                    P.op("act", lambda e, zk=zk, j=j: e.activation(out=kk2bf[:, :], in_=zk[:, :], func=AF.Square,
                                                                   scale=kkv[:, j:j + 1]),
                         reads=[kK, "kkv"], writes=["kk2bf"])
                    pss, pssk = sm_half()
                    P.op("pe", lambda e, pss=pss: e.matmul(pss, lhsT=bones[:, :], rhs=kk2bf[:, :], start=True,
                                                           stop=True),
                         reads=["bones", "kk2bf"], writes=[pssk])
                    P.op("dve", lambda e, pss=pss: e.tensor_scalar(out=t_rn[:, :], in0=pss, scalar1=1e-18,
                                                                   scalar2=None, op0=ALU.max),
                         reads=[pssk], writes=["t_rn"])
                    P.op("act", lambda e: e.activation(out=t_rn[:, :], in_=t_rn[:, :], func=AF.Ln),
                         reads=["t_rn"], writes=["t_rn"])
                    P.op("act", lambda e: e.activation(out=t_rn[:, :], in_=t_rn[:, :], func=AF.Exp, scale=-0.5),
                         reads=["t_rn"], writes=["t_rn"])
                    P.op("dve", lambda e, zk=zk, j=j: e.scalar_tensor_tensor(
                        out=t_kkn[:, :], in0=zk[:, :], scalar=kkv[:, j:j + 1], in1=t_rn[:, :],
                        op0=ALU.mult, op1=ALU.mult),
                        reads=[kK, "kkv", "t_rn"], writes=["t_kkn"])
                    P.op("act", lambda e, j=j: e.activation(out=t_kp[:, :], in_=t_a[:, :], func=AF.Identity,
                                                            scale=kav[:, j:j + 1], bias=omka[:, j:j + 1]),
                         reads=["t_a", "kav", "omka"], writes=["t_kp"])
                    P.op("pool", lambda e, zk=zk: e.tensor_tensor(out=t_kp[:, :], in0=t_kp[:, :], in1=zk[:, :],
                                                                  op=ALU.mult),
                         reads=["t_kp", kK], writes=["t_kp"])
                    P.op("pool", lambda e: e.tensor_tensor(out=t_b[:, :], in0=t_kkn[:, :], in1=t_a[:, :],
                                                           op=ALU.mult),
                         reads=["t_kkn", "t_a"], writes=["t_b"])

                    def v3(t):
                        return t[:, :].rearrange("p (c t) -> p c t", t=64)

                    prods = [
                        ("dve", PT[:, j, :, 0:64], t_kp, "t_e3", t_e3, "t_kp", ("PT", par, j)),
                        ("pool", PT[:, j, :, 64:128], t_b, "t_e3", t_e3, "t_b", ("PT", par, j)),
                        ("dve", PPsrc[:, :, 0:64], t_kp, "t_e4", t_e4, "t_kp", "PPsrc"),
                        ("pool", PPsrc[:, :, 64:128], t_b, "t_e4", t_e4, "t_b", "PPsrc"),
                    ]
                    for e2 in range(2):
                        rs = slice(64 * e2, 64 * e2 + 64)
                        P.op("dve", lambda e, rs=rs, e2=e2, j=j, zr=zr: e.tensor_tensor(
                            out=QT[rs, 2 * j + e2, :, 0:64], in0=v3(zr)[rs], in1=v3(t_e1)[rs], op=ALU.mult),
                            reads=[kR, "t_e1"], writes=[("QT", par, j)])
                        P.op("pool", lambda e, rs=rs, e2=e2, j=j: e.tensor_tensor(
                            out=QT[rs, 2 * j + e2, :, 64:128], in0=v3(t_kkn)[rs], in1=v3(t_e2)[rs], op=ALU.mult),
                            reads=["t_kkn", "t_e2"], writes=[("QT", par, j)])
                    for (en, o_ap, a_t, ek, e_t, akey, okey) in prods:
                        P.op(en, lambda e, o_ap=o_ap, a_t=a_t, e_t=e_t: e.tensor_tensor(
                            out=o_ap, in0=v3(a_t), in1=v3(e_t), op=ALU.mult),
                            reads=[akey, ek], writes=[okey])
                    P.op("dve", lambda e, zr=zr: e.tensor_tensor(out=rkbf[:, :], in0=zr[:, :], in1=t_kp[:, :],
                                                                 op=ALU.mult),
                         reads=[kR, "t_kp"], writes=["rkbf"])
                    P.op("act", lambda e, zv=zv: e.activation(out=vbf[:, :], in_=zv[:, :], func=AF.Copy),
                         reads=[kV], writes=["vbf"])
                    pbo, pbok = sm_half()
                    P.op("pe", lambda e, pbo=pbo, j=j: e.matmul(pbo, lhsT=blkrk4[:, j, :], rhs=rkbf[:, :],
                                                                start=True, stop=True),
                         reads=["blkrk4", "rkbf"], writes=[pbok])
                    P.op("dve", lambda e, pbo=pbo, zv=zv, j=j: e.tensor_tensor(out=bon[:, j, :], in0=pbo,
                                                                               in1=zv[:, :], op=ALU.mult),
                         reads=[pbok, kV], writes=[("bon", par, j)])
                    for c in range(NCH):
                        P.op("pe", lambda e, c=c: e.transpose(out=psTb[:, c * 128:(c + 1) * 128],
                                                              in_=PPsrc[:, c, :], identity=identb[:, :]),
                             reads=["PPsrc", "identb"], writes=["psT"])
                    for c in range(NCH):
                        P.op("pe", lambda e, c=c: e.transpose(out=psTb[0:64, 512 + c * 128:512 + (c + 1) * 128],
                                                              in_=vbf[:, c * 64:(c + 1) * 64],
                                                              identity=identb[:, :]),
                             reads=["vbf", "identb"], writes=["psT"])
                    P.op("act", lambda e, j=j: e.activation(
                        out=PPt[:, :, j, :], in_=psTb[:, 0:512].rearrange("p (c f) -> p c f", f=128),
                        func=AF.Copy),
                        reads=["psT"], writes=[("PPt", par, j)])
                    P.op("act", lambda e, j=j: e.activation(
                        out=Zt[0:64, :, j, :], in_=psTb[0:64, 512:1024].rearrange("p (c f) -> p c f", f=128),
                        func=AF.Copy),
                        reads=["psT"], writes=[("ZtV", par, j)])
                P.in_region = False
                if upto < 2.6:
                    continue
                if 4 <= ft <= 7:
                    g = ft - 4
                    w = 2 << g
                    u = ua[g]
                    src = u
                    bufs = [pA, pB]
                    for lvl in range(g + 1):
                        sh = 1 << lvl
                        dst = bufs[lvl % 2]
                        lo = 2 * sh - 1
                        P.op("pool", lambda e, src=src, dst=dst, sh=sh, lo=lo: e.tensor_tensor(
                            out=dst[:, lo:UW], in0=src[:, lo:UW], in1=src[:, lo - sh:UW - sh], op=ALU.add),
                            reads=[("ua", g), "pA", "pB"], writes=["pA" if dst is pA else "pB"])
                        src = dst
                    skey = "pA" if src is pA else "pB"
                    P.op("dve", lambda e, src=src, u=u, w=w: e.scalar_tensor_tensor(
                        out=pooled[:, :], in0=src[:, 15:UW], scalar=1.0 / w, in1=u[:, 15:UW],
                        op0=ALU.mult, op1=ALU.subtract),
                        reads=[skey, ("ua", g)], writes=["pooled"])
                    if tb == 0:
                        nfix = w - 1
                        P.op("dve", lambda e, src=src, g=g, nfix=nfix: e.tensor_tensor(
                            out=t1[:, 0:nfix], in0=src[:, 15:15 + nfix],
                            in1=cst[:, CO_ICNT + g * 15:CO_ICNT + g * 15 + nfix], op=ALU.mult),
                            reads=[skey, "cst"], writes=["t1"])
                        P.op("dve", lambda e, u=u, nfix=nfix: e.tensor_tensor(
                            out=pooled[:, 0:nfix], in0=t1[:, 0:nfix], in1=u[:, 15:15 + nfix], op=ALU.subtract),
                            reads=["t1", ("ua", g), "pooled"], writes=["pooled"])
                    P.op("pool", lambda e, u=u: e.tensor_copy(out=u[:, 0:15], in_=u[:, TB:TB + 15]),
                         reads=[("ua", g)], writes=[("ua", g)])
                    pm, pmk = sm_half()
                    P.op("pe", lambda e, pm=pm, g=g: e.matmul(pm, lhsT=poolw[:, g, :], rhs=pooled[:, :],
                                                              start=True, stop=True),
                         reads=["poolw", "pooled"], writes=[pmk])
                    P.op("dve", lambda e, pm=pm, g=g: e.scalar_tensor_tensor(
                        out=ycat[:, g, :], in0=pm, scalar=pscv[:, g:g + 1], in1=sga[:, g, :],
                        op0=ALU.mult, op1=ALU.mult),
                        reads=[pmk, "pscv", ("sga", g)], writes=[("ycat", g)])

            if tb == 0:
                dump("QT", None, None)

            if upto < 3:
                return
            P.in_region = True
            P.region_count = 0
            P.tag = (tb, "D")
            P.section = "D"
            mmt4 = cst[:, CO_MMT:CO_MMT + 128].unsqueeze(1).to_broadcast([128, 4, 128])
            ml2 = cst[:, CO_ML:CO_ML + 128].rearrange("p (m t) -> p m t", t=64).unsqueeze(2).to_broadcast(
                [128, 2, 4, 64])
            irep4 = cst[:, CO_IREP:CO_IREP + 64].unsqueeze(1).to_broadcast([128, 4, 64])
            for c in range(NCH):
                cg = tb * NCH + c
                sb_old = Sbf[cg % 2]
                sb_new = Sbf[(cg + 1) % 2]
                kold = "Sbf%d" % (cg % 2)
                knew = "Sbf%d" % ((cg + 1) % 2)
                qk = [("QT", par, j) for j in range(4)]
                pk = [("PT", par, j) for j in range(4)]
                for rnd in range(2):
                    for h in range(4 * rnd, 4 * rnd + 4):
                        j = h // 2
                        P.op("pe", lambda e, h=h, j=j, c=c: e.matmul(
                            psM1[:, (h % 4) * 128:(h % 4 + 1) * 128], lhsT=PT[:, j, c, :], rhs=QT[:, h, c, :],
                            start=True, stop=True),
                            reads=[("QT", par, j), ("PT", par, j)], writes=["psM1"])
                    P.op("dve", lambda e, rnd=rnd: e.tensor_tensor(
                        out=MTs[:, 4 * rnd:4 * rnd + 4, :], in0=psM1[:, :].rearrange("p (h t) -> p h t", t=128),
                        in1=mmt4, op=ALU.mult),
                        reads=["psM1", "cst"], writes=["MTs"])
                HORD = (0, 4, 1, 5, 2, 6, 3, 7)
                pbase = 2 * (cg % 2)

                def hq(h):
                    return slice(64 * (h // 4), 64 * (h // 4) + 64), h % 4

                for h in HORD:
                    j = h // 2
                    ps_, hh = hq(h)
                    P.op("pe", lambda e, h=h, j=j, c=c, ps_=ps_, hh=hh: e.matmul(
                        psG[ps_, hh * 64:(hh + 1) * 64], lhsT=QT[:, h, c, 64:128],
                        rhs=PT[:, j, c, 64:128], start=True, stop=True),
                        reads=[("QT", par, j), ("PT", par, j)], writes=["psG"])
                    P.op("pe", lambda e, h=h, j=j, c=c, ps_=ps_, hh=hh: e.matmul(
                        psG[ps_, 256 + hh * 64:256 + (hh + 1) * 64], lhsT=PT[:, j, c, 64:128],
                        rhs=QT[:, h, c, 64:128], start=True, stop=True),
                        reads=[("QT", par, j), ("PT", par, j)], writes=["psG"])
                P.op("dve", lambda e: e.tensor_tensor(
                    out=MM[0][:, :, :, :], in0=psG[:, :].rearrange("p (m h t) -> p m h t", m=2, t=64),
                    in1=ml2, op=ALU.mult),
                    reads=["psG", "cst"], writes=["MM0"])
                P.op("pool", lambda e, pbase=pbase: e.tensor_tensor(
                    out=PTch[pbase][:, :, :], in0=irep4, in1=MM[0][:, 1, :, :], op=ALU.subtract),
                    reads=["cst", "MM0"], writes=["PTch%d" % pbase])
                M_cur, P_cur = 0, 0
                for lvl in range(1, 6):
                    mi = MM[M_cur]
                    mo = MM[1 - M_cur]
                    kmi, kmo = "MM%d" % M_cur, "MM%d" % (1 - M_cur)
                    for h in HORD:
                        ps_, hh = hq(h)
                        P.op("pe", lambda e, ps_=ps_, hh=hh, mi=mi: e.matmul(
                            psNM[0][ps_, hh * 64:(hh + 1) * 64], lhsT=mi[ps_, 1, hh, :], rhs=mi[ps_, 0, hh, :],
                            start=True, stop=True),
                            reads=[kmi], writes=[("psNM", 0)])
                    if lvl < 5:
                        for h in HORD:
                            ps_, hh = hq(h)
                            P.op("pe", lambda e, ps_=ps_, hh=hh, mi=mi: e.matmul(
                                psNM[0][ps_, 256 + hh * 64:256 + (hh + 1) * 64], lhsT=mi[ps_, 0, hh, :],
                                rhs=mi[ps_, 1, hh, :], start=True, stop=True),
                                reads=[kmi], writes=[("psNM", 0)])
                        P.op("act", lambda e, mo=mo: e.activation(
                            out=mo[:, :, :, :].rearrange("p m h t -> p (m h t)"), in_=psNM[0][:, :], func=AF.Copy),
                            reads=[("psNM", 0)], writes=[kmo])
                    else:
                        P.op("act", lambda e, mo=mo: e.activation(
                            out=mo[:, 0, :, :].rearrange("p h t -> p (h t)"), in_=psNM[0][:, 0:256], func=AF.Copy),
                            reads=[("psNM", 0)], writes=[kmo])
                    M_cur = 1 - M_cur
                    p_in = PTch[pbase + P_cur]
                    p_out = PTch[pbase + 1 - P_cur]
                    kp_in, kp_out = "PTch%d" % (pbase + P_cur), "PTch%d" % (pbase + 1 - P_cur)
                    for h in HORD:
                        ps_, hh = hq(h)
                        P.op("pe", lambda e, ps_=ps_, hh=hh, mo=mo, p_in=p_in: e.matmul(
                            psSm[ps_, hh * 64:(hh + 1) * 64], lhsT=mo[ps_, 0, hh, :], rhs=p_in[ps_, hh, :],
                            start=True, stop=True),
                            reads=[kmo, kp_in], writes=["psSm"])
                    P.op("dve", lambda e, p_in=p_in, p_out=p_out: e.tensor_tensor(
                        out=p_out[:, :, :].rearrange("p h t -> p (h t)"), in0=psSm[:, 0:256],
                        in1=p_in[:, :, :].rearrange("p h t -> p (h t)"), op=ALU.add),
                        reads=["psSm", kp_in], writes=[kp_out])
                    P_cur = 1 - P_cur
                pfin = PTch[pbase + P_cur]
                kpfin = "PTch%d" % (pbase + P_cur)
                P.op("pool", lambda e, c=c: e.tensor_tensor(
                    out=Sdec[:, :, :], in0=Sst[:, :, :],
                    in1=gam[:, :, c].unsqueeze(2).to_broadcast([128, 4, 64]), op=ALU.mult),
                    reads=["Sst"] + [("gam", par, j) for j in range(4)], writes=["Sdec"])
                for h in HORD:
                    j, e2 = h // 2, h % 2
                    ps_, hh = hq(h)
                    P.op("pe", lambda e, h=h, j=j, e2=e2, c=c, ps_=ps_, hh=hh: e.matmul(
                        psNM[1][ps_, hh * 64:(hh + 1) * 64], lhsT=MTs[0:64, h, 64:128],
                        rhs=Zt[0:64, c, j, e2 * 64:(e2 + 1) * 64], start=True, stop=True),
                        reads=["MTs", ("ZtV", par, j)], writes=[("psNM", 1)])
                P.op("act", lambda e: e.activation(out=Xs[:, :, :].rearrange("p h t -> p (h t)"),
                                                   in_=psNM[1][:, 0:256], func=AF.Copy),
                     reads=[("psNM", 1)], writes=["Xs"])
                for h in HORD:
                    j = h // 2
                    ps_, hh = hq(h)
                    P.op("pe", lambda e, h=h, j=j, c=c, sb_old=sb_old, ps_=ps_, hh=hh: e.matmul(
                        psG[ps_, hh * 64:(hh + 1) * 64], lhsT=QT[:, h, c, 64:128],
                        rhs=sb_old[:, j, :], start=True, stop=True),
                        reads=[("QT", par, j), kold], writes=["psG"])
                P.op("dve", lambda e: e.scalar_tensor_tensor(
                    out=GXs[:, :, :].rearrange("p h t -> p (h t)"), in0=psG[:, 0:256], scalar=-1.0,
                    in1=Xs[:, :, :].rearrange("p h t -> p (h t)"), op0=ALU.mult, op1=ALU.subtract),
                    reads=["psG", "Xs"], writes=["GXs"])
                for h in HORD:
                    ps_, hh = hq(h)
                    ub = psSm if h < 4 else psNM[1]
                    ubk = "psSm" if h < 4 else ("psNM", 1)
                    P.op("pe", lambda e, ps_=ps_, hh=hh, pfin=pfin, ub=ub: e.matmul(
                        ub[64:128, hh * 64:(hh + 1) * 64], lhsT=pfin[ps_, hh, :], rhs=GXs[ps_, hh, :],
                        start=True, stop=True),
                        reads=[kpfin, "GXs"], writes=[ubk])
                P.op("dve", lambda e, c=c: e.tensor_copy(
                    out=Zt[64:128, c, 0:2, :].rearrange("p j f -> p (j f)"), in_=psSm[64:128, 0:256]),
                    reads=["psSm"], writes=[("ZtU", par, c)])
                P.op("act", lambda e, c=c: e.activation(
                    out=Zt[64:128, c, 2:4, :].rearrange("p j f -> p (j f)"), in_=psNM[1][64:128, 0:256],
                    func=AF.Copy),
                    reads=[("psNM", 1)], writes=[("ZtU2", par, c)])
                zkeys = [("ZtV", par, j) for j in range(4)] + [("ZtU", par, c), ("ZtU2", par, c)]
                ypb = 64 * (c % 2)
                for h in range(8):
                    j, pb, e2 = h // 2, 64 * (h % 2), h % 2
                    P.op("pe", lambda e, h=h, j=j, pb=pb, c=c, ypb=ypb, sb_old=sb_old: e.matmul(
                        psY[ypb:ypb + 64, h * 64:(h + 1) * 64], lhsT=QT[:, h, c, 0:64],
                        rhs=sb_old[:, j, :], start=True, stop=False),
                        reads=[("QT", par, j), kold], writes=["psY"])
                    P.op("pe", lambda e, h=h, j=j, e2=e2, c=c, ypb=ypb: e.matmul(
                        psY[ypb:ypb + 64, h * 64:(h + 1) * 64], lhsT=MTs[:, h, 0:64],
                        rhs=Zt[:, c, j, e2 * 64:(e2 + 1) * 64], start=False, stop=True),
                        reads=["MTs"] + zkeys, writes=["psY"])
                for h in range(8):
                    j, pb, e2 = h // 2, 64 * (h % 2), h % 2
                    P.op("pe", lambda e, h=h, j=j, pb=pb, e2=e2, c=c: e.matmul(
                        psSm[pb:pb + 64, j * 64:(j + 1) * 64], lhsT=PPt[:, c, j, e2 * 64:(e2 + 1) * 64],
                        rhs=Zt[:, c, j, e2 * 64:(e2 + 1) * 64], start=True, stop=True),
                        reads=[("PPt", par, j)] + zkeys, writes=["psSm"])
                P.op("dve", lambda e: e.tensor_tensor(out=Sst[:, :, :].rearrange("p j v -> p (j v)"),
                                                      in0=psSm[:, 0:256],
                                                      in1=Sdec[:, :, :].rearrange("p j v -> p (j v)"), op=ALU.add),
                     reads=["psSm", "Sdec"], writes=["Sst"])
                P.op("act", lambda e, sb_new=sb_new: e.activation(out=sb_new[:, :, :], in_=Sst[:, :, :],
                                                                  func=AF.Copy),
                     reads=["Sst"], writes=[knew])
                if c % 2 == 1:
                    ti = c // 2
                    P.op("act", lambda e: e.activation(out=ycp[:, :], in_=psY[:, :], func=AF.Copy),
                         reads=["psY"], writes=["ycp"])
                    P.op("act", lambda e: e.activation(out=ysq[:, :], in_=psY[:, :], func=AF.Square),
                         reads=["psY"], writes=["ysq"])
                    P.op("dve", lambda e: e.tensor_reduce(out=st1[:, :],
                                                          in_=ycp[:, :].rearrange("p (h v) -> p h v", v=64),
                                                          axis=AX.X, op=ALU.add),
                         reads=["ycp"], writes=["st1"])
                    P.op("dve", lambda e: e.tensor_reduce(out=st2[:, :],
                                                          in_=ysq[:, :].rearrange("p (h v) -> p h v", v=64),
                                                          axis=AX.X, op=ALU.add),
                         reads=["ysq"], writes=["st2"])
                    P.op("dve", lambda e: e.tensor_scalar(out=stm[:, :], in0=st1[:, :], scalar1=1.0 / 64,
                                                          scalar2=None, op0=ALU.mult),
                         reads=["st1"], writes=["stm"])
                    P.op("dve", lambda e: e.tensor_tensor(out=stv[:, :], in0=stm[:, :], in1=stm[:, :], op=ALU.mult),
                         reads=["stm"], writes=["stv"])
                    P.op("dve", lambda e: e.scalar_tensor_tensor(out=stv[:, :], in0=st2[:, :], scalar=1.0 / 64,
                                                                 in1=stv[:, :], op0=ALU.mult, op1=ALU.subtract),
                         reads=["st2", "stv"], writes=["stv"])
                    P.op("pool", lambda e: e.tensor_scalar(out=stv[:, :], in0=stv[:, :], scalar1=1.0, scalar2=GN_EPS,
                                                           op0=ALU.mult, op1=ALU.add),
                         reads=["stv"], writes=["stv"])
                    P.op("pool", lambda e: e.tensor_tensor(out=stv[:, :], in0=stv[:, :], in1=neghalf[:, 0:8],
                                                           op=ALU.pow),
                         reads=["stv", "neghalf"], writes=["stv"])
                    P.op("pool", lambda e: e.tensor_tensor(
                        out=ycp[:, :].rearrange("p (h v) -> p h v", v=64),
                        in0=ycp[:, :].rearrange("p (h v) -> p h v", v=64),
                        in1=stm[:, :].unsqueeze(2).to_broadcast([128, 8, 64]), op=ALU.subtract),
                        reads=["ycp", "stm"], writes=["ycp"])
                    P.op("pool", lambda e, ti=ti: e.tensor_tensor(
                        out=ynb[ti][:, :].rearrange("p (h v) -> p h v", v=64),
                        in0=ycp[:, :].rearrange("p (h v) -> p h v", v=64),
                        in1=stv[:, :].unsqueeze(2).to_broadcast([128, 8, 64]), op=ALU.mult),
                        reads=["ycp", "stv"], writes=[("ynb", ti)])

            P.in_region = False
            if upto < 4:
                return
            P.tag = (tb, "E")
            P.section = "E"
            for ti in range(NTT):
                for j in range(4):
                    P.op("pe", lambda e, ti=ti, j=j: e.transpose(
                        out=psYb[:, j * TB + ti * 128:j * TB + (ti + 1) * 128],
                        in_=ynb[ti][:, j * 128:(j + 1) * 128], identity=identb[:, :]),
                        reads=[("ynb", ti), "identb"], writes=["psY"])
            for j in range(4):
                P.op("dve", lambda e, j=j: e.tensor_scalar(out=t1[:, :], in0=psYb[:, j * TB:(j + 1) * TB],
                                                           scalar1=gng[:, j:j + 1], scalar2=gnb[:, j:j + 1],
                                                           op0=ALU.mult, op1=ALU.add),
                     reads=["psY", "gng", "gnb"], writes=["t1"])
                P.op("pool", lambda e, j=j: e.tensor_tensor(out=t1[:, :], in0=t1[:, :], in1=bon[:, j, :],
                                                            op=ALU.add),
                     reads=["t1", ("bon", par, j)], writes=["t1"])
                P.op("pool", lambda e, j=j: e.tensor_tensor(out=ycat[:, 4 + j, :], in0=t1[:, :], in1=sgb[:, j, :],
                                                            op=ALU.mult),
                     reads=["t1", ("sgb", par, j)], writes=[("ycat", 4 + j)])

            ykeys = [("ycat", k) for k in range(8)]
            pso = [psM1, psG]
            psok = ["psM1", "psG"]
            for i in range(NTT):
                col = S // 128 + tb * NTT + i
                r0 = t0 + i * 128
                xi = erot[0] % 2
                erot[0] += 1
                xb = xe[xi]
                dma("sync", xb[:, :], x[r0:r0 + 128, :], f"de{xi}", writes=[("xe", xi)])
                for hf in range(2):
                    for kc in range(8):
                        P.op("pe", lambda e, hf=hf, kc=kc, i=i: e.matmul(
                            pso[hf][:, :], lhsT=ycat[:, kc, i * 128:(i + 1) * 128],
                            rhs=woutb[:, kc, hf * 512:(hf + 1) * 512], start=(kc == 0), stop=(kc == 7)),
                            reads=ykeys + [("woutb", kc)], writes=[psok[hf]])
                    P.op("dve", lambda e, hf=hf, xb=xb: e.tensor_tensor(
                        out=xb[:, hf * 512:(hf + 1) * 512], in0=pso[hf][:, :],
                        in1=xb[:, hf * 512:(hf + 1) * 512], op=ALU.add),
                        reads=[psok[hf], ("xe", xi)], writes=[("xe", xi)])
                P.op("act", lambda e, col=col, xb=xb: e.activation(out=junk[:, :], in_=xb[:, :], func=AF.Square,
                                                                   accum_out=ssx[:, col:col + 1]),
                     reads=[("xe", xi)], writes=["junk", ("ssx", col)])
                P.op("pool", lambda e, col=col: e.tensor_scalar(out=sdx[:, col:col + 1], in0=ssx[:, col:col + 1],
                                                                scalar1=1.0 / D, scalar2=NORM_EPS,
                                                                op0=ALU.mult, op1=ALU.add),
                     reads=[("ssx", col)], writes=[("sdx", col)])
                P.op("pool", lambda e, col=col: e.tensor_tensor(out=rsx[:, col:col + 1], in0=sdx[:, col:col + 1],
                                                                in1=neghalf[:, 0:1], op=ALU.pow),
                     reads=[("sdx", col), "neghalf"], writes=[("rsx", col)])
                P.op("dve", lambda e, col=col, xb=xb: e.scalar_tensor_tensor(
                    out=xb[:, :], in0=xb[:, :], scalar=rsx[:, col:col + 1], in1=fgain[:, :],
                    op0=ALU.mult, op1=ALU.mult),
                    reads=[("xe", xi), ("rsx", col), "fgain"], writes=[("xe", xi)])
                dma("sync", out[r0:r0 + 128, :], xb[:, :], f"do{xi}", reads=[("xe", xi)], writes=[("outd", xi)])

        nbl = nblocks if upto >= 1 else 0
        if nbl:
            P.only = {"A"}
            block_body(0)
        for tb in range(nbl):
            P.only = {"B"}
            block_body(tb)
            if tb + 1 < nbl:
                P.only = {"A"}
                block_body(tb + 1)
            P.only = {"D", "E"}
            block_body(tb)
        P.only = None
        P.section = None
        P.tag = None

        for name, ap, key in dbg_dumps:
            if ap is None:
                continue
            dma("sync", dbg_out[name], ap, "ddbg", reads=[key])

        _INFO["sbuf_left"] = nc.sbuf_bytes_remaining
        if do_schedule:
            P.schedule()
            cands = [(P.est_makespan, {e: list(v) for e, v in P.ops.items()})]
            if SCHED_TRIALS > 0:
                import random
                for t in range(SCHED_TRIALS):
                    P.prio_rng = random.Random(1234 + t)
                    P.prio_noise = SCHED_NOISE
                    P.schedule()
                    cands.append((P.est_makespan, {e: list(v) for e, v in P.ops.items()}))
                P.prio_rng = None
            cands.sort(key=lambda c: c[0])
            P.est_makespan, P.ops = cands[min(SCHED_PICK, len(cands) - 1)]
            _INFO["est_us"] = P.est_makespan
        P.finalize()
        keys = P.sem_keys()
        sems = {k: es.enter_context(nc.semaphore("s_" + k)) for k in keys}
        out_keys = [k for k in keys if k.startswith("do") or k == "ddbg"]
        with nc.Block() as block:
            @block.sync
            def _(eng):
                P.run_engine("sync", eng, sems, final_waits=out_keys)

            @block.scalar
            def _(eng):
                P.run_engine("act", eng, sems)

            @block.vector
            def _(eng):
                P.run_engine("dve", eng, sems)

            @block.gpsimd
            def _(eng):
                P.run_engine("pool", eng, sems)

            @block.tensor
            def _(eng):
                P.run_engine("pe", eng, sems)
    return nc


_CACHE = {}
_INFO = {}


def _prep_inputs(inputs):
    f = lambda a: np.ascontiguousarray(np.asarray(a, dtype=np.float32))
    shared = {
        "norm_gain": f(inputs["norm_gain"]).reshape(D),
        "w_in": f(inputs["w_in"]).reshape(D, DIN),
        "pool_w": f(inputs["pool_w"]).reshape(4, 128, 128),
        "pool_scale": f(inputs["pool_scale"]).reshape(512),
        "shift_mu": f(inputs["shift_mu"]).reshape(1664),
        "w0": f(inputs["w0"]).reshape(512),
        "w_up": f(inputs["w_up"]).reshape(64, 512),
        "a0": f(inputs["a0"]).reshape(512),
        "a_up": f(inputs["a_up"]).reshape(64, 512),
        "k_k": f(inputs["k_k"]).reshape(512),
        "k_a": f(inputs["k_a"]).reshape(512),
        "r_k": f(inputs["r_k"]).reshape(512),
        "gn_gain": f(inputs["gn_gain"]).reshape(512),
        "gn_bias": f(inputs["gn_bias"]).reshape(512),
        "w_out": f(inputs["w_out"]).reshape(D, D),
        "final_gain": f(inputs["final_gain"]).reshape(D),
        "cst": _make_consts(),
    }
    xs = f(inputs["x"])
    return [dict(shared, x=xs[b]) for b in range(xs.shape[0])]


def kernel(**inputs):
    in_maps = _prep_inputs(inputs)
    if "nc" not in _CACHE:
        _CACHE["nc"] = build()
    nc = _CACHE["nc"]
    res = run_bass_kernel_spmd(nc, in_maps, core_ids=list(range(8)))
    return np.stack([np.asarray(r["out"], dtype=np.float32) for r in res.results], axis=0)
```
